# Optimizing a Trainium2 kernel written in Bass

```python
import jax, jax.numpy as jnp
from jax import lax
import numpy as np

D_MODEL = 1024
BATCH = 2
SEQ = 8192
DEPTH = 2

CHUNK = 64
Q_BLOCK = 128
NORM_EPS = 1e-6

GDN_HEADS = 4
GDN_HEAD_DIM = 128
GDN_WIDTH = GDN_HEADS * GDN_HEAD_DIM
CONV_WIDTH = 4

MLA_HEADS = 8
MLA_NOPE_DIM = 64
MLA_ROPE_DIM = 32
MLA_V_DIM = 64
MLA_Q_RANK = 256
MLA_KV_RANK = 128
MLA_WIDTH = MLA_HEADS * MLA_V_DIM
ROPE_THETA = 10000.0

MIX_WIDTH = GDN_WIDTH + MLA_WIDTH
D_FF = 4 * D_MODEL

IN_SIZES = (GDN_WIDTH, GDN_WIDTH, GDN_WIDTH, GDN_WIDTH, GDN_HEADS, GDN_HEADS,
            MLA_Q_RANK, MLA_KV_RANK, MLA_ROPE_DIM)
IN_WIDTH = 4 * GDN_WIDTH + 2 * GDN_HEADS + MLA_Q_RANK + MLA_KV_RANK + MLA_ROPE_DIM

kernel_name = "hybrid_gdn_mla_sandwich_block"


def rms_norm(x, gain):
    xf = x.astype(jnp.float32)
    y = xf * lax.rsqrt(jnp.mean(xf * xf, axis=-1, keepdims=True) + NORM_EPS)
    return (y * gain.astype(jnp.float32)).astype(x.dtype)


def l2_norm(x):
    xf = x.astype(jnp.float32)
    return xf * lax.rsqrt(jnp.sum(xf * xf, axis=-1, keepdims=True) + NORM_EPS)


def causal_depthwise_conv(x, w):
    k = w.shape[0]
    return lax.conv_general_dilated(
        x, w[:, None, :].astype(x.dtype), window_strides=(1,), padding=[(k - 1, 0)],
        dimension_numbers=("NWC", "WIO", "NWC"), feature_group_count=x.shape[-1])


def rope_angles(positions):
    inv_freq = ROPE_THETA ** (-jnp.arange(0, MLA_ROPE_DIM, 2, dtype=jnp.float32) / MLA_ROPE_DIM)
    ang = positions.astype(jnp.float32)[..., None] * inv_freq
    return jnp.cos(ang), jnp.sin(ang)


def apply_rope(x, cos, sin):
    xf = x.astype(jnp.float32)
    x1, x2 = jnp.split(xf, 2, axis=-1)
    return jnp.concatenate([x1 * cos - x2 * sin, x2 * cos + x1 * sin], axis=-1).astype(x.dtype)


def gated_delta_rule(q, k, v, beta, log_a):
    b, s, h, dk = q.shape
    dv = v.shape[-1]
    n = s // CHUNK

    def chunks(t):
        t = t.reshape((b, n, CHUNK, h) + t.shape[3:])
        return jnp.moveaxis(t, 3, 1)

    q, k, v = chunks(q), chunks(k), chunks(v)
    beta, log_a = chunks(beta), chunks(log_a)
    g = jnp.cumsum(log_a, axis=-1)
    idx = jnp.arange(CHUNK)
    strict = idx[:, None] > idx[None, :]
    causal = idx[:, None] >= idx[None, :]
    gdiff = g[..., :, None] - g[..., None, :]
    decay_strict = jnp.exp(jnp.where(strict, gdiff, -jnp.inf))
    decay_causal = jnp.exp(jnp.where(causal, gdiff, -jnp.inf))

    k_beta = k * beta[..., None]
    a_mat = jnp.einsum("bhnid,bhnjd->bhnij", k_beta, k) * decay_strict
    eye = jnp.eye(CHUNK, dtype=jnp.float32)
    t_inv = lax.linalg.triangular_solve(eye + a_mat, jnp.broadcast_to(eye, a_mat.shape),
                                        left_side=True, lower=True)
    u = t_inv @ (v * beta[..., None])
    w = t_inv @ (k_beta * jnp.exp(g)[..., None])
    p = jnp.einsum("bhnid,bhnjd->bhnij", q, k) * decay_causal
    q_dec = q * jnp.exp(g)[..., None]
    g_last = g[..., -1]
    k_dec = k * jnp.exp(g_last[..., None] - g)[..., None]

    def step(state, xs):
        q_c, k_c, u_c, w_c, p_c, gl = xs
        v_new = u_c - jnp.einsum("bhcd,bhde->bhce", w_c, state)
        o = (jnp.einsum("bhcd,bhde->bhce", q_c, state)
             + jnp.einsum("bhij,bhje->bhie", p_c, v_new))
        state = state * jnp.exp(gl)[..., None, None] + jnp.einsum("bhcd,bhce->bhde", k_c, v_new)
        return state, o

    xs = tuple(jnp.moveaxis(t, 2, 0) for t in (q_dec, k_dec, u, w, p, g_last))
    state0 = jnp.zeros((b, h, dk, dv), jnp.float32)
    _, o = lax.scan(step, state0, xs)
    return jnp.transpose(o, (1, 0, 3, 2, 4)).reshape(b, s, h, dv)


def gated_deltanet_group(q, k, v, gate, a_logit, b_logit, conv_w, a_log, dt_bias, out_norm):
    bsz, s, _ = q.shape
    qkv = jax.nn.silu(causal_depthwise_conv(jnp.concatenate([q, k, v], axis=-1), conv_w))
    q, k, v = jnp.split(qkv, 3, axis=-1)
    q = l2_norm(q.reshape(bsz, s, GDN_HEADS, GDN_HEAD_DIM)) * (GDN_HEAD_DIM ** -0.5)
    k = l2_norm(k.reshape(bsz, s, GDN_HEADS, GDN_HEAD_DIM))
    v = v.reshape(bsz, s, GDN_HEADS, GDN_HEAD_DIM).astype(jnp.float32)
    beta = jax.nn.sigmoid(b_logit.astype(jnp.float32))
    log_a = -jnp.exp(a_log.astype(jnp.float32)) * jax.nn.softplus(
        a_logit.astype(jnp.float32) + dt_bias.astype(jnp.float32))
    o = gated_delta_rule(q, k, v, beta, log_a)
    g = jax.nn.silu(gate.astype(jnp.float32)).reshape(bsz, s, GDN_HEADS, GDN_HEAD_DIM)
    o = rms_norm(o, out_norm) * g
    return o.reshape(bsz, s, GDN_WIDTH).astype(gate.dtype)


def mla_group(c_q, c_kv, k_rope, cos, sin, q_norm, w_q_up, kv_norm, w_kv_up):
    b, s, _ = c_q.shape
    dqk = MLA_NOPE_DIM + MLA_ROPE_DIM
    q = (rms_norm(c_q, q_norm) @ w_q_up).reshape(b, s, MLA_HEADS, dqk)
    q_nope, q_rope = q[..., :MLA_NOPE_DIM], q[..., MLA_NOPE_DIM:]
    kv = (rms_norm(c_kv, kv_norm) @ w_kv_up).reshape(b, s, MLA_HEADS, MLA_NOPE_DIM + MLA_V_DIM)
    k_nope, v = kv[..., :MLA_NOPE_DIM], kv[..., MLA_NOPE_DIM:]
    q_rope = apply_rope(q_rope, cos[:, :, None, :], sin[:, :, None, :])
    k_rope = apply_rope(k_rope, cos, sin)
    q = jnp.concatenate([q_nope, q_rope], axis=-1)
    k = jnp.concatenate(
        [k_nope, jnp.broadcast_to(k_rope[:, :, None, :], (b, s, MLA_HEADS, MLA_ROPE_DIM))], axis=-1)
    scale = dqk ** -0.5
    n_blk = s // Q_BLOCK
    q_blocks = jnp.moveaxis(q.reshape(b, n_blk, Q_BLOCK, MLA_HEADS, dqk), 1, 0)
    key_chunk = jnp.arange(s) // CHUNK

    def attend(xs):
        q_b, blk = xs
        q_chunk = (blk * Q_BLOCK + jnp.arange(Q_BLOCK)) // CHUNK
        scores = jnp.einsum("bqhd,bkhd->bhqk", q_b, k,
                            preferred_element_type=jnp.float32) * scale
        mask = key_chunk[None, :] <= q_chunk[:, None]
        probs = jax.nn.softmax(jnp.where(mask, scores, -jnp.inf), axis=-1).astype(v.dtype)
        return jnp.einsum("bhqk,bkhd->bqhd", probs, v)

    o = lax.map(attend, (q_blocks, jnp.arange(n_blk)))
    return jnp.moveaxis(o, 0, 1).reshape(b, s, MLA_WIDTH)


def setup_inputs(seed: int = 0) -> dict:
    key = jax.random.key(seed)
    ks = jax.random.split(key, 20)
    f32 = jnp.float32

    def normal(k, shape, fan_in):
        return jax.random.normal(k, shape, f32) * (fan_in ** -0.5)

    def gain(k, shape):
        return 1.0 + 0.1 * jax.random.normal(k, shape, f32)

    x = jax.random.normal(ks[0], (BATCH, SEQ, D_MODEL), f32)
    offsets = jax.random.randint(ks[1], (BATCH, 1), 0, 4096)
    positions = (offsets + jnp.arange(SEQ)[None, :]).astype(jnp.int32)
    a_log = jnp.log(jax.random.uniform(ks[2], (DEPTH, GDN_HEADS), f32, 1.0, 16.0))
    dt = jnp.exp(jax.random.uniform(ks[3], (DEPTH, GDN_HEADS), f32,
                                    float(np.log(1e-3)), float(np.log(1e-1))))
    dt_bias = dt + jnp.log(-jnp.expm1(-dt))
    return {
        "x": x,
        "positions": positions,
        "mix_pre_norm": gain(ks[4], (DEPTH, D_MODEL)),
        "w_in": normal(ks[5], (DEPTH, D_MODEL, IN_WIDTH), D_MODEL),
        "conv_w": normal(ks[6], (DEPTH, CONV_WIDTH, 3 * GDN_WIDTH), CONV_WIDTH),
        "a_log": a_log,
        "dt_bias": dt_bias,
        "gdn_out_norm": gain(ks[7], (DEPTH, GDN_HEAD_DIM)),
        "q_norm": gain(ks[8], (DEPTH, MLA_Q_RANK)),
        "w_q_up": normal(ks[9], (DEPTH, MLA_Q_RANK, MLA_HEADS * (MLA_NOPE_DIM + MLA_ROPE_DIM)), MLA_Q_RANK),
        "kv_norm": gain(ks[10], (DEPTH, MLA_KV_RANK)),
        "w_kv_up": normal(ks[11], (DEPTH, MLA_KV_RANK, MLA_HEADS * (MLA_NOPE_DIM + MLA_V_DIM)), MLA_KV_RANK),
        "w_out": normal(ks[12], (DEPTH, MIX_WIDTH, D_MODEL), MIX_WIDTH),
        "mix_post_norm": gain(ks[13], (DEPTH, D_MODEL)),
        "ffn_pre_norm": gain(ks[14], (DEPTH, D_MODEL)),
        "w_up": normal(ks[15], (DEPTH, D_MODEL, D_FF), D_MODEL),
        "w_down": normal(ks[16], (DEPTH, D_FF, D_MODEL), D_FF),
        "ffn_post_norm": gain(ks[17], (DEPTH, D_MODEL)),
    }


def reference(x, positions, mix_pre_norm, w_in, conv_w, a_log, dt_bias, gdn_out_norm,
              q_norm, w_q_up, kv_norm, w_kv_up, w_out, mix_post_norm,
              ffn_pre_norm, w_up, w_down, ffn_post_norm):
    cos, sin = rope_angles(positions)
    offsets = [int(o) for o in np.cumsum(IN_SIZES)[:-1]]
    for l in range(DEPTH):
        h = rms_norm(x, mix_pre_norm[l])
        proj = h @ w_in[l]
        gq, gk, gv, gg, ga, gb, cq, ckv, kr = jnp.split(proj, offsets, axis=-1)
        o_gdn = gated_deltanet_group(gq, gk, gv, gg, ga, gb, conv_w[l], a_log[l],
                                     dt_bias[l], gdn_out_norm[l])
        o_mla = mla_group(cq, ckv, kr, cos, sin, q_norm[l], w_q_up[l], kv_norm[l], w_kv_up[l])
        mixed = jnp.concatenate([o_gdn.astype(x.dtype), o_mla.astype(x.dtype)], axis=-1) @ w_out[l]
        x = x + rms_norm(mixed, mix_post_norm[l])
        h = rms_norm(x, ffn_pre_norm[l])
        f = jnp.square(jax.nn.relu(h @ w_up[l])) @ w_down[l]
        x = x + rms_norm(f, ffn_post_norm[l])
    return x
```

```python
import types
import numpy as np
import concourse.bass as bass
import concourse.mybir as mybir
from concourse.bass_utils import run_bass_kernel_spmd

F32 = mybir.dt.float32
BF16 = mybir.dt.bfloat16
I32 = mybir.dt.int32
AF = mybir.ActivationFunctionType
ALU = mybir.AluOpType
AX = mybir.AxisListType

D = 1024
SEQ = 8192
DFF = 4096
EPS = 1e-6
INW = 2472
ENGS = ("pe", "act", "dve", "pool", "sp")


class T:
    __slots__ = ("ap", "name", "w", "r")

    def __init__(self, ap, name=""):
        self.ap = ap
        self.name = name
        self.w = None
        self.r = []

    def __getitem__(self, idx):
        return self.ap[idx]


class Op:
    __slots__ = ("eng", "fn", "deps", "inc", "sem", "val", "dma", "idx", "waits", "name")

    def __init__(self, eng, fn, dma, name):
        self.eng = eng
        self.fn = fn
        self.deps = set()
        self.inc = False
        self.sem = None
        self.val = 0
        self.dma = dma
        self.waits = []
        self.name = name


def _freeze(fn):
    if fn.__closure__ is None:
        return fn
    cells = []
    for c in fn.__closure__:
        try:
            cells.append(types.CellType(c.cell_contents))
        except ValueError:
            cells.append(c)
    return types.FunctionType(fn.__code__, fn.__globals__, fn.__name__, fn.__defaults__, tuple(cells))


class RR:
    def __init__(self, items):
        self.items = list(items)
        self.i = 0

    def next(self):
        t = self.items[self.i % len(self.items)]
        self.i += 1
        return t


class Prog:
    def __init__(self, nc, n_dma_sems=24):
        self.nc = nc
        self.ops = {e: [] for e in ENGS}
        self.n_dma_sems = n_dma_sems
        self._stack = []
        self.nops = 0
        self.last = {e: None for e in ENGS}
        self.dmas_since_bar = []

    def mark(self):
        return len(self._stack)

    def release(self, mark):
        while len(self._stack) > mark:
            g = self._stack.pop()
            g.__exit__(None, None, None)

    def tile(self, name, shape, dtype, space="sbuf"):
        self.uid = getattr(self, "uid", 0) + 1
        name = f"t{self.uid}_{name}"
        if space == "sbuf":
            g = self.nc.sbuf_tensor(name, list(shape), dtype)
        else:
            g = self.nc.psum_tensor(name, list(shape), dtype)
        h = g.__enter__()
        self._stack.append(g)
        return T(h, name)

    def dram(self, name, shape, dtype, kind="Internal"):
        h = self.nc.dram_tensor(name, list(shape), dtype, kind=kind)
        return T(h, name)

    def op(self, eng, fn, r=(), w=(), dma=False, name="", extra=()):
        o = Op(eng, _freeze(fn), dma, name)
        for t in r:
            if t.w is not None:
                o.deps.add(t.w)
        for t in w:
            if t.w is not None:
                o.deps.add(t.w)
            for rd in t.r:
                o.deps.add(rd)
        for d in extra:
            if d is not None:
                o.deps.add(d)
        for t in w:
            t.w = o
            t.r = []
        for t in r:
            if t.w is not o:
                t.r.append(o)
        o.deps.discard(o)
        o.idx = len(self.ops[eng])
        self.ops[eng].append(o)
        self.last[eng] = o
        if dma:
            self.dmas_since_bar.append(o)
        self.nops += 1
        return o

    def dma(self, out_ap, in_ap, r=(), w=(), eng="sp", name="", **kw):
        return self.op(eng, lambda e: e.dma_start(out=out_ap, in_=in_ap, **kw), r=r, w=w, dma=True, name=name)

    def barrier(self):
        marks = [self.last[e] for e in ENGS] + list(self.dmas_since_bar)
        self.dmas_since_bar = []
        for e in ENGS:
            self.op(e, lambda en: en.nop(), extra=marks, name="bar")

    def emit(self):
        nc = self.nc

        def skip(d, o):
            return d.eng == "pe" and o.eng == "pe" and not d.dma and not o.dma

        for e in ENGS:
            for o in self.ops[e]:
                for d in o.deps:
                    if not skip(d, o):
                        d.inc = True
        eng_sem = {e: nc.alloc_semaphore(name=f"s_{e}") for e in ENGS}
        dma_pool = {}
        dma_state = {}
        for e in ENGS:
            if any(o.dma for o in self.ops[e]):
                dma_pool[e] = [nc.alloc_semaphore(name=f"d_{e}{i}") for i in range(self.n_dma_sems)]
                dma_state[e] = [0, [0] * self.n_dma_sems]
        for e in ENGS:
            cnt = 0
            for o in self.ops[e]:
                if o.dma:
                    st = dma_state[e]
                    i = st[0] % self.n_dma_sems
                    st[0] += 1
                    prev = st[1][i]
                    o.sem = dma_pool[e][i]
                    o.val = prev + 16
                    st[1][i] = o.val
                    o.inc = True
                    o.waits.append((o.sem, prev))
                elif o.inc:
                    cnt += 1
                    o.sem = eng_sem[e]
                    o.val = cnt
        for e in ENGS:
            seen = {}
            for o in self.ops[e]:
                need = {}
                for (s, v) in o.waits:
                    if v > 0:
                        need[id(s)] = (s, v)
                for d in o.deps:
                    if skip(d, o):
                        continue
                    k = id(d.sem)
                    if k not in need or need[k][1] < d.val:
                        need[k] = (d.sem, d.val)
                waits = []
                for k, (s, v) in need.items():
                    if seen.get(k, 0) >= v:
                        continue
                    seen[k] = v
                    waits.append((s, v))
                o.waits = waits

        with nc.Block() as block:
            def mk(e):
                def body(eng):
                    for o in self.ops[e]:
                        for (s, v) in o.waits:
                            eng.wait_ge(s, v)
                        ins = o.fn(eng)
                        if o.inc:
                            ins.then_inc(o.sem, 16 if o.dma else 1)
                return body
            if self.ops["sp"]:
                block.sync(mk("sp"))
            if self.ops["pe"]:
                block.tensor(mk("pe"))
            if self.ops["act"]:
                block.scalar(mk("act"))
            if self.ops["dve"]:
                block.vector(mk("dve"))
            if self.ops["pool"]:
                block.gpsimd(mk("pool"))

    def close(self):
        self.release(0)


class Common:
    pass


def setup_common(p):
    c = Common()
    idf = p.tile("c_idf", [128, 128], F32)
    p.op("pool", lambda e: e.memset(idf[:, :], 0.0), w=[idf])
    p.op("pool", lambda e: e.affine_select(out=idf[:, :], in_=idf[:, :], pattern=[[-1, 128]],
                                           compare_op=ALU.not_equal, fill=1.0, base=0, channel_multiplier=1),
         r=[idf], w=[idf])
    c.identf = idf
    c.ident = p.tile("c_ident", [128, 128], BF16)
    p.op("dve", lambda e: e.tensor_copy(out=c.ident[:, :], in_=idf[:, :]), r=[idf], w=[c.ident])
    p.eps_tile = p.tile("c_eps", [128, 1], F32)
    p.op("pool", lambda e: e.memset(p.eps_tile[:, :], EPS), w=[p.eps_tile])
    return c


def bcast_load(p, dst, src_ap_1d, n, eng="sp"):
    p.dma(dst[:, :], src_ap_1d.partition_broadcast(128), w=[dst], eng=eng)


def rstd_from_ss(p, ss, rstd, n, tmp):
    p.op("act", lambda e: e.activation(out=tmp[:, :], in_=ss[:, :], func=AF.Sqrt, bias=p.eps_tile[:, 0:1], scale=1.0 / n),
         r=[ss, p.eps_tile], w=[tmp])
    p.op("dve", lambda e: e.reciprocal(out=rstd[:, :], in_=tmp[:, :]), r=[tmp], w=[rstd])


def norm_to_hT(p, cm, x_ap, gain_t, hT_blk, j, sc, psT):
    xt = sc["xt"]
    p.op("act", lambda e: e.activation(out=sc["junk"][:, :], in_=x_ap, func=AF.Square, accum_out=sc["ss1"][:, 0:1]),
         r=[xt], w=[sc["ss1"]])
    rstd_from_ss(p, sc["ss1"], sc["rs1"], D, sc["tmp1"])
    hb = sc["hb"]
    p.op("dve", lambda e: e.scalar_tensor_tensor(out=hb[:, :], in0=x_ap, scalar=sc["rs1"][:, 0:1], in1=gain_t[:, :],
                                                 op0=ALU.mult, op1=ALU.mult), r=[xt, sc["rs1"], gain_t], w=[hb])
    pt = psT.next()
    for k in range(8):
        p.op("pe", lambda e, k=k: e.transpose(out=pt[:, k * 128:(k + 1) * 128], in_=hb[:, k * 128:(k + 1) * 128],
                                              identity=cm.ident[:, :]), r=[hb, cm.ident], w=[pt])
    p.op("act", lambda e: e.copy(out=hT_blk[:, :, j * 128:(j + 1) * 128],
                                 in_=pt[:, :].rearrange("p (k t) -> p k t", k=8)), r=[pt], w=[hT_blk])


def phase_tok(p, cm, l, NT, W, x_in, oT, x_out, hT_out, do_mix=True, do_ffn=True, gain_next=None, wout_rowmap=None):
    mk = p.mark()
    TB = 512
    nblk = NT // TB
    psA = RR([p.tile(f"psA{i}", [128, 512], F32, space="psum") for i in range(5)])
    psT = RR([p.tile(f"psT{i}", [128, 1024], BF16, space="psum") for i in range(2)])

    def gain_tile(name, ap1d):
        t = p.tile("g_" + name, [128, D], F32)
        bcast_load(p, t, ap1d, D)
        return t

    if do_mix:
        g_post = gain_tile("post", W["mix_post_norm"][l, :])
        wout = p.tile("wout", [128, 8, D], BF16)
        wo = W["w_out"].ap
        for k in range(8):
            r0 = wout_rowmap(k) if wout_rowmap else k * 128
            p.dma(wout[:, k, :], wo[l, r0:r0 + 128, :], w=[wout], eng="pool")
        oTb = p.tile("oTb", [128, 8, TB], BF16)
    if do_ffn:
        g_pre = gain_tile("pre", W["ffn_pre_norm"][l, :])
        g_fpost = gain_tile("fpost", W["ffn_post_norm"][l, :])
        wup_bufs = RR([p.tile(f"wup{i}", [128, 8, 512], BF16) for i in range(3)])
        wdn_bufs = RR([p.tile(f"wdn{i}", [128, 32, 256], BF16) for i in range(2)])
        fT = p.tile("fT", [128, 32, TB], BF16)
        h2T = p.tile("h2T", [128, 8, TB], BF16)
    if gain_next is not None:
        g_next = gain_tile("next", gain_next)
        hTn = RR([p.tile(f"hTn{i}", [128, 8, TB], BF16) for i in range(2)])
    xb = p.tile("xb", [128, 4, D], F32)
    fsb = p.tile("fsb", [128, 4, D], F32)
    ss4 = p.tile("ss4", [128, 4, 4], F32)
    sc = {
        "xt": xb,
        "junk": p.tile("junk", [128, D], BF16),
        "ss1": p.tile("ss1", [128, 1], F32), "rs1": p.tile("rs1", [128, 1], F32), "tmp1": p.tile("tmp1", [128, 1], F32),
        "sst": p.tile("sst", [128, 1], F32),
        "hb": p.tile("hb", [128, D], BF16),
    }
    tt = p.tile("tt", [128, D], F32)
    relu_t = RR([p.tile(f"relu{i}", [128, 512], BF16) for i in range(2)])

    x_in_v = x_in.ap.ap().rearrange("(n j q) d -> n q j d", j=4, q=128)
    x_out_v = x_out.ap.ap().rearrange("(n j q) d -> n q j d", j=4, q=128)

    def evac(ps, j, col0, ncol, slot):
        p.op("act", lambda e: e.activation(out=sc["junk"][:, 0:ncol], in_=ps[:, 0:ncol], func=AF.Square,
                                           accum_out=ss4[:, j, slot:slot + 1]), r=[ps], w=[ss4])
        p.op("act", lambda e: e.copy(out=fsb[:, j, col0:col0 + ncol], in_=ps[:, 0:ncol]), r=[ps], w=[fsb])

    def finish_residual(j, nslot, gain_t):
        p.op("dve", lambda e: e.tensor_reduce(out=sc["sst"][:, :], in_=ss4[:, j, 0:nslot], axis=AX.X, op=ALU.add),
             r=[ss4], w=[sc["sst"]])
        rstd_from_ss(p, sc["sst"], sc["rs1"], D, sc["tmp1"])
        p.op("dve", lambda e: e.scalar_tensor_tensor(out=tt[:, :], in0=fsb[:, j, :], scalar=sc["rs1"][:, 0:1], in1=gain_t[:, :],
                                                     op0=ALU.mult, op1=ALU.mult), r=[fsb, sc["rs1"], gain_t], w=[tt])
        p.op("pool", lambda e: e.tensor_tensor(out=xb[:, j, :], in0=xb[:, j, :], in1=tt[:, :], op=ALU.add),
             r=[tt, xb], w=[xb])

    for n in range(nblk):
        tok0 = n * TB
        p.dma(xb[:, :, :], x_in_v[n], r=[x_in], w=[xb])
        if do_mix:
            p.dma(oTb[:, :, :], oT.ap.ap().rearrange("(k q) t -> q k t", q=128)[:, :, tok0:tok0 + TB], r=[oT], w=[oTb])
            for j in range(4):
                for h in range(2):
                    ps = psA.next()
                    for k in range(8):
                        p.op("pe", lambda e, k=k, h=h, ps=ps, j=j: e.matmul(
                            ps[:, :], lhsT=oTb[:, k, j * 128:(j + 1) * 128], rhs=wout[:, k, h * 512:(h + 1) * 512],
                            start=(k == 0), stop=(k == 7)), r=[oTb, wout], w=[ps])
                    evac(ps, j, h * 512, 512, h)
                finish_residual(j, 2, g_post)
        if do_ffn:
            for j in range(4):
                norm_to_hT(p, cm, xb[:, j, :], g_pre, h2T, j, sc, psT)
            wu = W["w_up"].ap.ap()
            wd = W["w_down"].ap.ap()
            for g in range(8):
                wb = wup_bufs.next()
                p.dma(wb[:, :, :], wu[l].rearrange("(k q) f -> q k f", q=128)[:, :, g * 512:(g + 1) * 512], w=[wb], eng="pool")
                for cc in range(4):
                    c = g * 4 + cc
                    ps = psA.next()
                    for k in range(8):
                        p.op("pe", lambda e, k=k, cc=cc, ps=ps, wb=wb: e.matmul(
                            ps[:, :], lhsT=wb[:, k, cc * 128:(cc + 1) * 128], rhs=h2T[:, k, :],
                            start=(k == 0), stop=(k == 7)), r=[wb, h2T], w=[ps])
                    rt = relu_t.next()
                    p.op("act", lambda e, ps=ps, rt=rt: e.activation(out=rt[:, :], in_=ps[:, :], func=AF.Relu), r=[ps], w=[rt])
                    p.op("pool", lambda e, c=c, rt=rt: e.tensor_tensor(out=fT[:, c, :], in0=rt[:, :], in1=rt[:, :], op=ALU.mult),
                         r=[rt], w=[fT])
            for qd in range(4):
                wq = wdn_bufs.next()
                p.dma(wq[:, :, :], wd[l].rearrange("(c q) f -> q c f", q=128)[:, :, qd * 256:(qd + 1) * 256], w=[wq], eng="pool")
                for j in range(4):
                    ps = psA.next()
                    for c in range(32):
                        p.op("pe", lambda e, c=c, j=j, ps=ps, wq=wq: e.matmul(
                            ps[:, 0:256], lhsT=fT[:, c, j * 128:(j + 1) * 128], rhs=wq[:, c, :],
                            start=(c == 0), stop=(c == 31)), r=[fT, wq], w=[ps])
                    evac(ps, j, qd * 256, 256, qd)
            for j in range(4):
                finish_residual(j, 4, g_fpost)
        p.dma(x_out_v[n], xb[:, :, :], r=[xb], w=[x_out])
        if gain_next is not None:
            hb_ = hTn.next()
            for j in range(4):
                norm_to_hT(p, cm, xb[:, j, :], g_next, hb_, j, sc, psT)
            p.dma(hT_out.ap.ap().rearrange("(k q) t -> q k t", q=128)[:, :, tok0:tok0 + TB], hb_[:, :, :], r=[hb_], w=[hT_out])
    p.barrier()
    p.release(mk)


GQ, GK, GV, GG, CQ, CKV, KR, KROT, AB, NCOL = 0, 128, 256, 384, 512, 768, 896, 928, 960, 962
NEG = -1.0e30
import os
DBG = set(os.environ.get("MIXDBG", "").split(","))


def phase_mix(p, cm, l, S, Wc, hT_src, pos, oT_own, dbg=None):
    mk = p.mark()
    TB = 512
    nblk = S // TB
    NKB = S // 128
    banks = [p.tile(f"bank{i}", [128, 512], F32, space="psum") for i in range(8)]
    misc = RR(banks[0:2])
    psS = RR(banks[2:4])
    psO = banks[4:6]
    bR = banks[6]
    bO = banks[7]

    def bfv(bank):
        return bank.ap.bitcast(BF16)

    def t_(name, shape, dt=F32):
        return p.tile(name, shape, dt)

    ones_bf = t_("ones_bf", [128, 128], BF16)
    p.op("pool", lambda e: e.memset(ones_bf[:, :], 1.0), w=[ones_bf])
    ones64 = t_("ones64", [64, 64])
    p.op("pool", lambda e: e.memset(ones64[:, :], 1.0), w=[ones64])

    def mask_tile(name, shape, init, pattern, cmul, cmp_op, fill):
        t = t_(name, shape)
        p.op("pool", lambda e: e.memset(t.ap[:], init), w=[t])
        p.op("pool", lambda e: e.affine_select(out=t.ap[:], in_=t.ap[:], pattern=pattern, compare_op=cmp_op, fill=fill,
                                               base=0, channel_multiplier=cmul), r=[t], w=[t])
        return t

    U = mask_tile("mU", [64, 64], 1.0, [[1, 64]], -1, ALU.is_ge, 0.0)
    Ust = mask_tile("mUst", [64, 64], 1.0, [[-1, 64]], 1, ALU.is_gt, 0.0)
    Mneg = mask_tile("mMneg", [64, 64], 0.0, [[1, 64]], -1, ALU.is_ge, NEG)
    S01 = mask_tile("mS01", [64, 64], 1.0, [[1, 64]], -1, ALU.is_gt, 0.0)

    def b8(m):
        return m.ap[0:64, 0:64].unsqueeze(1).to_broadcast([64, 8, 64])
    c_eps128 = t_("c_eps128", [128, 1])
    p.op("pool", lambda e: e.memset(c_eps128[:, :], 128.0 * EPS), w=[c_eps128])
    c_one = t_("c_one", [128, 1])
    p.op("pool", lambda e: e.memset(c_one[:, :], 1.0), w=[c_one])

    Wown = t_("Wown", [128, 8, NCOL], BF16)
    wsrc = Wc["w_in_own"].ap.ap()[l].rearrange("(k q) c -> q k c", q=128)
    for k0 in range(0, 8, 2):
        p.dma(Wown[:, k0:k0 + 2, :], wsrc[:, k0:k0 + 2, :], w=[Wown], eng="pool")
    convw = t_("convw", [128, 3, 4])
    p.dma(convw[:, :, :], Wc["conv_own"].ap.ap()[l], w=[convw])
    gnorm = t_("gnorm", [64, 128])
    p.dma(gnorm[:, :], Wc["gdn_out_norm"].ap.ap()[l].partition_broadcast(64), w=[gnorm])
    nA = t_("nA", [64, 1])
    p.dma(nA[:, :], Wc["alog_own"].ap.ap()[l].partition_broadcast(64), w=[nA])
    p.op("act", lambda e: e.activation(out=nA[:, :], in_=nA[:, :], func=AF.Exp), r=[nA], w=[nA])
    p.op("dve", lambda e: e.tensor_scalar(out=nA[:, :], in0=nA[:, :], scalar1=-1.0, scalar2=None, op0=ALU.mult), r=[nA], w=[nA])
    dtb = t_("dtb", [64, 1])
    p.dma(dtb[:, :], Wc["dtb_own"].ap.ap()[l].partition_broadcast(64), w=[dtb])
    invf = t_("invf", [96, 1])
    p.dma(invf[:, :], Wc["invf"].ap.ap(), w=[invf])
    wq_f = t_("wq_f", [128, 2, 256])
    p.dma(wq_f[:, :, :], Wc["wq_own"].ap.ap()[l].rearrange("(k q) c -> q k c", q=128), w=[wq_f])
    qg = t_("qg", [128, 2])
    p.dma(qg[:, :], Wc["q_norm"].ap.ap()[l].rearrange("(k q) -> q k", q=128), w=[qg], allow_slow_non_contiguous=True)
    wq = t_("wq", [128, 2, 256], BF16)
    p.op("dve", lambda e: e.tensor_tensor(out=wq[:, :, :], in0=wq_f[:, :, :], in1=qg[:, :].unsqueeze(2).to_broadcast([128, 2, 256]),
                                          op=ALU.mult), r=[wq_f, qg], w=[wq])
    wkv_f = t_("wkv_f", [128, 256])
    p.dma(wkv_f[:, :], Wc["wkv_own"].ap.ap()[l], w=[wkv_f])
    kvg = t_("kvg", [128, 1])
    p.dma(kvg[:, :], Wc["kv_norm"].ap.ap()[l].rearrange("(q o) -> q o", o=1), w=[kvg])
    wkv = t_("wkv", [128, 256], BF16)
    p.op("dve", lambda e: e.tensor_scalar(out=wkv[:, :], in0=wkv_f[:, :], scalar1=kvg[:, 0:1], scalar2=None, op0=ALU.mult),
         r=[wkv_f, kvg], w=[wkv])

    kT = [t_(f"kT{h}", [96, S], BF16) for h in range(2)]
    vaug = [t_(f"vaug{h}", [128, NKB, 65], BF16) for h in range(2)]
    kTb = [[T(kT[h].ap, f"kT{h}_{i}") for i in range(nblk)] for h in range(2)]
    vab = [[T(vaug[h].ap, f"va{h}_{i}") for i in range(nblk)] for h in range(2)]
    for h in range(2):
        p.op("pool", lambda e, h=h: e.memset(vaug[h][:, :, 64:65], 1.0), w=vab[h])

    hTb = RR([t_(f"hTb{i}", [128, 8, TB], BF16) for i in range(2)])
    raw = [t_(f"raw{s}", [128, 3 + TB]) for s in range(3)]
    for s in range(3):
        p.op("pool", lambda e, s=s: e.memset(raw[s][:, 0:3], 0.0), w=[raw[s]])
    cacc = t_("cacc", [128, TB])
    sil1 = t_("sil", [128, TB])
    sqb = t_("sqb", [128, 2, TB], BF16)
    rn = t_("rn", [128, TB])
    rtmp = t_("rtmp", [128, TB])
    qTg = t_("qTg", [128, TB], BF16)
    kTg = t_("kTg", [128, TB], BF16)
    vTg = t_("vTg", [128, TB], BF16)
    gate = t_("gate", [128, TB])
    cqT = t_("cqT", [128, 2, TB], BF16)
    ckvT = t_("ckvT", [128, TB], BF16)
    rstd_q = t_("rstd_q", [128, TB])
    rstd_kv = t_("rstd_kv", [128, TB])
    rkv_col = t_("rkv_col", [128, 4])
    rkv_tmp = t_("rkv_tmp", [128, 4])
    ab_row = t_("ab_row", [2, TB])
    ab_col = t_("ab_col", [64, 8, 2])
    la_col = t_("la_col", [64, 8])
    sp_t = t_("sp_t", [64, 8])
    beta_col = t_("beta_col", [64, 8])
    nbeta_col = t_("nbeta_col", [64, 8])
    eg_col = t_("eg_col", [64, 16])
    X1 = t_("X1", [64, 8, 128])
    X2 = t_("X2", [64, 8, 64])
    DT = t_("DT", [64, 8, 64])
    DST = t_("DST", [64, 8, 64])
    tmpA = t_("tmpA", [64, 8, 64])
    chX = [t_(f"chX{i}", [64, 8, 64]) for i in range(2)]
    chY = [t_(f"chY{i}", [64, 8, 64]) for i in range(2)]
    chT = [t_(f"chT{i}", [64, 8, 64]) for i in range(2)]
    TTb = t_("TTb", [64, 8, 64], BF16)
    PT = t_("PTg", [64, 8, 64], BF16)
    EGB = t_("EGB", [128, 8, 64])
    qdecT = t_("qdecT", [128, TB], BF16)
    kTok = t_("kTok", [64, 8, 128], BF16)
    vTok = t_("vTok", [64, 8, 128], BF16)
    keg = t_("keg", [64, 8, 128], BF16)
    kdec = t_("kdec", [64, 8, 128], BF16)
    bu = t_("bu", [64, 8, 128])
    wT = t_("wT", [128, 8, 64], BF16)
    Sst = t_("Sst", [128, 128])
    Sb = t_("Sb", [128, 128], BF16)
    p.op("pool", lambda e: e.memset(Sst[:, :], 0.0), w=[Sst])
    p.op("pool", lambda e: e.memset(Sb[:, :], 0.0), w=[Sb])
    vnew = RR([t_(f"vnew{i}", [64, 128], BF16) for i in range(2)])
    osq = t_("osq", [64, 4, 128])
    oss = t_("oss", [64, 4])
    ors = t_("ors", [64, 4])
    otmp = t_("otmp", [64, 4])
    on1 = t_("on1", [64, 4, 128])
    on2 = t_("on2", [64, 4, 128], BF16)
    oTg = t_("oTg", [128, TB], BF16)
    posi = t_("posi", [96, TB], I32)
    ph = t_("ph", [96, 2, TB])
    phi = t_("phi", [96, 2, TB], I32)
    phm = t_("phm", [96, 2, TB])
    CS = t_("CS", [96, 2, TB])
    u1 = t_("u1", [96, TB])
    u2 = t_("u2", [96, TB])
    qT = [t_(f"qT{h}", [96, TB], BF16) for h in range(2)]
    PTa = RR([t_(f"PTa{i}", [128, TB], BF16) for i in range(3)])
    rden = t_("rden", [128, 4])
    on_tok = t_("on_tok", [128, 4, 128], BF16)
    oTm = t_("oTm", [128, TB], BF16)
    SCALE = float(96 ** -0.5)

    def mm(out_ap, lhsT, rhs, start, stop, r, w):
        p.op("pe", lambda e: e.matmul(out_ap, lhsT=lhsT, rhs=rhs, start=start, stop=stop, skip_group_check=True), r=r, w=w)

    def proj_group(hb, c0, M):
        ps = misc.next()
        for k in range(8):
            mm(ps[0:M, :], Wown[:, k, c0:c0 + M], hb[:, k, :], k == 0, k == 7, [Wown, hb], [ps])
        return ps

    for t in range(nblk):
        tok0 = t * TB
        hb = hTb.next()
        p.dma(hb[:, :, :], hT_src(tok0), w=[hb])
        p.dma(posi[64:96, :], pos.ap.ap()[tok0:tok0 + TB].partition_broadcast(32), r=[pos], w=[posi])

        for s, c0 in enumerate((GQ, GK, GV)):
            if t > 0:
                p.op("pool", lambda e, s=s: e.tensor_copy(out=raw[s][:, 0:3], in_=raw[s][:, TB:TB + 3]), r=[raw[s]], w=[raw[s]])
            ps = proj_group(hb, c0, 128)
            p.op("act", lambda e, s=s, ps=ps: e.copy(out=raw[s][:, 3:3 + TB], in_=ps[:, :]), r=[ps], w=[raw[s]])
        ps = proj_group(hb, GG, 128)
        p.op("act", lambda e, ps=ps: e.activation(out=gate[:, :], in_=ps[:, :], func=AF.Silu), r=[ps], w=[gate])
        for kc in range(2):
            ps = proj_group(hb, CQ + kc * 128, 128)
            p.op("act", lambda e, ps=ps, kc=kc: e.copy(out=cqT[:, kc, :], in_=ps[:, :]), r=[ps], w=[cqT])
            p.op("act", lambda e, ps=ps, kc=kc: e.activation(out=sqb[:, kc, :], in_=ps[:, :], func=AF.Square), r=[ps], w=[sqb])
        ps = misc.next()
        for kc in range(2):
            mm(ps[:, :], ones_bf[:, :], sqb[:, kc, :], kc == 0, kc == 1, [ones_bf, sqb], [ps])
        p.op("act", lambda e, ps=ps: e.activation(out=rtmp[:, :], in_=ps[:, :], func=AF.Sqrt, bias=p.eps_tile[:, 0:1], scale=1.0 / 256),
             r=[ps, p.eps_tile], w=[rtmp])
        p.op("dve", lambda e: e.reciprocal(out=rstd_q[:, :], in_=rtmp[:, :]), r=[rtmp], w=[rstd_q])
        ps = proj_group(hb, CKV, 128)
        p.op("act", lambda e, ps=ps: e.copy(out=ckvT[:, :], in_=ps[:, :]), r=[ps], w=[ckvT])
        p.op("act", lambda e, ps=ps: e.activation(out=sqb[:, 0, :], in_=ps[:, :], func=AF.Square), r=[ps], w=[sqb])
        ps = misc.next()
        mm(ps[:, :], ones_bf[:, :], sqb[:, 0, :], True, True, [ones_bf, sqb], [ps])
        p.op("act", lambda e, ps=ps: e.activation(out=rtmp[:, :], in_=ps[:, :], func=AF.Sqrt, bias=p.eps_tile[:, 0:1], scale=1.0 / 128),
             r=[ps, p.eps_tile], w=[rtmp])
        p.op("dve", lambda e: e.reciprocal(out=rstd_kv[:, :], in_=rtmp[:, :]), r=[rtmp], w=[rstd_kv])
        ps = misc.next()
        for j in range(4):
            mm(ps[:, j:j + 1], sqb[:, 0, j * 128:(j + 1) * 128], ones_bf[:, 0:1], j == 0, j == 3, [ones_bf, sqb], [ps])
        p.op("act", lambda e, ps=ps: e.activation(out=rkv_tmp[:, :], in_=ps[:, 0:4], func=AF.Sqrt, bias=p.eps_tile[:, 0:1], scale=1.0 / 128),
             r=[ps, p.eps_tile], w=[rkv_tmp])
        p.op("dve", lambda e: e.reciprocal(out=rkv_col[:, :], in_=rkv_tmp[:, :]), r=[rkv_tmp], w=[rkv_col])

        R_ = slice(64, 96)
        p.op("dve", lambda e: e.tensor_copy(out=u1[R_, :], in_=posi[R_, :]), r=[posi], w=[u1])
        p.op("dve", lambda e: e.tensor_scalar(out=ph[R_, 0, :], in0=u1[R_, :], scalar1=invf[R_, 0:1], scalar2=None, op0=ALU.mult),
             r=[u1, invf], w=[ph])
        p.op("dve", lambda e: e.tensor_scalar(out=ph[R_, 1, :], in0=ph[R_, 0, :], scalar1=0.25, scalar2=None, op0=ALU.add),
             r=[ph], w=[ph])
        p.op("dve", lambda e: e.tensor_copy(out=phi[R_, :, :], in_=ph[R_, :, :]), r=[ph], w=[phi])
        p.op("pool", lambda e: e.tensor_copy(out=phm[R_, :, :], in_=phi[R_, :, :]), r=[phi], w=[phm])
        p.op("pool", lambda e: e.tensor_tensor(out=ph[R_, :, :], in0=ph[R_, :, :], in1=phm[R_, :, :], op=ALU.subtract), r=[ph, phm], w=[ph])
        p.op("pool", lambda e: e.tensor_scalar(out=phm[R_, :, :], in0=ph[R_, :, :], scalar1=0.5, scalar2=None, op0=ALU.is_gt), r=[ph], w=[phm])
        p.op("pool", lambda e: e.tensor_tensor(out=ph[R_, :, :], in0=ph[R_, :, :], in1=phm[R_, :, :], op=ALU.subtract), r=[ph, phm], w=[ph])
        p.op("pool", lambda e: e.tensor_scalar(out=phm[R_, :, :], in0=ph[R_, :, :], scalar1=-0.5, scalar2=None, op0=ALU.is_lt), r=[ph], w=[phm])
        p.op("pool", lambda e: e.tensor_tensor(out=ph[R_, :, :], in0=ph[R_, :, :], in1=phm[R_, :, :], op=ALU.add), r=[ph, phm], w=[ph])
        p.op("act", lambda e: e.activation(out=CS[R_, :, :], in_=ph[R_, :, :], func=AF.Sin, scale=float(2 * np.pi)), r=[ph], w=[CS])

        psKA = proj_group(hb, KR - 64, 96)
        p.op("dve", lambda e, ps=psKA: e.tensor_tensor(out=u1[R_, :], in0=ps[R_, :], in1=CS[R_, 1, :], op=ALU.mult), r=[psKA, CS], w=[u1])
        psKB = proj_group(hb, KROT - 64, 96)
        p.op("dve", lambda e, ps=psKB: e.tensor_tensor(out=u2[R_, :], in0=ps[R_, :], in1=CS[R_, 0, :], op=ALU.mult), r=[psKB, CS], w=[u2])
        for h in range(2):
            p.op("pool", lambda e, h=h: e.tensor_tensor(out=kT[h][R_, tok0:tok0 + TB], in0=u1[R_, :], in1=u2[R_, :], op=ALU.add),
                 r=[u1, u2], w=[kTb[h][t]])
        ps = proj_group(hb, AB, 2)
        p.op("act", lambda e, ps=ps: e.copy(out=ab_row[:, :], in_=ps[0:2, :]), r=[ps], w=[ab_row])

        for h in range(2):
            psA = misc.next()
            for kc in range(2):
                mm(psA[0:96, :], wq[:, kc, h * 128:h * 128 + 96], cqT[:, kc, :], kc == 0, kc == 1, [wq, cqT], [psA])
            p.op("dve", lambda e, h=h, ps=psA: e.tensor_tensor(out=qT[h][0:64, :], in0=ps[0:64, :], in1=rstd_q[0:64, :], op=ALU.mult),
                 r=[psA, rstd_q], w=[qT[h]])
            p.op("dve", lambda e, ps=psA: e.tensor_tensor(out=u1[R_, :], in0=ps[R_, :], in1=CS[R_, 1, :], op=ALU.mult), r=[psA, CS], w=[u1])
            psB = misc.next()
            for kc in range(2):
                mm(psB[0:96, :], wq[:, kc, h * 128 + 32:h * 128 + 128], cqT[:, kc, :], kc == 0, kc == 1, [wq, cqT], [psB])
            p.op("dve", lambda e, ps=psB: e.tensor_tensor(out=u2[R_, :], in0=ps[R_, :], in1=CS[R_, 0, :], op=ALU.mult), r=[psB, CS], w=[u2])
            p.op("pool", lambda e: e.tensor_tensor(out=u1[R_, :], in0=u1[R_, :], in1=u2[R_, :], op=ALU.add), r=[u1, u2], w=[u1])
            p.op("pool", lambda e, h=h: e.tensor_tensor(out=qT[h][R_, :], in0=u1[R_, :], in1=rstd_q[R_, :], op=ALU.mult),
                 r=[u1, rstd_q], w=[qT[h]])
            psK = misc.next()
            mm(psK[0:64, :], wkv[:, h * 64:(h + 1) * 64], ckvT[:, :], True, True, [wkv, ckvT], [psK])
            p.op("dve", lambda e, h=h, ps=psK: e.tensor_tensor(out=kT[h][0:64, tok0:tok0 + TB], in0=ps[0:64, :], in1=rstd_kv[0:64, :],
                                                               op=ALU.mult), r=[psK, rstd_kv], w=[kTb[h][t]])
        psV = misc.next()
        for j in range(4):
            mm(psV[:, j * 128:(j + 1) * 128], ckvT[:, j * 128:(j + 1) * 128], wkv[:, 128:256], j == 0, j == 3, [ckvT, wkv], [psV])
        for h in range(2):
            p.op("dve", lambda e, h=h, ps=psV: e.tensor_tensor(
                out=vaug[h][:, 4 * t:4 * t + 4, 0:64],
                in0=ps[:, :].rearrange("q (j c) -> q j c", j=4)[:, :, h * 64:(h + 1) * 64],
                in1=rkv_col[:, :].unsqueeze(2).to_broadcast([128, 4, 64]), op=ALU.mult), r=[psV, rkv_col], w=[vab[h][t]])

        if 'nogdn' not in DBG:
            for s, dst in ((0, qTg), (1, kTg), (2, vTg)):
                p.op("dve", lambda e, s=s: e.tensor_scalar(out=cacc[:, :], in0=raw[s][:, 0:TB], scalar1=convw[:, s, 0:1], scalar2=None,
                                                           op0=ALU.mult), r=[raw[s], convw], w=[cacc])
                for k in range(1, 4):
                    p.op("dve", lambda e, s=s, k=k: e.scalar_tensor_tensor(out=cacc[:, :], in0=raw[s][:, k:k + TB], scalar=convw[:, s, k:k + 1],
                                                                           in1=cacc[:, :], op0=ALU.mult, op1=ALU.add),
                         r=[raw[s], convw, cacc], w=[cacc])
                if s == 2:
                    p.op("act", lambda e, dst=dst: e.activation(out=dst[:, :], in_=cacc[:, :], func=AF.Silu), r=[cacc], w=[dst])
                    continue
                p.op("act", lambda e: e.activation(out=sil1[:, :], in_=cacc[:, :], func=AF.Silu), r=[cacc], w=[sil1])
                p.op("act", lambda e: e.activation(out=sqb[:, 0, :], in_=sil1[:, :], func=AF.Square), r=[sil1], w=[sqb])
                ps = misc.next()
                mm(ps[:, :], ones_bf[:, :], sqb[:, 0, :], True, True, [ones_bf, sqb], [ps])
                if s == 0:
                    p.op("act", lambda e, ps=ps: e.activation(out=rtmp[:, :], in_=ps[:, :], func=AF.Sqrt, bias=c_eps128[:, 0:1], scale=128.0),
                         r=[ps, c_eps128], w=[rtmp])
                else:
                    p.op("act", lambda e, ps=ps: e.activation(out=rtmp[:, :], in_=ps[:, :], func=AF.Sqrt, bias=p.eps_tile[:, 0:1], scale=1.0),
                         r=[ps, p.eps_tile], w=[rtmp])
                p.op("dve", lambda e: e.reciprocal(out=rn[:, :], in_=rtmp[:, :]), r=[rtmp], w=[rn])
                p.op("dve", lambda e, dst=dst: e.tensor_tensor(out=dst[:, :], in0=sil1[:, :], in1=rn[:, :], op=ALU.mult),
                     r=[sil1, rn], w=[dst])
            for src, dst in ((kTg, kTok), (vTg, vTok)):
                ps = misc.next()
                for c in range(8):
                    p.op("pe", lambda e, c=c, ps=ps, src=src: e.transpose(out=bfv(ps)[0:64, c * 128:(c + 1) * 128],
                                                                          in_=src[:, c * 64:(c + 1) * 64], identity=cm.ident[:, :]),
                         r=[src, cm.ident], w=[ps])
                p.op("act", lambda e, ps=ps, dst=dst: e.copy(out=dst[:, :, :], in_=bfv(ps)[0:64, :].rearrange("q (c d) -> q c d", c=8)),
                     r=[ps], w=[dst])
            ps = misc.next()
            for c in range(8):
                p.op("pe", lambda e, c=c, ps=ps: e.transpose(out=ps[0:64, 2 * c:2 * c + 2], in_=ab_row[0:2, c * 64:(c + 1) * 64],
                                                             identity=cm.identf[0:2, 0:2]), r=[ab_row, cm.identf], w=[ps])
            p.op("act", lambda e, ps=ps: e.copy(out=ab_col[:, :, :], in_=ps[0:64, 0:16].rearrange("q (c two) -> q c two", two=2)),
                 r=[ps], w=[ab_col])
            p.op("act", lambda e: e.activation(out=beta_col[:, :], in_=ab_col[:, :, 1], func=AF.Sigmoid), r=[ab_col], w=[beta_col])
            p.op("dve", lambda e: e.tensor_scalar(out=nbeta_col[:, :], in0=beta_col[:, :], scalar1=-1.0, scalar2=None, op0=ALU.mult),
                 r=[beta_col], w=[nbeta_col])
            p.op("act", lambda e: e.activation(out=sp_t[:, :], in_=ab_col[:, :, 0], func=AF.Exp, bias=dtb[:, 0:1]), r=[ab_col, dtb], w=[sp_t])
            p.op("act", lambda e: e.activation(out=sp_t[:, :], in_=sp_t[:, :], func=AF.Ln, bias=c_one[0:64, 0:1]), r=[sp_t, c_one], w=[sp_t])
            p.op("dve", lambda e: e.tensor_scalar(out=la_col[:, :], in0=sp_t[:, :], scalar1=nA[:, 0:1], scalar2=None, op0=ALU.mult),
                 r=[sp_t, nA], w=[la_col])
            p.op("dve", lambda e: e.tensor_copy(out=X1[:, :, :], in_=la_col[:, :].unsqueeze(2).to_broadcast([64, 8, 128])), r=[la_col], w=[X1])
            p.op("dve", lambda e: e.tensor_scalar(out=sp_t[:, :], in0=la_col[:, :], scalar1=-1.0, scalar2=None, op0=ALU.mult),
                 r=[la_col], w=[sp_t])
            p.op("pool", lambda e: e.tensor_tensor(out=X2[:, :, :], in0=b8(U), in1=sp_t[:, :].unsqueeze(2).to_broadcast([64, 8, 64]),
                                                   op=ALU.mult), r=[U, sp_t], w=[X2])
            ps = misc.next()
            mm(ps[0:64, 0:8], U[:, :], la_col[:, :], True, True, [U, la_col], [ps])
            mm(ps[0:64, 8:16], Ust[:, :], la_col[:, :], False, True, [Ust, la_col], [ps])
            p.op("act", lambda e, ps=ps: e.activation(out=eg_col[:, :], in_=ps[0:64, 0:16], func=AF.Exp), r=[ps], w=[eg_col])
            ps = misc.next()
            for c in range(8):
                mm(ps[0:64, c * 64:(c + 1) * 64], X1[:, c, 0:64], U[:, :], c == 0, False, [X1, U], [ps])
                mm(ps[0:64, c * 64:(c + 1) * 64], X2[:, c, :], ones64[:, :], False, True, [X2, ones64], [ps])
            p.op("dve", lambda e, ps=ps: e.tensor_tensor(out=tmpA[:, :, :], in0=ps[0:64, :].rearrange("q (c i) -> q c i", c=8),
                                                         in1=b8(Mneg), op=ALU.add), r=[ps, Mneg], w=[tmpA])
            p.op("act", lambda e: e.activation(out=DT[:, :, :], in_=tmpA[:, :, :], func=AF.Exp), r=[tmpA], w=[DT])
            p.op("pool", lambda e: e.tensor_tensor(out=DST[:, :, :], in0=DT[:, :, :], in1=b8(S01), op=ALU.mult), r=[DT, S01], w=[DST])
            ps = misc.next()
            for c in range(8):
                mm(ps[:, c * 64:(c + 1) * 64], X1[:, c, :], U[:, :], c == 0, c == 7, [X1, U], [ps])
            p.op("act", lambda e, ps=ps: e.activation(out=EGB[:, :, :], in_=ps[:, :].rearrange("q (c i) -> q c i", c=8), func=AF.Exp),
                 r=[ps], w=[EGB])
            p.op("dve", lambda e: e.tensor_tensor(out=qdecT[:, :], in0=qTg[:, :], in1=EGB[:, :, :].rearrange("q c i -> q (c i)"), op=ALU.mult),
                 r=[qTg, EGB], w=[qdecT])
            ps = misc.next()
            for c in range(8):
                mm(ps[0:64, c * 64:(c + 1) * 64], kTg[:, c * 64:(c + 1) * 64], kTg[:, c * 64:(c + 1) * 64], c == 0, c == 7, [kTg], [ps])
            Y = chY[0]
            p.op("dve", lambda e, ps=ps: e.tensor_tensor(out=tmpA[:, :, :], in0=ps[0:64, :].rearrange("q (c i) -> q c i", c=8),
                                                         in1=DST[:, :, :], op=ALU.mult), r=[ps, DST], w=[tmpA])
            p.op("pool", lambda e, Y=Y: e.tensor_tensor(out=Y[:, :, :], in0=tmpA[:, :, :],
                                                        in1=nbeta_col[:, :].unsqueeze(2).to_broadcast([64, 8, 64]), op=ALU.mult),
                 r=[tmpA, nbeta_col], w=[Y])
            ps = misc.next()
            for c in range(8):
                mm(ps[0:64, c * 64:(c + 1) * 64], kTg[:, c * 64:(c + 1) * 64], qTg[:, c * 64:(c + 1) * 64], c == 0, c == 7, [kTg, qTg], [ps])
            p.op("dve", lambda e, ps=ps: e.tensor_tensor(out=PT[:, :, :], in0=ps[0:64, :].rearrange("q (c i) -> q c i", c=8),
                                                         in1=DT[:, :, :], op=ALU.mult), r=[ps, DT], w=[PT])
            X = chX[0]
            ps = misc.next()
            for c in range(8):
                p.op("pe", lambda e, c=c, ps=ps, Y=Y: e.transpose(out=ps[0:64, c * 64:(c + 1) * 64], in_=Y[:, c, :],
                                                                  identity=cm.identf[0:64, 0:64]), r=[Y, cm.identf], w=[ps])
            p.op("act", lambda e, ps=ps, X=X: e.copy(out=X[:, :, :], in_=ps[0:64, :].rearrange("q (c i) -> q c i", c=8)), r=[ps], w=[X])
            TT = chT[0]
            p.op("pool", lambda e, TT=TT, Y=Y: e.tensor_tensor(out=TT[:, :, :], in0=Y[:, :, :], in1=b8(cm.identf), op=ALU.add), r=[Y, cm.identf], w=[TT])
            for lvl in range(5):
                Xn, Yn, Tn = chX[(lvl + 1) % 2], chY[(lvl + 1) % 2], chT[(lvl + 1) % 2]
                ps = misc.next()
                for c in range(8):
                    mm(ps[0:64, c * 64:(c + 1) * 64], Y[:, c, :], X[:, c, :], c == 0, c == 7, [X, Y], [ps])
                p.op("act", lambda e, ps=ps, Xn=Xn: e.copy(out=Xn[:, :, :], in_=ps[0:64, :].rearrange("q (c i) -> q c i", c=8)), r=[ps], w=[Xn])
                if lvl < 4:
                    ps = misc.next()
                    for c in range(8):
                        mm(ps[0:64, c * 64:(c + 1) * 64], X[:, c, :], Y[:, c, :], c == 0, c == 7, [X, Y], [ps])
                    p.op("dve", lambda e, ps=ps, Yn=Yn: e.tensor_copy(out=Yn[:, :, :], in_=ps[0:64, :].rearrange("q (c i) -> q c i", c=8)),
                         r=[ps], w=[Yn])
                ps = misc.next()
                for c in range(8):
                    mm(ps[0:64, c * 64:(c + 1) * 64], Xn[:, c, :], TT[:, c, :], c == 0, c == 7, [Xn, TT], [ps])
                p.op("dve", lambda e, ps=ps, Tn=Tn, TT=TT: e.tensor_tensor(out=Tn[:, :, :], in0=ps[0:64, :].rearrange("q (c i) -> q c i", c=8),
                                                                           in1=TT[:, :, :], op=ALU.add), r=[ps, TT], w=[Tn])
                X, Y, TT = Xn, Yn, Tn
            p.op("act", lambda e, TT=TT: e.copy(out=TTb[:, :, :], in_=TT[:, :, :]), r=[TT], w=[TTb])
            p.op("pool", lambda e: e.tensor_tensor(out=keg[:, :, :], in0=kTok[:, :, :], in1=eg_col[:, 0:8].unsqueeze(2).to_broadcast([64, 8, 128]),
                                                   op=ALU.mult), r=[kTok, eg_col], w=[keg])
            p.op("pool", lambda e: e.tensor_tensor(out=kdec[:, :, :], in0=kTok[:, :, :], in1=eg_col[:, 8:16].unsqueeze(2).to_broadcast([64, 8, 128]),
                                                   op=ALU.mult), r=[kTok, eg_col], w=[kdec])
            for half in range(2):
                ps = misc.next()
                for cc in range(4):
                    c = half * 4 + cc
                    mm(ps[0:64, cc * 128:(cc + 1) * 128], TTb[:, c, :], vTok[:, c, :], cc == 0, cc == 3, [TTb, vTok], [ps])
                p.op("dve", lambda e, ps=ps, half=half: e.tensor_tensor(
                    out=bu[:, half * 4:half * 4 + 4, :], in0=ps[0:64, :].rearrange("q (c d) -> q c d", c=4),
                    in1=beta_col[:, half * 4:half * 4 + 4].unsqueeze(2).to_broadcast([64, 4, 128]), op=ALU.mult), r=[ps, beta_col], w=[bu])
            ps = misc.next()
            for c in range(8):
                mm(ps[:, c * 64:(c + 1) * 64], keg[:, c, :], TTb[:, c, :], c == 0, c == 7, [keg, TTb], [ps])
            p.op("act", lambda e, ps=ps: e.copy(out=wT[:, :, :], in_=ps[:, :].rearrange("q (c i) -> q c i", c=8)), r=[ps], w=[wT])

        def gdn_a(c):
            mm(bR[0:64, 0:128], wT[:, c, :], Sb[:, :], True, True, [wT, Sb], [bR])
            vn = vnew.next()
            p.op("dve", lambda e: e.scalar_tensor_tensor(out=vn[:, :], in0=bR[0:64, 0:128], scalar=nbeta_col[:, c:c + 1], in1=bu[:, c, :],
                                                         op0=ALU.mult, op1=ALU.add), r=[bR, nbeta_col, bu], w=[vn])
            return vn

        def gdn_b(c, vn):
            cc = c % 4
            mm(bO[0:64, cc * 128:(cc + 1) * 128], qdecT[:, c * 64:(c + 1) * 64], Sb[:, :], cc == 0, False, [qdecT, Sb], [bO])
            mm(bO[0:64, cc * 128:(cc + 1) * 128], PT[:, c, :], vn[:, :], False, True, [PT, vn], [bO])
            mm(bR[:, 128:256], kdec[:, c, :], vn[:, :], True, True, [kdec, vn], [bR])
            p.op("dve", lambda e: e.scalar_tensor_tensor(out=Sb[:, :], in0=Sst[:, :], scalar=EGB[:, c, 63:64], in1=bR[:, 128:256],
                                                         op0=ALU.mult, op1=ALU.add), r=[Sst, EGB, bR], w=[Sb])
            p.op("dve", lambda e: e.scalar_tensor_tensor(out=Sst[:, :], in0=Sst[:, :], scalar=EGB[:, c, 63:64], in1=bR[:, 128:256],
                                                         op0=ALU.mult, op1=ALU.add), r=[Sst, EGB, bR], w=[Sst])
            if cc == 3:
                gdn_out(c - 3)

        def gdn_out(c0):
            o3 = bO[0:64, :].rearrange("q (c d) -> q c d", c=4)
            p.op("act", lambda e: e.activation(out=osq[:, :, :], in_=o3, func=AF.Square), r=[bO], w=[osq])
            p.op("dve", lambda e: e.tensor_reduce(out=oss[:, :], in_=osq[:, :, :], axis=AX.X, op=ALU.add), r=[osq], w=[oss])
            p.op("act", lambda e: e.activation(out=otmp[:, :], in_=oss[:, :], func=AF.Sqrt, bias=p.eps_tile[0:64, 0:1], scale=1.0 / 128),
                 r=[oss, p.eps_tile], w=[otmp])
            p.op("dve", lambda e: e.reciprocal(out=ors[:, :], in_=otmp[:, :]), r=[otmp], w=[ors])
            p.op("dve", lambda e: e.tensor_tensor(out=on1[:, :, :], in0=o3, in1=ors[:, :].unsqueeze(2).to_broadcast([64, 4, 128]), op=ALU.mult),
                 r=[bO, ors], w=[on1])
            p.op("pool", lambda e: e.tensor_tensor(out=on2[:, :, :], in0=on1[:, :, :], in1=gnorm[:, :].unsqueeze(1).to_broadcast([64, 4, 128]),
                                                   op=ALU.mult), r=[on1, gnorm], w=[on2])
            ps = misc.next()
            for cc in range(4):
                p.op("pe", lambda e, cc=cc, ps=ps: e.transpose(out=bfv(ps)[:, cc * 64:(cc + 1) * 64], in_=on2[:, cc, :],
                                                               identity=cm.ident[0:64, 0:64]), r=[on2, cm.ident], w=[ps])
            p.op("dve", lambda e, ps=ps: e.tensor_tensor(out=oTg[:, c0 * 64:c0 * 64 + 256], in0=bfv(ps)[:, 0:256],
                                                         in1=gate[:, c0 * 64:c0 * 64 + 256], op=ALU.mult), r=[ps, gate], w=[oTg])

        def attn_unit(h, kb):
            r_ = kb - 4 * t
            q0 = max(0, r_) * 128
            nq = TB - q0
            ps = psS.next()
            mm(ps[:, 0:nq], kT[h][:, kb * 128:(kb + 1) * 128], qT[h][:, q0:TB], True, True, [kTb[h][kb // 4], qT[h]], [ps])
            pt = PTa.next()
            p.op("act", lambda e: e.activation(out=pt[:, 0:nq], in_=ps[:, 0:nq], func=AF.Exp, scale=SCALE), r=[ps], w=[pt])
            if r_ >= 0:
                p.op("pool", lambda e: e.memset(pt[64:128, 0:64], 0.0), r=[], w=[pt])
            for qs in range(q0 // 128, 4):
                first = (kb == 0 and qs == 0)
                last = (kb == 4 * t + qs)
                mm(psO[h][:, qs * 65:(qs + 1) * 65], pt[:, qs * 128 - q0:(qs + 1) * 128 - q0], vaug[h][:, kb, :], first, last,
                   [pt, vab[h][kb // 4]], [psO[h]])

        def attn_out():
            for h in range(2):
                o3 = psO[h][:, 0:260].rearrange("q (s c) -> q s c", s=4)
                p.op("dve", lambda e, o3=o3: e.reciprocal(out=rden[:, :], in_=o3[:, :, 64]), r=[psO[h]], w=[rden])
                p.op("dve", lambda e, o3=o3, h=h: e.tensor_tensor(out=on_tok[:, :, h * 64:(h + 1) * 64], in0=o3[:, :, 0:64],
                                                                  in1=rden[:, :].unsqueeze(2).to_broadcast([128, 4, 64]), op=ALU.mult),
                     r=[psO[h], rden], w=[on_tok])
            ps = misc.next()
            for qs in range(4):
                p.op("pe", lambda e, qs=qs, ps=ps: e.transpose(out=bfv(ps)[:, qs * 128:(qs + 1) * 128], in_=on_tok[:, qs, :],
                                                               identity=cm.ident[:, :]), r=[on_tok, cm.ident], w=[ps])
            p.op("act", lambda e, ps=ps: e.copy(out=oTm[:, :], in_=bfv(ps)[:, 0:512]), r=[ps], w=[oTm])
            p.dma(oT_own.ap.ap()[128:256, tok0:tok0 + TB], oTm[:, :], r=[oTm], w=[oT_own])

        units = [(h, kb) for h in range(2) for kb in range(4 * t + 4)]
        if 'noattn' in DBG:
            units = []
        nu = len(units)
        steps = []
        for c in range(8):
            steps.append(("a", c))
            steps.append(("b", c))
        if 'nogdn' in DBG or 'norec' in DBG:
            steps = [("x", 0)]
        ui = 0
        vn_cur = {}
        for si, (kind, c) in enumerate(steps):
            if kind == "a":
                vn_cur[c] = gdn_a(c)
            elif kind == "b":
                gdn_b(c, vn_cur[c])
            target = (si + 1) * nu // len(steps)
            while ui < target:
                attn_unit(*units[ui])
                ui += 1
        while ui < nu:
            attn_unit(*units[ui])
            ui += 1
        if 'nogdn' not in DBG and 'norec' not in DBG:
            p.dma(oT_own.ap.ap()[0:128, tok0:tok0 + TB], oTg[:, :], r=[oTg], w=[oT_own])
        if 'noattn' not in DBG:
            attn_out()
    p.barrier()
    p.release(mk)


def core_weight_slices(inp, r):
    f32 = np.float32
    w_in = inp["w_in"]
    L = w_in.shape[0]
    o_q, o_k, o_v, o_g, o_a, o_b, o_cq, o_ckv, o_kr = 0, 512, 1024, 1536, 2048, 2052, 2056, 2312, 2440
    hs = slice(r * 128, (r + 1) * 128)
    kr = w_in[:, :, o_kr:o_kr + 32]
    krot = np.concatenate([kr[:, :, 16:32], kr[:, :, 0:16]], axis=-1)
    w_in_own = np.concatenate([
        w_in[:, :, o_q:o_q + 512][:, :, hs], w_in[:, :, o_k:o_k + 512][:, :, hs], w_in[:, :, o_v:o_v + 512][:, :, hs],
        w_in[:, :, o_g:o_g + 512][:, :, hs], w_in[:, :, o_cq:o_cq + 256], w_in[:, :, o_ckv:o_ckv + 128], kr, krot,
        w_in[:, :, o_a + r:o_a + r + 1], w_in[:, :, o_b + r:o_b + r + 1]], axis=-1)
    cw = inp["conv_w"]
    conv_own = np.stack([cw[:, :, s * 512 + r * 128: s * 512 + (r + 1) * 128] for s in range(3)], axis=1)
    conv_own = np.ascontiguousarray(np.transpose(conv_own, (0, 3, 1, 2)))
    wq = inp["w_q_up"]
    parts = []
    for hh in range(2):
        base = (2 * r + hh) * 96
        nope = wq[:, :, base:base + 64]
        rope = wq[:, :, base + 64:base + 96]
        rot = np.concatenate([rope[:, :, 16:32], rope[:, :, 0:16]], axis=-1)
        parts += [nope, rope, rot]
    wq_own = np.concatenate(parts, axis=-1)
    wkv = inp["w_kv_up"]
    b0, b1 = (2 * r) * 128, (2 * r + 1) * 128
    wkv_own = np.concatenate([wkv[:, :, b0:b0 + 64], wkv[:, :, b1:b1 + 64], wkv[:, :, b0 + 64:b0 + 128], wkv[:, :, b1 + 64:b1 + 128]], axis=-1)
    return {
        "w_in_own": np.ascontiguousarray(w_in_own, dtype=f32),
        "conv_own": np.ascontiguousarray(conv_own, dtype=f32),
        "alog_own": np.ascontiguousarray(inp["a_log"][:, r:r + 1], dtype=f32),
        "dtb_own": np.ascontiguousarray(inp["dt_bias"][:, r:r + 1], dtype=f32),
        "wq_own": np.ascontiguousarray(wq_own, dtype=f32),
        "wkv_own": np.ascontiguousarray(wkv_own, dtype=f32),
    }


def rope_freq_const():
    inv = (np.float32(10000.0) ** (-np.arange(0, 32, 2, dtype=np.float32) / np.float32(32))).astype(np.float32)
    t = np.zeros((96, 1), np.float32)
    t[64:80, 0] = -inv / np.float32(2 * np.pi)
    t[80:96, 0] = inv / np.float32(2 * np.pi)
    return t


MIX_W_SHAPES = {"w_in_own": [2, 1024, NCOL], "conv_own": [2, 128, 3, 4], "alog_own": [2, 1], "dtb_own": [2, 1],
                "wq_own": [2, 256, 256], "wkv_own": [2, 128, 256], "invf": [96, 1],
                "gdn_out_norm": [2, 128], "q_norm": [2, 256], "kv_norm": [2, 128]}


TOK_W = {"mix_post_norm": [2, D], "ffn_pre_norm": [2, D], "ffn_post_norm": [2, D], "mix_pre_norm": [2, D],
         "w_out": [2, D, D], "w_up": [2, D, DFF], "w_down": [2, DFF, D]}


def wout_rowmap(kk):
    return (kk // 2) * 128 if kk % 2 == 0 else 512 + (kk // 2) * 128


def build_norm_prog(NT):
    nc = bass.Bass("TRN2", target_bir_lowering=False)
    p = Prog(nc)
    W = {k: p.dram(k, TOK_W[k], F32, kind="ExternalInput") for k in ("mix_pre_norm",)}
    x_in = p.dram("x_in", [NT, D], F32, kind="ExternalInput")
    x_dummy = p.dram("x_dummy", [NT, D], F32)
    hT_out = p.dram("hT_out", [D, NT], BF16, kind="ExternalOutput")
    cm = setup_common(p)
    phase_tok(p, cm, 0, NT, W, x_in, None, x_dummy, hT_out, do_mix=False, do_ffn=False, gain_next=W["mix_pre_norm"][0, :])
    p.op("sp", lambda e: e.nop(), r=[hT_out])
    p.emit()
    p.close()
    return nc


def build_mix_prog(l, S):
    nc = bass.Bass("TRN2", target_bir_lowering=False)
    p = Prog(nc)
    Wc = {k: p.dram(k, shp, F32, kind="ExternalInput") for k, shp in MIX_W_SHAPES.items()}
    hT = p.dram("hT", [D, S], BF16, kind="ExternalInput")
    pos = p.dram("pos", [S], I32, kind="ExternalInput")
    oT = p.dram("oT_own", [256, S], BF16, kind="ExternalOutput")
    cm = setup_common(p)
    hv = hT.ap.ap().rearrange("(k q) t -> q k t", q=128)
    phase_mix(p, cm, l, S, Wc, lambda tok0: hv[:, :, tok0:tok0 + 512], pos, oT)
    p.op("sp", lambda e: e.nop(), r=[oT])
    p.emit()
    p.close()
    return nc


def build_tok_prog(l, NT, last):
    nc = bass.Bass("TRN2", target_bir_lowering=False)
    p = Prog(nc)
    W = {k: p.dram(k, shp, F32, kind="ExternalInput") for k, shp in TOK_W.items()}
    x_in = p.dram("x_in", [NT, D], F32, kind="ExternalInput")
    oT = p.dram("oT", [D, NT], BF16, kind="ExternalInput")
    x_out = p.dram("x_out", [NT, D], F32, kind="ExternalOutput")
    outs = [x_out]
    hT_out = None
    if not last:
        hT_out = p.dram("hT_out", [D, NT], BF16, kind="ExternalOutput")
        outs.append(hT_out)
    cm = setup_common(p)
    phase_tok(p, cm, l, NT, W, x_in, oT, x_out, hT_out, gain_next=(None if last else W["mix_pre_norm"][l + 1, :]),
              wout_rowmap=wout_rowmap)
    p.op("sp", lambda e: e.nop(), r=outs)
    p.emit()
    p.close()
    return nc


def kernel(**inp):
    inp = {k: np.asarray(v) for k, v in inp.items()}
    B, S = inp["x"].shape[0], inp["x"].shape[1]
    NR = 4
    NT = S // NR
    ncore = B * NR
    cores = list(range(ncore))
    x = np.ascontiguousarray(inp["x"], dtype=np.float32)
    tokw = {k: np.ascontiguousarray(inp[k], dtype=np.float32) for k in TOK_W}
    mixw = []
    for c in cores:
        r = c % NR
        m = core_weight_slices(inp, r)
        m["invf"] = rope_freq_const()
        for k in ("gdn_out_norm", "q_norm", "kv_norm"):
            m[k] = np.ascontiguousarray(inp[k], dtype=np.float32)
        m["pos"] = np.ascontiguousarray(inp["positions"][c // NR], dtype=np.int32)
        mixw.append(m)

    def own(c):
        b, r = c // NR, c % NR
        return b, slice(r * NT, (r + 1) * NT)

    x_cur = [np.ascontiguousarray(x[own(c)[0], own(c)[1]]) for c in cores]
    res = run_bass_kernel_spmd(build_norm_prog(NT), [{"mix_pre_norm": tokw["mix_pre_norm"], "x_in": x_cur[c]} for c in cores],
                               core_ids=cores)
    hT_own = [res.results[c]["hT_out"] for c in cores]
    for l in range(2):
        hT_full = [np.ascontiguousarray(np.concatenate([hT_own[b * NR + r] for r in range(NR)], axis=1)) for b in range(B)]
        ims = []
        for c in cores:
            m = dict(mixw[c])
            m["hT"] = hT_full[c // NR]
            ims.append(m)
        res = run_bass_kernel_spmd(build_mix_prog(l, S), ims, core_ids=cores)
        oT_own = [res.results[c]["oT_own"] for c in cores]
        ims = []
        for c in cores:
            b, r = c // NR, c % NR
            oT_tok = np.ascontiguousarray(np.concatenate([oT_own[b * NR + rr][:, r * NT:(r + 1) * NT] for rr in range(NR)], axis=0))
            m = dict(tokw)
            m["x_in"] = x_cur[c]
            m["oT"] = oT_tok
            ims.append(m)
        res = run_bass_kernel_spmd(build_tok_prog(l, NT, last=(l == 1)), ims, core_ids=cores)
        x_cur = [res.results[c]["x_out"] for c in cores]
        if l == 0:
            hT_own = [res.results[c]["hT_out"] for c in cores]
    out = np.empty_like(x)
    for c in cores:
        b, sl = own(c)
        out[b, sl] = x_cur[c]
    return out
```

```python
import types
import numpy as np
import concourse.bass as bass
import concourse.mybir as mybir
from concourse.bass_utils import run_bass_kernel_spmd

F32 = mybir.dt.float32
BF16 = mybir.dt.bfloat16
I32 = mybir.dt.int32
AF = mybir.ActivationFunctionType
ALU = mybir.AluOpType
AX = mybir.AxisListType

D = 1024
SEQ = 8192
DFF = 4096
EPS = 1e-6
INW = 2472
ENGS = ("pe", "act", "dve", "pool", "sp")


class T:
    __slots__ = ("ap", "name", "w", "r")

    def __init__(self, ap, name=""):
        self.ap = ap
        self.name = name
        self.w = None
        self.r = []

    def __getitem__(self, idx):
        return self.ap[idx]


class Op:
    __slots__ = ("eng", "fn", "deps", "inc", "sem", "val", "dma", "idx", "waits", "name", "incv")

    def __init__(self, eng, fn, dma, name):
        self.incv = 16
        self.eng = eng
        self.fn = fn
        self.deps = set()
        self.inc = False
        self.sem = None
        self.val = 0
        self.dma = dma
        self.waits = []
        self.name = name


def _freeze(fn):
    if fn.__closure__ is None:
        return fn
    cells = []
    for c in fn.__closure__:
        try:
            cells.append(types.CellType(c.cell_contents))
        except ValueError:
            cells.append(c)
    return types.FunctionType(fn.__code__, fn.__globals__, fn.__name__, fn.__defaults__, tuple(cells))


class RR:
    def __init__(self, items):
        self.items = list(items)
        self.i = 0

    def next(self):
        t = self.items[self.i % len(self.items)]
        self.i += 1
        return t


class Prog:
    def __init__(self, nc, n_dma_sems=24):
        self.nc = nc
        self.ops = {e: [] for e in ENGS}
        self.n_dma_sems = n_dma_sems
        self._stack = []
        self.nops = 0
        self.last = {e: None for e in ENGS}
        self.dmas_since_bar = []

    def mark(self):
        return len(self._stack)

    def release(self, mark):
        while len(self._stack) > mark:
            g = self._stack.pop()
            g.__exit__(None, None, None)

    def tile(self, name, shape, dtype, space="sbuf"):
        self.uid = getattr(self, "uid", 0) + 1
        name = f"t{self.uid}_{name}"
        if space == "sbuf":
            g = self.nc.sbuf_tensor(name, list(shape), dtype)
        else:
            g = self.nc.psum_tensor(name, list(shape), dtype)
        h = g.__enter__()
        self._stack.append(g)
        return T(h, name)

    def dram(self, name, shape, dtype, kind="Internal"):
        h = self.nc.dram_tensor(name, list(shape), dtype, kind=kind)
        return T(h, name)

    def op(self, eng, fn, r=(), w=(), dma=False, name="", extra=(), incv=16):
        o = Op(eng, _freeze(fn), dma, name)
        o.incv = incv
        for t in r:
            if t.w is not None:
                o.deps.add(t.w)
        for t in w:
            if t.w is not None:
                o.deps.add(t.w)
            for rd in t.r:
                o.deps.add(rd)
        for d in extra:
            if d is not None:
                o.deps.add(d)
        for t in w:
            t.w = o
            t.r = []
        for t in r:
            if t.w is not o:
                t.r.append(o)
        o.deps.discard(o)
        o.idx = len(self.ops[eng])
        self.ops[eng].append(o)
        self.last[eng] = o
        if dma:
            self.dmas_since_bar.append(o)
        self.nops += 1
        return o

    def dma(self, out_ap, in_ap, r=(), w=(), eng="sp", name="", **kw):
        return self.op(eng, lambda e: e.dma_start(out=out_ap, in_=in_ap, **kw), r=r, w=w, dma=True, name=name)

    def barrier(self):
        pend = list(self.dmas_since_bar)
        self.dmas_since_bar = []
        for e in ENGS:
            mine = [o for o in pend if o.eng == e]
            if mine:
                self.op(e, lambda en: en.nop(), extra=mine, name="bar_dma")
        marks = [self.last[e] for e in ENGS]
        for e in ENGS:
            self.op(e, lambda en: en.nop(), extra=marks, name="bar")

    def load_meta_regs(self, meta_sb, n):
        self.regs = {}

        def fn(e):
            last = None
            for i in range(n):
                g = e.alloc_register(f"meta{i}")
                last = e.reg_load(g, meta_sb[0:1, i:i + 1])
                self.regs[i] = g
            self.regs["scratch"] = e.alloc_register("metas")
            return last
        self.op("pool", fn, r=[meta_sb], name="meta_regs")

    def dma_dyn(self, out_ap, src_t, reg_i, static_off, ap_list, r=(), w=(), name=""):
        def fn(e):
            gs = self.regs["scratch"]
            e.reg_add(gs, self.regs[reg_i], static_off)
            return e.dma_start(out=out_ap, in_=bass.AP(src_t.ap, gs, ap_list))
        return self.op("pool", fn, r=list(r) + [src_t], w=w, dma=True, name=name)

    def all_gather(self, src_t, dst_t, groups):
        return self.op("pool", lambda e: e.collective_compute("AllGather", ALU.bypass, replica_groups=groups,
                                                              ins=[src_t.ap.ap()], outs=[dst_t.ap.ap()]),
                       r=[src_t], w=[dst_t], dma=True, incv=1, name="allgather")

    def emit(self):
        nc = self.nc

        def skip(d, o):
            return d.eng == "pe" and o.eng == "pe" and not d.dma and not o.dma

        for e in ENGS:
            for o in self.ops[e]:
                for d in o.deps:
                    if not skip(d, o):
                        d.inc = True
        eng_sem = {e: nc.alloc_semaphore(name=f"s_{e}") for e in ENGS}
        dma_pool = {}
        dma_state = {}
        for e in ENGS:
            if any(o.dma for o in self.ops[e]):
                dma_pool[e] = [nc.alloc_semaphore(name=f"d_{e}{i}") for i in range(self.n_dma_sems)]
                dma_state[e] = [0, [0] * self.n_dma_sems]
        for e in ENGS:
            cnt = 0
            for o in self.ops[e]:
                if o.dma:
                    st = dma_state[e]
                    i = st[0] % self.n_dma_sems
                    st[0] += 1
                    prev = st[1][i]
                    o.sem = dma_pool[e][i]
                    o.val = prev + o.incv
                    st[1][i] = o.val
                    o.inc = True
                    o.waits.append((o.sem, prev))
                elif o.inc:
                    cnt += 1
                    o.sem = eng_sem[e]
                    o.val = cnt
        for e in ENGS:
            seen = {}
            for o in self.ops[e]:
                need = {}
                for (s, v) in o.waits:
                    if v > 0:
                        need[id(s)] = (s, v)
                if e == "pool" and o.inc and not o.dma and o.val > 1:
                    need[id(eng_sem[e])] = (eng_sem[e], o.val - 1)
                for d in o.deps:
                    if skip(d, o):
                        continue
                    k = id(d.sem)
                    if k not in need or need[k][1] < d.val:
                        need[k] = (d.sem, d.val)
                waits = []
                for k, (s, v) in need.items():
                    if seen.get(k, 0) >= v:
                        continue
                    seen[k] = v
                    waits.append((s, v))
                o.waits = waits

        with nc.Block() as block:
            def mk(e):
                def body(eng):
                    for o in self.ops[e]:
                        for (s, v) in o.waits:
                            eng.wait_ge(s, v)
                        ins = o.fn(eng)
                        if o.inc:
                            ins.then_inc(o.sem, o.incv if o.dma else 1)
                return body
            if self.ops["sp"]:
                block.sync(mk("sp"))
            if self.ops["pe"]:
                block.tensor(mk("pe"))
            if self.ops["act"]:
                block.scalar(mk("act"))
            if self.ops["dve"]:
                block.vector(mk("dve"))
            if self.ops["pool"]:
                block.gpsimd(mk("pool"))

    def close(self):
        self.release(0)


class Common:
    pass


def setup_common(p):
    c = Common()
    idf = p.tile("c_idf", [128, 128], F32)
    p.op("pool", lambda e: e.memset(idf[:, :], 0.0), w=[idf])
    p.op("pool", lambda e: e.affine_select(out=idf[:, :], in_=idf[:, :], pattern=[[-1, 128]],
                                           compare_op=ALU.not_equal, fill=1.0, base=0, channel_multiplier=1),
         r=[idf], w=[idf])
    c.identf = idf
    c.ident = p.tile("c_ident", [128, 128], BF16)
    p.op("dve", lambda e: e.tensor_copy(out=c.ident[:, :], in_=idf[:, :]), r=[idf], w=[c.ident])
    p.eps_tile = p.tile("c_eps", [128, 1], F32)
    p.op("pool", lambda e: e.memset(p.eps_tile[:, :], EPS), w=[p.eps_tile])
    return c


def bcast_load(p, dst, src_ap_1d, n, eng="sp"):
    p.dma(dst[:, :], src_ap_1d.partition_broadcast(128), w=[dst], eng=eng)


def rsqrt_act(p, out_ap, in_ap, scale, bias_ap, tmp_ap, r, w, wtmp):
    p.op("act", lambda e: e.activation(out=tmp_ap, in_=in_ap, func=AF.Ln, bias=bias_ap, scale=scale), r=list(r) + [p.eps_tile], w=[wtmp])
    p.op("act", lambda e: e.activation(out=out_ap, in_=tmp_ap, func=AF.Exp, scale=-0.5), r=[wtmp], w=list(w))


def rstd_from_ss(p, ss, rstd, n, tmp):
    rsqrt_act(p, rstd[:, :], ss[:, :], 1.0 / n, p.eps_tile[:, 0:1], tmp[:, :], [ss], [rstd], tmp)


def norm_to_hT(p, cm, x_ap, gain_t, hT_blk, j, sc, psT):
    xt = sc["xt"]
    p.op("act", lambda e: e.activation(out=sc["junk"][:, :], in_=x_ap, func=AF.Square, accum_out=sc["ss1"][:, 0:1]),
         r=[xt], w=[sc["ss1"]])
    rstd_from_ss(p, sc["ss1"], sc["rs1"], D, sc["tmp1"])
    hb = sc["hb"]
    p.op("dve", lambda e: e.scalar_tensor_tensor(out=hb[:, :], in0=x_ap, scalar=sc["rs1"][:, 0:1], in1=gain_t[:, :],
                                                 op0=ALU.mult, op1=ALU.mult), r=[xt, sc["rs1"], gain_t], w=[hb])
    pt = psT.next()
    for k in range(8):
        p.op("pe", lambda e, k=k: e.transpose(out=pt[:, k * 128:(k + 1) * 128], in_=hb[:, k * 128:(k + 1) * 128],
                                              identity=cm.ident[:, :]), r=[hb, cm.ident], w=[pt])
    p.op("act", lambda e: e.copy(out=hT_blk[:, :, j * 128:(j + 1) * 128],
                                 in_=pt[:, :].rearrange("p (k t) -> p k t", k=8)), r=[pt], w=[hT_blk])


def phase_tok(p, cm, l, NT, W, x_in, oT, x_out, hT_out, do_mix=True, do_ffn=True, gain_next=None, wout_rowmap=None,
              load_oT=None, wsrc=None):
    mk = p.mark()
    TB = 512
    nblk = NT // TB
    psA = RR([p.tile(f"psA{i}", [128, 512], F32, space="psum") for i in range(5)])
    psT = RR([p.tile(f"psT{i}", [128, 1024], BF16, space="psum") for i in range(2)])

    def gain_tile(name, ap1d):
        t = p.tile("g_" + name, [128, D], F32)
        bcast_load(p, t, ap1d, D)
        return t

    if do_mix:
        g_post = gain_tile("post", W["mix_post_norm"][l, :])
        wout = p.tile("wout", [128, 8, D], BF16)
        wo = W["w_out"].ap
        for k in range(8):
            r0 = wout_rowmap(k) if wout_rowmap else k * 128
            if wsrc is not None:
                p.dma(wout[:, k, :], wsrc["w_out"].ap.ap()[r0:r0 + 128, :], r=[wsrc["w_out"]], w=[wout])
            else:
                p.dma(wout[:, k, :], wo[l, r0:r0 + 128, :], w=[wout], eng="pool")
        oTb = p.tile("oTb", [128, 8, TB], BF16)
    if do_ffn:
        g_pre = gain_tile("pre", W["ffn_pre_norm"][l, :])
        g_fpost = gain_tile("fpost", W["ffn_post_norm"][l, :])
        wup_bufs = RR([p.tile(f"wup{i}", [128, 8, 512], BF16) for i in range(3)])
        wdn_bufs = RR([p.tile(f"wdn{i}", [128, 32, 256], BF16) for i in range(2)])
        fT = p.tile("fT", [128, 32, TB], BF16)
        h2T = p.tile("h2T", [128, 8, TB], BF16)
    if gain_next is not None:
        g_next = gain_tile("next", gain_next)
        hTn = RR([p.tile(f"hTn{i}", [128, 8, TB], BF16) for i in range(2)])
    xb = p.tile("xb", [128, 4, D], F32)
    fsb = p.tile("fsb", [128, 4, D], F32)
    ss4 = p.tile("ss4", [128, 4, 4], F32)
    sc = {
        "xt": xb,
        "junk": p.tile("junk", [128, D], BF16),
        "ss1": p.tile("ss1", [128, 1], F32), "rs1": p.tile("rs1", [128, 1], F32), "tmp1": p.tile("tmp1", [128, 1], F32),
        "sst": p.tile("sst", [128, 1], F32),
        "hb": p.tile("hb", [128, D], BF16),
    }
    tt = p.tile("tt", [128, D], F32)
    relu_t = RR([p.tile(f"relu{i}", [128, 512], BF16) for i in range(2)])

    x_in_v = x_in.ap.ap().rearrange("(n j q) d -> n q j d", j=4, q=128)
    x_out_v = x_out.ap.ap().rearrange("(n j q) d -> n q j d", j=4, q=128)

    def evac(ps, j, col0, ncol, slot):
        p.op("act", lambda e: e.activation(out=sc["junk"][:, 0:ncol], in_=ps[:, 0:ncol], func=AF.Square,
                                           accum_out=ss4[:, j, slot:slot + 1]), r=[ps], w=[ss4])
        p.op("act", lambda e: e.copy(out=fsb[:, j, col0:col0 + ncol], in_=ps[:, 0:ncol]), r=[ps], w=[fsb])

    def finish_residual(j, nslot, gain_t):
        p.op("dve", lambda e: e.tensor_reduce(out=sc["sst"][:, :], in_=ss4[:, j, 0:nslot], axis=AX.X, op=ALU.add),
             r=[ss4], w=[sc["sst"]])
        rstd_from_ss(p, sc["sst"], sc["rs1"], D, sc["tmp1"])
        p.op("dve", lambda e: e.scalar_tensor_tensor(out=tt[:, :], in0=fsb[:, j, :], scalar=sc["rs1"][:, 0:1], in1=gain_t[:, :],
                                                     op0=ALU.mult, op1=ALU.mult), r=[fsb, sc["rs1"], gain_t], w=[tt])
        p.op("dve", lambda e: e.tensor_tensor(out=xb[:, j, :], in0=xb[:, j, :], in1=tt[:, :], op=ALU.add),
             r=[tt, xb], w=[xb])

    for n in range(nblk):
        tok0 = n * TB
        p.dma(xb[:, :, :], x_in_v[n], r=[x_in], w=[xb])
        if do_mix:
            if load_oT is not None:
                load_oT(p, oTb, tok0)
            else:
                p.dma(oTb[:, :, :], oT.ap.ap().rearrange("(k q) t -> q k t", q=128)[:, :, tok0:tok0 + TB], r=[oT], w=[oTb])
            for j in range(4):
                for h in range(2):
                    ps = psA.next()
                    for k in range(8):
                        p.op("pe", lambda e, k=k, h=h, ps=ps, j=j: e.matmul(
                            ps[:, :], lhsT=oTb[:, k, j * 128:(j + 1) * 128], rhs=wout[:, k, h * 512:(h + 1) * 512],
                            start=(k == 0), stop=(k == 7)), r=[oTb, wout], w=[ps])
                    evac(ps, j, h * 512, 512, h)
                finish_residual(j, 2, g_post)
        if do_ffn:
            for j in range(4):
                norm_to_hT(p, cm, xb[:, j, :], g_pre, h2T, j, sc, psT)
            if wsrc is not None:
                wu_l, wd_l = wsrc["w_up"].ap.ap(), wsrc["w_down"].ap.ap()
                wq_eng, wr_u, wr_d = "sp", [wsrc["w_up"]], [wsrc["w_down"]]
            else:
                wu_l, wd_l = W["w_up"].ap.ap()[l], W["w_down"].ap.ap()[l]
                wq_eng, wr_u, wr_d = "pool", [], []
            for g in range(8):
                wb = wup_bufs.next()
                p.dma(wb[:, :, :], wu_l.rearrange("(k q) f -> q k f", q=128)[:, :, g * 512:(g + 1) * 512], r=wr_u, w=[wb], eng=wq_eng)
                for cc in range(4):
                    c = g * 4 + cc
                    ps = psA.next()
                    for k in range(8):
                        p.op("pe", lambda e, k=k, cc=cc, ps=ps, wb=wb: e.matmul(
                            ps[:, :], lhsT=wb[:, k, cc * 128:(cc + 1) * 128], rhs=h2T[:, k, :],
                            start=(k == 0), stop=(k == 7)), r=[wb, h2T], w=[ps])
                    rt = relu_t.next()
                    p.op("act", lambda e, ps=ps, rt=rt: e.activation(out=rt[:, :], in_=ps[:, :], func=AF.Relu), r=[ps], w=[rt])
                    p.op("pool", lambda e, c=c, rt=rt: e.tensor_tensor(out=fT[:, c, :], in0=rt[:, :], in1=rt[:, :], op=ALU.mult),
                         r=[rt], w=[fT])
            for qd in range(4):
                wq = wdn_bufs.next()
                p.dma(wq[:, :, :], wd_l.rearrange("(c q) f -> q c f", q=128)[:, :, qd * 256:(qd + 1) * 256], r=wr_d, w=[wq], eng=wq_eng)
                for j in range(4):
                    ps = psA.next()
                    for c in range(32):
                        p.op("pe", lambda e, c=c, j=j, ps=ps, wq=wq: e.matmul(
                            ps[:, 0:256], lhsT=fT[:, c, j * 128:(j + 1) * 128], rhs=wq[:, c, :],
                            start=(c == 0), stop=(c == 31)), r=[fT, wq], w=[ps])
                    evac(ps, j, qd * 256, 256, qd)
            for j in range(4):
                finish_residual(j, 4, g_fpost)
        p.dma(x_out_v[n], xb[:, :, :], r=[xb], w=[x_out])
        if gain_next is not None:
            hb_ = hTn.next()
            for j in range(4):
                norm_to_hT(p, cm, xb[:, j, :], g_next, hb_, j, sc, psT)
            p.dma(hT_out.ap.ap().rearrange("(k q) t -> q k t", q=128)[:, :, tok0:tok0 + TB], hb_[:, :, :], r=[hb_], w=[hT_out])
    p.barrier()
    p.release(mk)


GQ, GK, GV, GG, CQ, CKV, KR, KROT, AB, NCOL = 0, 128, 256, 384, 512, 768, 896, 928, 960, 962
NEG = -1.0e30


def phase_mix(p, cm, l, S, Wc, hT_src, pos, oT_own, dbg=None):
    mk = p.mark()
    TB = 512
    nblk = S // TB
    NKB = S // 128
    banks = [p.tile(f"bank{i}", [128, 512], F32, space="psum") for i in range(8)]
    misc = RR(banks[0:2])
    psS = RR(banks[2:4])
    psO = banks[4:6]
    bR = banks[6]
    bO = banks[7]

    def bfv(bank):
        return bank.ap.bitcast(BF16)

    def t_(name, shape, dt=F32):
        return p.tile(name, shape, dt)

    ones_bf = t_("ones_bf", [128, 128], BF16)
    p.op("pool", lambda e: e.memset(ones_bf[:, :], 1.0), w=[ones_bf])
    ones64 = t_("ones64", [64, 64])
    p.op("pool", lambda e: e.memset(ones64[:, :], 1.0), w=[ones64])

    def mask_tile(name, shape, init, pattern, cmul, cmp_op, fill):
        t = t_(name, shape)
        p.op("pool", lambda e: e.memset(t.ap[:], init), w=[t])
        p.op("pool", lambda e: e.affine_select(out=t.ap[:], in_=t.ap[:], pattern=pattern, compare_op=cmp_op, fill=fill,
                                               base=0, channel_multiplier=cmul), r=[t], w=[t])
        return t

    U = mask_tile("mU", [64, 64], 1.0, [[1, 64]], -1, ALU.is_ge, 0.0)
    Ust = mask_tile("mUst", [64, 64], 1.0, [[-1, 64]], 1, ALU.is_gt, 0.0)
    Mneg = mask_tile("mMneg", [64, 64], 0.0, [[1, 64]], -1, ALU.is_ge, NEG)
    S01 = mask_tile("mS01", [64, 64], 1.0, [[1, 64]], -1, ALU.is_gt, 0.0)

    def b8(m):
        return m.ap[0:64, 0:64].unsqueeze(1).to_broadcast([64, 8, 64])
    c_eps128 = t_("c_eps128", [128, 1])
    p.op("pool", lambda e: e.memset(c_eps128[:, :], 128.0 * EPS), w=[c_eps128])
    c_one = t_("c_one", [128, 1])
    p.op("pool", lambda e: e.memset(c_one[:, :], 1.0), w=[c_one])

    Wown = t_("Wown", [128, 8, NCOL], BF16)
    wsrc = Wc["w_in_own"].ap.ap()[l].rearrange("(k q) c -> q k c", q=128)
    for k0 in range(0, 8, 2):
        p.dma(Wown[:, k0:k0 + 2, :], wsrc[:, k0:k0 + 2, :], w=[Wown], eng="pool")
    convw = t_("convw", [128, 3, 4])
    p.dma(convw[:, :, :], Wc["conv_own"].ap.ap()[l], w=[convw])
    gnorm = t_("gnorm", [64, 128])
    p.dma(gnorm[:, :], Wc["gdn_out_norm"].ap.ap()[l].partition_broadcast(64), w=[gnorm])
    nA = t_("nA", [64, 1])
    p.dma(nA[:, :], Wc["alog_own"].ap.ap()[l].partition_broadcast(64), w=[nA])
    p.op("act", lambda e: e.activation(out=nA[:, :], in_=nA[:, :], func=AF.Exp), r=[nA], w=[nA])
    p.op("dve", lambda e: e.tensor_scalar(out=nA[:, :], in0=nA[:, :], scalar1=-1.0, scalar2=None, op0=ALU.mult), r=[nA], w=[nA])
    dtb = t_("dtb", [64, 1])
    p.dma(dtb[:, :], Wc["dtb_own"].ap.ap()[l].partition_broadcast(64), w=[dtb])
    invf = t_("invf", [96, 1])
    p.dma(invf[:, :], Wc["invf"].ap.ap(), w=[invf])
    wq_f = t_("wq_f", [128, 2, 256])
    p.dma(wq_f[:, :, :], Wc["wq_own"].ap.ap()[l].rearrange("(k q) c -> q k c", q=128), w=[wq_f])
    qg = t_("qg", [128, 2])
    p.dma(qg[:, :], Wc["q_norm"].ap.ap()[l].rearrange("(k q) -> q k", q=128), w=[qg], allow_slow_non_contiguous=True)
    wq = t_("wq", [128, 2, 256], BF16)
    p.op("dve", lambda e: e.tensor_tensor(out=wq[:, :, :], in0=wq_f[:, :, :], in1=qg[:, :].unsqueeze(2).to_broadcast([128, 2, 256]),
                                          op=ALU.mult), r=[wq_f, qg], w=[wq])
    wkv_f = t_("wkv_f", [128, 256])
    p.dma(wkv_f[:, :], Wc["wkv_own"].ap.ap()[l], w=[wkv_f])
    kvg = t_("kvg", [128, 1])
    p.dma(kvg[:, :], Wc["kv_norm"].ap.ap()[l].rearrange("(q o) -> q o", o=1), w=[kvg])
    wkv = t_("wkv", [128, 256], BF16)
    p.op("dve", lambda e: e.tensor_scalar(out=wkv[:, :], in0=wkv_f[:, :], scalar1=kvg[:, 0:1], scalar2=None, op0=ALU.mult),
         r=[wkv_f, kvg], w=[wkv])

    kT = [t_(f"kT{h}", [96, S], BF16) for h in range(2)]
    vaug = [t_(f"vaug{h}", [128, NKB, 65], BF16) for h in range(2)]
    kTb = [[T(kT[h].ap, f"kT{h}_{i}") for i in range(nblk)] for h in range(2)]
    vab = [[T(vaug[h].ap, f"va{h}_{i}") for i in range(nblk)] for h in range(2)]
    for h in range(2):
        p.op("pool", lambda e, h=h: e.memset(vaug[h][:, :, 64:65], 1.0), w=vab[h])

    hTb = RR([t_(f"hTb{i}", [128, 8, TB], BF16) for i in range(2)])
    raw = [t_(f"raw{s}", [128, 3 + TB]) for s in range(3)]
    for s in range(3):
        p.op("pool", lambda e, s=s: e.memset(raw[s][:, 0:3], 0.0), w=[raw[s]])
    caccs = [t_(f"cacc{s}", [128, TB]) for s in range(3)]
    sqb = t_("sqb", [128, 2, TB], BF16)
    rn = t_("rn", [128, TB])
    rtmp = t_("rtmp", [128, TB])
    qTg = t_("qTg", [128, TB], BF16)
    kTg = t_("kTg", [128, TB], BF16)
    vTg = t_("vTg", [128, TB], BF16)
    gate = t_("gate", [128, TB])
    cqT = t_("cqT", [128, 2, TB], BF16)
    ckvT = t_("ckvT", [128, TB], BF16)
    rstd_q = t_("rstd_q", [128, TB])
    rstd_kv = t_("rstd_kv", [128, TB])
    rkv_col = t_("rkv_col", [128, 4])
    rkv_tmp = t_("rkv_tmp", [128, 4])
    ab_row = t_("ab_row", [2, TB])
    ab_col = t_("ab_col", [64, 8, 2])
    la_col = t_("la_col", [64, 8])
    sp_t = t_("sp_t", [64, 8])
    beta_col = t_("beta_col", [64, 8])
    nbeta_col = t_("nbeta_col", [64, 8])
    eg_col = t_("eg_col", [64, 16])
    X1 = t_("X1", [64, 8, 128])
    X2 = t_("X2", [64, 8, 64])
    DT = t_("DT", [64, 8, 64])
    DST = t_("DST", [64, 8, 64])
    tmpA = t_("tmpA", [64, 8, 64])
    chX = [t_(f"chX{i}", [64, 8, 64]) for i in range(2)]
    chY = [t_(f"chY{i}", [64, 8, 64]) for i in range(2)]
    chT = [t_(f"chT{i}", [64, 8, 64]) for i in range(2)]
    TTb = t_("TTb", [64, 8, 64], BF16)
    PT = t_("PTg", [64, 8, 64], BF16)
    EGB = t_("EGB", [128, 8, 64])
    qdecT = t_("qdecT", [128, TB], BF16)
    kTok = t_("kTok", [64, 8, 128], BF16)
    vTok = t_("vTok", [64, 8, 128], BF16)
    keg = t_("keg", [64, 8, 128], BF16)
    kdec = t_("kdec", [64, 8, 128], BF16)
    bu = t_("bu", [64, 8, 128])
    wT = t_("wT", [128, 8, 64], BF16)
    Sst = t_("Sst", [128, 128])
    Sb = t_("Sb", [128, 128], BF16)
    p.op("pool", lambda e: e.memset(Sst[:, :], 0.0), w=[Sst])
    p.op("pool", lambda e: e.memset(Sb[:, :], 0.0), w=[Sb])
    vnew = RR([t_(f"vnew{i}", [64, 128], BF16) for i in range(2)])
    osq = t_("osq", [64, 4, 128])
    oss = t_("oss", [64, 4])
    ors = t_("ors", [64, 4])
    otmp = t_("otmp", [64, 4])
    on1 = t_("on1", [64, 4, 128])
    on2 = t_("on2", [64, 4, 128], BF16)
    oTg = t_("oTg", [128, TB], BF16)
    posi = t_("posi", [96, TB], I32)
    ph = t_("ph", [96, 2, TB])
    phi = t_("phi", [96, 2, TB], I32)
    phm = t_("phm", [96, 2, TB])
    CS = t_("CS", [96, 2, TB])
    u1 = t_("u1", [96, TB])
    u2 = t_("u2", [96, TB])
    qT = [t_(f"qT{h}", [96, TB], BF16) for h in range(2)]
    PTa = RR([t_(f"PTa{i}", [128, TB], BF16) for i in range(3)])
    rden = t_("rden", [128, 4])
    on_tok = t_("on_tok", [128, 4, 128], BF16)
    oTm = t_("oTm", [128, TB], BF16)
    SCALE = float(96 ** -0.5)

    def mm(out_ap, lhsT, rhs, start, stop, r, w):
        p.op("pe", lambda e: e.matmul(out_ap, lhsT=lhsT, rhs=rhs, start=start, stop=stop, skip_group_check=True), r=r, w=w)

    def proj_group(hb, c0, M):
        ps = misc.next()
        for k in range(8):
            mm(ps[0:M, :], Wown[:, k, c0:c0 + M], hb[:, k, :], k == 0, k == 7, [Wown, hb], [ps])
        return ps

    for t in range(nblk):
        tok0 = t * TB
        hb = hTb.next()
        if callable(getattr(hT_src, "load", None)):
            hT_src.load(p, hb, tok0)
        else:
            p.dma(hb[:, :, :], hT_src(tok0), w=[hb])
        p.dma(posi[64:96, :], pos.ap.ap()[tok0:tok0 + TB].partition_broadcast(32), r=[pos], w=[posi])
        R_ = slice(64, 96)
        p.op("dve", lambda e: e.tensor_copy(out=u1[R_, :], in_=posi[R_, :]), r=[posi], w=[u1])
        p.op("dve", lambda e: e.tensor_scalar(out=ph[R_, 0, :], in0=u1[R_, :], scalar1=invf[R_, 0:1], scalar2=None, op0=ALU.mult),
             r=[u1, invf], w=[ph])
        p.op("dve", lambda e: e.tensor_scalar(out=ph[R_, 1, :], in0=ph[R_, 0, :], scalar1=0.25, scalar2=None, op0=ALU.add),
             r=[ph], w=[ph])
        p.op("dve", lambda e: e.tensor_copy(out=phi[R_, :, :], in_=ph[R_, :, :]), r=[ph], w=[phi])
        p.op("dve", lambda e: e.tensor_copy(out=phm[R_, :, :], in_=phi[R_, :, :]), r=[phi], w=[phm])
        p.op("dve", lambda e: e.tensor_tensor(out=ph[R_, :, :], in0=ph[R_, :, :], in1=phm[R_, :, :], op=ALU.subtract), r=[ph, phm], w=[ph])
        p.op("dve", lambda e: e.scalar_tensor_tensor(out=phm[R_, :, :], in0=ph[R_, :, :], scalar=0.5, in1=ph[R_, :, :],
                                                     op0=ALU.is_gt, op1=ALU.subtract), r=[ph], w=[phm])
        p.op("dve", lambda e: e.scalar_tensor_tensor(out=ph[R_, :, :], in0=phm[R_, :, :], scalar=0.5, in1=phm[R_, :, :],
                                                     op0=ALU.is_gt, op1=ALU.subtract), r=[phm], w=[ph])
        p.op("act", lambda e: e.activation(out=CS[R_, :, :], in_=ph[R_, :, :], func=AF.Sin, scale=float(2 * np.pi)), r=[ph], w=[CS])


        for s, c0 in enumerate((GQ, GK, GV)):
            if t > 0:
                p.op("pool", lambda e, s=s: e.tensor_copy(out=raw[s][:, 0:3], in_=raw[s][:, TB:TB + 3]), r=[raw[s]], w=[raw[s]])
            ps = proj_group(hb, c0, 128)
            p.op("act", lambda e, s=s, ps=ps: e.copy(out=raw[s][:, 3:3 + TB], in_=ps[:, :]), r=[ps], w=[raw[s]])
        ps = proj_group(hb, GG, 128)
        p.op("act", lambda e, ps=ps: e.copy(out=gate[:, :], in_=ps[:, :]), r=[ps], w=[gate])
        for kc in range(2):
            ps = proj_group(hb, CQ + kc * 128, 128)
            p.op("act", lambda e, ps=ps, kc=kc: e.copy(out=cqT[:, kc, :], in_=ps[:, :]), r=[ps], w=[cqT])
            p.op("act", lambda e, ps=ps, kc=kc: e.activation(out=sqb[:, kc, :], in_=ps[:, :], func=AF.Square), r=[ps], w=[sqb])
        ps = misc.next()
        for kc in range(2):
            mm(ps[:, :], ones_bf[:, :], sqb[:, kc, :], kc == 0, kc == 1, [ones_bf, sqb], [ps])
        rsqrt_act(p, rstd_q[:, :], ps[:, :], 1.0 / 256, p.eps_tile[:, 0:1], rtmp[:, :], [ps], [rstd_q], rtmp)
        ps = proj_group(hb, CKV, 128)
        p.op("act", lambda e, ps=ps: e.copy(out=ckvT[:, :], in_=ps[:, :]), r=[ps], w=[ckvT])
        p.op("act", lambda e, ps=ps: e.activation(out=sqb[:, 0, :], in_=ps[:, :], func=AF.Square), r=[ps], w=[sqb])
        ps = misc.next()
        mm(ps[:, :], ones_bf[:, :], sqb[:, 0, :], True, True, [ones_bf, sqb], [ps])
        rsqrt_act(p, rstd_kv[:, :], ps[:, :], 1.0 / 128, p.eps_tile[:, 0:1], rtmp[:, :], [ps], [rstd_kv], rtmp)
        ps = misc.next()
        for j in range(4):
            mm(ps[:, j:j + 1], sqb[:, 0, j * 128:(j + 1) * 128], ones_bf[:, 0:1], j == 0, j == 3, [ones_bf, sqb], [ps])
        rsqrt_act(p, rkv_col[:, :], ps[:, 0:4], 1.0 / 128, p.eps_tile[:, 0:1], rkv_tmp[:, :], [ps], [rkv_col], rkv_tmp)

        psKA = proj_group(hb, KR - 64, 96)
        p.op("dve", lambda e, ps=psKA: e.tensor_tensor(out=u1[R_, :], in0=ps[R_, :], in1=CS[R_, 1, :], op=ALU.mult), r=[psKA, CS], w=[u1])
        psKB = proj_group(hb, KROT - 64, 96)
        p.op("dve", lambda e, ps=psKB: e.tensor_tensor(out=u2[R_, :], in0=ps[R_, :], in1=CS[R_, 0, :], op=ALU.mult), r=[psKB, CS], w=[u2])
        for h in range(2):
            p.op("pool", lambda e, h=h: e.tensor_tensor(out=kT[h][R_, tok0:tok0 + TB], in0=u1[R_, :], in1=u2[R_, :], op=ALU.add),
                 r=[u1, u2], w=[kTb[h][t]])
        ps = proj_group(hb, AB, 2)
        p.op("act", lambda e, ps=ps: e.copy(out=ab_row[:, :], in_=ps[0:2, :]), r=[ps], w=[ab_row])

        for h in range(2):
            psA = misc.next()
            for kc in range(2):
                mm(psA[0:96, :], wq[:, kc, h * 128:h * 128 + 96], cqT[:, kc, :], kc == 0, kc == 1, [wq, cqT], [psA])
            p.op("dve", lambda e, h=h, ps=psA: e.tensor_tensor(out=qT[h][0:64, :], in0=ps[0:64, :], in1=rstd_q[0:64, :], op=ALU.mult),
                 r=[psA, rstd_q], w=[qT[h]])
            p.op("dve", lambda e, ps=psA: e.tensor_tensor(out=u1[R_, :], in0=ps[R_, :], in1=CS[R_, 1, :], op=ALU.mult), r=[psA, CS], w=[u1])
            psB = misc.next()
            for kc in range(2):
                mm(psB[0:96, :], wq[:, kc, h * 128 + 32:h * 128 + 128], cqT[:, kc, :], kc == 0, kc == 1, [wq, cqT], [psB])
            p.op("dve", lambda e, ps=psB: e.tensor_tensor(out=u2[R_, :], in0=ps[R_, :], in1=CS[R_, 0, :], op=ALU.mult), r=[psB, CS], w=[u2])
            p.op("pool", lambda e: e.tensor_tensor(out=u1[R_, :], in0=u1[R_, :], in1=u2[R_, :], op=ALU.add), r=[u1, u2], w=[u1])
            p.op("pool", lambda e, h=h: e.tensor_tensor(out=qT[h][R_, :], in0=u1[R_, :], in1=rstd_q[R_, :], op=ALU.mult),
                 r=[u1, rstd_q], w=[qT[h]])
            psK = misc.next()
            mm(psK[0:64, :], wkv[:, h * 64:(h + 1) * 64], ckvT[:, :], True, True, [wkv, ckvT], [psK])
            p.op("dve", lambda e, h=h, ps=psK: e.tensor_tensor(out=kT[h][0:64, tok0:tok0 + TB], in0=ps[0:64, :], in1=rstd_kv[0:64, :],
                                                               op=ALU.mult), r=[psK, rstd_kv], w=[kTb[h][t]])
        psV = misc.next()
        for j in range(4):
            mm(psV[:, j * 128:(j + 1) * 128], ckvT[:, j * 128:(j + 1) * 128], wkv[:, 128:256], j == 0, j == 3, [ckvT, wkv], [psV])
        for h in range(2):
            p.op("dve", lambda e, h=h, ps=psV: e.tensor_tensor(
                out=vaug[h][:, 4 * t:4 * t + 4, 0:64],
                in0=ps[:, :].rearrange("q (j c) -> q j c", j=4)[:, :, h * 64:(h + 1) * 64],
                in1=rkv_col[:, :].unsqueeze(2).to_broadcast([128, 4, 64]), op=ALU.mult), r=[psV, rkv_col], w=[vab[h][t]])

        def gdn_prep_gen():
            for s in range(3):
                cacc = caccs[s]
                p.op("dve", lambda e, s=s, cacc=cacc: e.tensor_scalar(out=cacc[:, :], in0=raw[s][:, 0:TB], scalar1=convw[:, s, 0:1],
                                                                      scalar2=None, op0=ALU.mult), r=[raw[s], convw], w=[cacc])
                for k in range(1, 4):
                    p.op("dve", lambda e, s=s, k=k, cacc=cacc: e.scalar_tensor_tensor(
                        out=cacc[:, :], in0=raw[s][:, k:k + TB], scalar=convw[:, s, k:k + 1], in1=cacc[:, :], op0=ALU.mult, op1=ALU.add),
                        r=[raw[s], convw, cacc], w=[cacc])
            p.op("act", lambda e: e.activation(out=gate[:, :], in_=gate[:, :], func=AF.Silu), r=[gate], w=[gate])
            p.op("act", lambda e: e.activation(out=vTg[:, :], in_=caccs[2][:, :], func=AF.Silu), r=[caccs[2]], w=[vTg])
            for s in range(2):
                p.op("act", lambda e, s=s: e.activation(out=caccs[s][:, :], in_=caccs[s][:, :], func=AF.Silu), r=[caccs[s]], w=[caccs[s]])
            yield
            for s, dst in ((0, qTg), (1, kTg)):
                sil = caccs[s]
                p.op("act", lambda e, sil=sil, s=s: e.activation(out=sqb[:, s, :], in_=sil[:, :], func=AF.Square), r=[sil], w=[sqb])
                yield
                ps = misc.next()
                mm(ps[:, :], ones_bf[:, :], sqb[:, s, :], True, True, [ones_bf, sqb], [ps])
                if s == 0:
                    rsqrt_act(p, rn[:, :], ps[:, :], 128.0, c_eps128[:, 0:1], rtmp[:, :], [ps, c_eps128], [rn], rtmp)
                else:
                    rsqrt_act(p, rn[:, :], ps[:, :], 1.0, p.eps_tile[:, 0:1], rtmp[:, :], [ps], [rn], rtmp)
                p.op("dve", lambda e, dst=dst, sil=sil: e.tensor_tensor(out=dst[:, :], in0=sil[:, :], in1=rn[:, :], op=ALU.mult),
                     r=[sil, rn], w=[dst])
            for src, dst in ((kTg, kTok), (vTg, vTok)):
                yield
                ps = misc.next()
                for c in range(8):
                    p.op("pe", lambda e, c=c, ps=ps, src=src: e.transpose(out=bfv(ps)[0:64, c * 128:(c + 1) * 128],
                                                                          in_=src[:, c * 64:(c + 1) * 64], identity=cm.ident[:, :]),
                         r=[src, cm.ident], w=[ps])
                p.op("act", lambda e, ps=ps, dst=dst: e.copy(out=dst[:, :, :], in_=bfv(ps)[0:64, :].rearrange("q (c d) -> q c d", c=8)),
                     r=[ps], w=[dst])
            yield
            ps = misc.next()
            for c in range(8):
                p.op("pe", lambda e, c=c, ps=ps: e.transpose(out=ps[0:64, 2 * c:2 * c + 2], in_=ab_row[0:2, c * 64:(c + 1) * 64],
                                                             identity=cm.identf[0:2, 0:2]), r=[ab_row, cm.identf], w=[ps])
            p.op("act", lambda e, ps=ps: e.copy(out=ab_col[:, :, :], in_=ps[0:64, 0:16].rearrange("q (c two) -> q c two", two=2)),
                 r=[ps], w=[ab_col])
            p.op("act", lambda e: e.activation(out=nbeta_col[:, :], in_=ab_col[:, :, 1], func=AF.Exp, scale=-1.0), r=[ab_col], w=[nbeta_col])
            p.op("dve", lambda e: e.tensor_scalar(out=nbeta_col[:, :], in0=nbeta_col[:, :], scalar1=1.0, scalar2=None, op0=ALU.add),
                 r=[nbeta_col], w=[nbeta_col])
            p.op("dve", lambda e: e.reciprocal(out=beta_col[:, :], in_=nbeta_col[:, :]), r=[nbeta_col], w=[beta_col])
            p.op("dve", lambda e: e.tensor_scalar(out=nbeta_col[:, :], in0=beta_col[:, :], scalar1=-1.0, scalar2=None, op0=ALU.mult),
                 r=[beta_col], w=[nbeta_col])
            p.op("act", lambda e: e.activation(out=sp_t[:, :], in_=ab_col[:, :, 0], func=AF.Exp, bias=dtb[:, 0:1]), r=[ab_col, dtb], w=[sp_t])
            p.op("act", lambda e: e.activation(out=sp_t[:, :], in_=sp_t[:, :], func=AF.Ln, bias=c_one[0:64, 0:1]), r=[sp_t, c_one], w=[sp_t])
            p.op("dve", lambda e: e.tensor_scalar(out=la_col[:, :], in0=sp_t[:, :], scalar1=nA[:, 0:1], scalar2=None, op0=ALU.mult),
                 r=[sp_t, nA], w=[la_col])
            p.op("dve", lambda e: e.tensor_copy(out=X1[:, :, :], in_=la_col[:, :].unsqueeze(2).to_broadcast([64, 8, 128])), r=[la_col], w=[X1])
            p.op("dve", lambda e: e.tensor_scalar(out=sp_t[:, :], in0=la_col[:, :], scalar1=-1.0, scalar2=None, op0=ALU.mult),
                 r=[la_col], w=[sp_t])
            p.op("pool", lambda e: e.tensor_tensor(out=X2[:, :, :], in0=b8(U), in1=sp_t[:, :].unsqueeze(2).to_broadcast([64, 8, 64]),
                                                   op=ALU.mult), r=[U, sp_t], w=[X2])
            yield
            ps = misc.next()
            mm(ps[0:64, 0:8], U[:, :], la_col[:, :], True, True, [U, la_col], [ps])
            mm(ps[0:64, 8:16], Ust[:, :], la_col[:, :], False, True, [Ust, la_col], [ps])
            p.op("act", lambda e, ps=ps: e.activation(out=eg_col[:, :], in_=ps[0:64, 0:16], func=AF.Exp), r=[ps], w=[eg_col])
            yield
            ps = misc.next()
            for c in range(8):
                mm(ps[0:64, c * 64:(c + 1) * 64], X1[:, c, 0:64], U[:, :], c == 0, False, [X1, U], [ps])
                mm(ps[0:64, c * 64:(c + 1) * 64], X2[:, c, :], ones64[:, :], False, True, [X2, ones64], [ps])
            p.op("dve", lambda e, ps=ps: e.tensor_tensor(out=tmpA[:, :, :], in0=ps[0:64, :].rearrange("q (c i) -> q c i", c=8),
                                                         in1=b8(Mneg), op=ALU.add), r=[ps, Mneg], w=[tmpA])
            p.op("act", lambda e: e.activation(out=DT[:, :, :], in_=tmpA[:, :, :], func=AF.Exp), r=[tmpA], w=[DT])
            p.op("pool", lambda e: e.tensor_tensor(out=DST[:, :, :], in0=DT[:, :, :], in1=b8(S01), op=ALU.mult), r=[DT, S01], w=[DST])
            yield
            ps = misc.next()
            for c in range(8):
                mm(ps[:, c * 64:(c + 1) * 64], X1[:, c, :], U[:, :], c == 0, c == 7, [X1, U], [ps])
            p.op("act", lambda e, ps=ps: e.activation(out=EGB[:, :, :], in_=ps[:, :].rearrange("q (c i) -> q c i", c=8), func=AF.Exp),
                 r=[ps], w=[EGB])
            p.op("dve", lambda e: e.tensor_tensor(out=qdecT[:, :], in0=qTg[:, :], in1=EGB[:, :, :].rearrange("q c i -> q (c i)"), op=ALU.mult),
                 r=[qTg, EGB], w=[qdecT])
            yield
            ps = misc.next()
            for c in range(8):
                mm(ps[0:64, c * 64:(c + 1) * 64], kTg[:, c * 64:(c + 1) * 64], kTg[:, c * 64:(c + 1) * 64], c == 0, c == 7, [kTg], [ps])
            Y = chY[0]
            p.op("dve", lambda e, ps=ps: e.tensor_tensor(out=tmpA[:, :, :], in0=ps[0:64, :].rearrange("q (c i) -> q c i", c=8),
                                                         in1=DST[:, :, :], op=ALU.mult), r=[ps, DST], w=[tmpA])
            p.op("pool", lambda e, Y=Y: e.tensor_tensor(out=Y[:, :, :], in0=tmpA[:, :, :],
                                                        in1=nbeta_col[:, :].unsqueeze(2).to_broadcast([64, 8, 64]), op=ALU.mult),
                 r=[tmpA, nbeta_col], w=[Y])
            yield
            ps = misc.next()
            for c in range(8):
                mm(ps[0:64, c * 64:(c + 1) * 64], kTg[:, c * 64:(c + 1) * 64], qTg[:, c * 64:(c + 1) * 64], c == 0, c == 7, [kTg, qTg], [ps])
            p.op("dve", lambda e, ps=ps: e.tensor_tensor(out=PT[:, :, :], in0=ps[0:64, :].rearrange("q (c i) -> q c i", c=8),
                                                         in1=DT[:, :, :], op=ALU.mult), r=[ps, DT], w=[PT])
            X = chX[0]
            yield
            ps = misc.next()
            for c in range(8):
                p.op("pe", lambda e, c=c, ps=ps, Y=Y: e.transpose(out=ps[0:64, c * 64:(c + 1) * 64], in_=Y[:, c, :],
                                                                  identity=cm.identf[0:64, 0:64]), r=[Y, cm.identf], w=[ps])
            p.op("act", lambda e, ps=ps, X=X: e.copy(out=X[:, :, :], in_=ps[0:64, :].rearrange("q (c i) -> q c i", c=8)), r=[ps], w=[X])
            TT = chT[0]
            p.op("pool", lambda e, TT=TT, Y=Y: e.tensor_tensor(out=TT[:, :, :], in0=Y[:, :, :], in1=b8(cm.identf), op=ALU.add), r=[Y, cm.identf], w=[TT])
            for lvl in range(5):
                Xn, Yn, Tn = chX[(lvl + 1) % 2], chY[(lvl + 1) % 2], chT[(lvl + 1) % 2]
                yield
                ps = misc.next()
                for c in range(8):
                    mm(ps[0:64, c * 64:(c + 1) * 64], Y[:, c, :], X[:, c, :], c == 0, c == 7, [X, Y], [ps])
                p.op("act", lambda e, ps=ps, Xn=Xn: e.copy(out=Xn[:, :, :], in_=ps[0:64, :].rearrange("q (c i) -> q c i", c=8)), r=[ps], w=[Xn])
                if lvl < 4:
                    yield
                    ps = misc.next()
                    for c in range(8):
                        mm(ps[0:64, c * 64:(c + 1) * 64], X[:, c, :], Y[:, c, :], c == 0, c == 7, [X, Y], [ps])
                    p.op("dve", lambda e, ps=ps, Yn=Yn: e.tensor_copy(out=Yn[:, :, :], in_=ps[0:64, :].rearrange("q (c i) -> q c i", c=8)),
                         r=[ps], w=[Yn])
                yield
                ps = misc.next()
                for c in range(8):
                    mm(ps[0:64, c * 64:(c + 1) * 64], Xn[:, c, :], TT[:, c, :], c == 0, c == 7, [Xn, TT], [ps])
                p.op("dve", lambda e, ps=ps, Tn=Tn, TT=TT: e.tensor_tensor(out=Tn[:, :, :], in0=ps[0:64, :].rearrange("q (c i) -> q c i", c=8),
                                                                           in1=TT[:, :, :], op=ALU.add), r=[ps, TT], w=[Tn])
                X, Y, TT = Xn, Yn, Tn
            p.op("act", lambda e, TT=TT: e.copy(out=TTb[:, :, :], in_=TT[:, :, :]), r=[TT], w=[TTb])
            p.op("pool", lambda e: e.tensor_tensor(out=keg[:, :, :], in0=kTok[:, :, :], in1=eg_col[:, 0:8].unsqueeze(2).to_broadcast([64, 8, 128]),
                                                   op=ALU.mult), r=[kTok, eg_col], w=[keg])
            p.op("pool", lambda e: e.tensor_tensor(out=kdec[:, :, :], in0=kTok[:, :, :], in1=eg_col[:, 8:16].unsqueeze(2).to_broadcast([64, 8, 128]),
                                                   op=ALU.mult), r=[kTok, eg_col], w=[kdec])
            for half in range(2):
                yield
                ps = misc.next()
                for cc in range(4):
                    c = half * 4 + cc
                    mm(ps[0:64, cc * 128:(cc + 1) * 128], TTb[:, c, :], vTok[:, c, :], cc == 0, cc == 3, [TTb, vTok], [ps])
                p.op("dve", lambda e, ps=ps, half=half: e.tensor_tensor(
                    out=bu[:, half * 4:half * 4 + 4, :], in0=ps[0:64, :].rearrange("q (c d) -> q c d", c=4),
                    in1=beta_col[:, half * 4:half * 4 + 4].unsqueeze(2).to_broadcast([64, 4, 128]), op=ALU.mult), r=[ps, beta_col], w=[bu])
            yield
            ps = misc.next()
            for c in range(8):
                mm(ps[:, c * 64:(c + 1) * 64], keg[:, c, :], TTb[:, c, :], c == 0, c == 7, [keg, TTb], [ps])
            p.op("act", lambda e, ps=ps: e.copy(out=wT[:, :, :], in_=ps[:, :].rearrange("q (c i) -> q c i", c=8)), r=[ps], w=[wT])

        def gdn_a(c):
            mm(bR[0:64, 0:128], wT[:, c, :], Sb[:, :], True, True, [wT, Sb], [bR])
            vn = vnew.next()
            p.op("dve", lambda e: e.scalar_tensor_tensor(out=vn[:, :], in0=bR[0:64, 0:128], scalar=nbeta_col[:, c:c + 1], in1=bu[:, c, :],
                                                         op0=ALU.mult, op1=ALU.add), r=[bR, nbeta_col, bu], w=[vn])
            return vn

        def gdn_b(c, vn):
            cc = c % 4
            mm(bO[0:64, cc * 128:(cc + 1) * 128], qdecT[:, c * 64:(c + 1) * 64], Sb[:, :], cc == 0, False, [qdecT, Sb], [bO])
            mm(bO[0:64, cc * 128:(cc + 1) * 128], PT[:, c, :], vn[:, :], False, True, [PT, vn], [bO])
            mm(bR[:, 128:256], kdec[:, c, :], vn[:, :], True, True, [kdec, vn], [bR])
            p.op("dve", lambda e: e.scalar_tensor_tensor(out=Sb[:, :], in0=Sst[:, :], scalar=EGB[:, c, 63:64], in1=bR[:, 128:256],
                                                         op0=ALU.mult, op1=ALU.add), r=[Sst, EGB, bR], w=[Sb])
            p.op("dve", lambda e: e.scalar_tensor_tensor(out=Sst[:, :], in0=Sst[:, :], scalar=EGB[:, c, 63:64], in1=bR[:, 128:256],
                                                         op0=ALU.mult, op1=ALU.add), r=[Sst, EGB, bR], w=[Sst])
            if cc == 3:
                gdn_out(c - 3)

        def gdn_out(c0):
            o3 = bO[0:64, :].rearrange("q (c d) -> q c d", c=4)
            p.op("act", lambda e: e.activation(out=osq[:, :, :], in_=o3, func=AF.Square), r=[bO], w=[osq])
            p.op("dve", lambda e: e.tensor_reduce(out=oss[:, :], in_=osq[:, :, :], axis=AX.X, op=ALU.add), r=[osq], w=[oss])
            rsqrt_act(p, ors[:, :], oss[:, :], 1.0 / 128, p.eps_tile[0:64, 0:1], otmp[:, :], [oss], [ors], otmp)
            p.op("dve", lambda e: e.tensor_tensor(out=on1[:, :, :], in0=o3, in1=ors[:, :].unsqueeze(2).to_broadcast([64, 4, 128]), op=ALU.mult),
                 r=[bO, ors], w=[on1])
            p.op("pool", lambda e: e.tensor_tensor(out=on2[:, :, :], in0=on1[:, :, :], in1=gnorm[:, :].unsqueeze(1).to_broadcast([64, 4, 128]),
                                                   op=ALU.mult), r=[on1, gnorm], w=[on2])
            ps = misc.next()
            for cc in range(4):
                p.op("pe", lambda e, cc=cc, ps=ps: e.transpose(out=bfv(ps)[:, cc * 64:(cc + 1) * 64], in_=on2[:, cc, :],
                                                               identity=cm.ident[0:64, 0:64]), r=[on2, cm.ident], w=[ps])
            p.op("dve", lambda e, ps=ps: e.tensor_tensor(out=oTg[:, c0 * 64:c0 * 64 + 256], in0=bfv(ps)[:, 0:256],
                                                         in1=gate[:, c0 * 64:c0 * 64 + 256], op=ALU.mult), r=[ps, gate], w=[oTg])

        def attn_unit(h, kb):
            r_ = kb - 4 * t
            q0 = max(0, r_) * 128
            nq = TB - q0
            ps = psS.next()
            mm(ps[:, 0:nq], kT[h][:, kb * 128:(kb + 1) * 128], qT[h][:, q0:TB], True, True, [kTb[h][kb // 4], qT[h]], [ps])
            pt = PTa.next()
            p.op("act", lambda e: e.activation(out=pt[:, 0:nq], in_=ps[:, 0:nq], func=AF.Exp, scale=SCALE), r=[ps], w=[pt])
            if r_ >= 0:
                p.op("pool", lambda e: e.memset(pt[64:128, 0:64], 0.0), r=[], w=[pt])
            return (h, kb, q0, pt)

        def attn_unit_pv(st):
            h, kb, q0, pt = st
            for qs in range(q0 // 128, 4):
                first = (kb == 0 and qs == 0)
                last = (kb == 4 * t + qs)
                mm(psO[h][:, qs * 65:(qs + 1) * 65], pt[:, qs * 128 - q0:(qs + 1) * 128 - q0], vaug[h][:, kb, :], first, last,
                   [pt, vab[h][kb // 4]], [psO[h]])

        def attn_out():
            for h in range(2):
                o3 = psO[h][:, 0:260].rearrange("q (s c) -> q s c", s=4)
                p.op("dve", lambda e, o3=o3: e.reciprocal(out=rden[:, :], in_=o3[:, :, 64]), r=[psO[h]], w=[rden])
                p.op("dve", lambda e, o3=o3, h=h: e.tensor_tensor(out=on_tok[:, :, h * 64:(h + 1) * 64], in0=o3[:, :, 0:64],
                                                                  in1=rden[:, :].unsqueeze(2).to_broadcast([128, 4, 64]), op=ALU.mult),
                     r=[psO[h], rden], w=[on_tok])
            ps = misc.next()
            for qs in range(4):
                p.op("pe", lambda e, qs=qs, ps=ps: e.transpose(out=bfv(ps)[:, qs * 128:(qs + 1) * 128], in_=on_tok[:, qs, :],
                                                               identity=cm.ident[:, :]), r=[on_tok, cm.ident], w=[ps])
            p.op("act", lambda e, ps=ps: e.copy(out=oTm[:, :], in_=bfv(ps)[:, 0:512]), r=[ps], w=[oTm])
            p.dma(oT_own.ap.ap()[128:256, tok0:tok0 + TB], oTm[:, :], r=[oTm], w=[oT_own])

        units = [(h, kb) for h in range(2) for kb in range(4 * t + 4)]
        nu = len(units)

        def gdn_all():
            yield from gdn_prep_gen()
            vn_cur = {}
            for c in range(8):
                yield
                vn_cur[c] = gdn_a(c)
                yield
                gdn_b(c, vn_cur[c])

        NPIECES = 50
        LAG = 1
        ui = 0
        npc = 0
        pend = []

        def emit_unit(i):
            pend.append(attn_unit(*units[i]))
            if len(pend) > LAG:
                attn_unit_pv(pend.pop(0))

        for _ in gdn_all():
            npc += 1
            target = min(nu, npc * nu // NPIECES)
            while ui < target:
                emit_unit(ui)
                ui += 1
        assert npc <= NPIECES, npc
        while ui < nu:
            emit_unit(ui)
            ui += 1
        while pend:
            attn_unit_pv(pend.pop(0))
        p.dma(oT_own.ap.ap()[0:128, tok0:tok0 + TB], oTg[:, :], r=[oTg], w=[oT_own])
        attn_out()
    p.barrier()
    p.release(mk)


def core_weight_slices(inp, r):
    f32 = np.float32
    w_in = inp["w_in"]
    L = w_in.shape[0]
    o_q, o_k, o_v, o_g, o_a, o_b, o_cq, o_ckv, o_kr = 0, 512, 1024, 1536, 2048, 2052, 2056, 2312, 2440
    hs = slice(r * 128, (r + 1) * 128)
    kr = w_in[:, :, o_kr:o_kr + 32]
    krot = np.concatenate([kr[:, :, 16:32], kr[:, :, 0:16]], axis=-1)
    w_in_own = np.concatenate([
        w_in[:, :, o_q:o_q + 512][:, :, hs], w_in[:, :, o_k:o_k + 512][:, :, hs], w_in[:, :, o_v:o_v + 512][:, :, hs],
        w_in[:, :, o_g:o_g + 512][:, :, hs], w_in[:, :, o_cq:o_cq + 256], w_in[:, :, o_ckv:o_ckv + 128], kr, krot,
        w_in[:, :, o_a + r:o_a + r + 1], w_in[:, :, o_b + r:o_b + r + 1]], axis=-1)
    cw = inp["conv_w"]
    conv_own = np.stack([cw[:, :, s * 512 + r * 128: s * 512 + (r + 1) * 128] for s in range(3)], axis=1)
    conv_own = np.ascontiguousarray(np.transpose(conv_own, (0, 3, 1, 2)))
    wq = inp["w_q_up"]
    parts = []
    for hh in range(2):
        base = (2 * r + hh) * 96
        nope = wq[:, :, base:base + 64]
        rope = wq[:, :, base + 64:base + 96]
        rot = np.concatenate([rope[:, :, 16:32], rope[:, :, 0:16]], axis=-1)
        parts += [nope, rope, rot]
    wq_own = np.concatenate(parts, axis=-1)
    wkv = inp["w_kv_up"]
    b0, b1 = (2 * r) * 128, (2 * r + 1) * 128
    wkv_own = np.concatenate([wkv[:, :, b0:b0 + 64], wkv[:, :, b1:b1 + 64], wkv[:, :, b0 + 64:b0 + 128], wkv[:, :, b1 + 64:b1 + 128]], axis=-1)
    return {
        "w_in_own": np.ascontiguousarray(w_in_own, dtype=f32),
        "conv_own": np.ascontiguousarray(conv_own, dtype=f32),
        "alog_own": np.ascontiguousarray(inp["a_log"][:, r:r + 1], dtype=f32),
        "dtb_own": np.ascontiguousarray(inp["dt_bias"][:, r:r + 1], dtype=f32),
        "wq_own": np.ascontiguousarray(wq_own, dtype=f32),
        "wkv_own": np.ascontiguousarray(wkv_own, dtype=f32),
    }


def rope_freq_const():
    inv = (np.float32(10000.0) ** (-np.arange(0, 32, 2, dtype=np.float32) / np.float32(32))).astype(np.float32)
    t = np.zeros((96, 1), np.float32)
    t[64:80, 0] = -inv / np.float32(2 * np.pi)
    t[80:96, 0] = inv / np.float32(2 * np.pi)
    return t


MIX_W_SHAPES = {"w_in_own": [2, 1024, NCOL], "conv_own": [2, 128, 3, 4], "alog_own": [2, 1], "dtb_own": [2, 1],
                "wq_own": [2, 256, 256], "wkv_own": [2, 128, 256], "invf": [96, 1],
                "gdn_out_norm": [2, 128], "q_norm": [2, 256], "kv_norm": [2, 128]}


TOK_W = {"mix_post_norm": [2, D], "ffn_pre_norm": [2, D], "ffn_post_norm": [2, D], "mix_pre_norm": [2, D],
         "w_out": [2, D, D], "w_up": [2, D, DFF], "w_down": [2, DFF, D]}


def wout_rowmap(kk):
    return (kk // 2) * 128 if kk % 2 == 0 else 512 + (kk // 2) * 128


def build_norm_prog(NT):
    nc = bass.Bass("TRN2", target_bir_lowering=False)
    p = Prog(nc)
    W = {k: p.dram(k, TOK_W[k], F32, kind="ExternalInput") for k in ("mix_pre_norm",)}
    x_in = p.dram("x_in", [NT, D], F32, kind="ExternalInput")
    x_dummy = p.dram("x_dummy", [NT, D], F32)
    hT_out = p.dram("hT_out", [D, NT], BF16, kind="ExternalOutput")
    cm = setup_common(p)
    phase_tok(p, cm, 0, NT, W, x_in, None, x_dummy, hT_out, do_mix=False, do_ffn=False, gain_next=W["mix_pre_norm"][0, :])
    p.op("sp", lambda e: e.nop(), r=[hT_out])
    p.emit()
    p.close()
    return nc


def build_mix_prog(l, S):
    nc = bass.Bass("TRN2", target_bir_lowering=False)
    p = Prog(nc)
    Wc = {k: p.dram(k, shp, F32, kind="ExternalInput") for k, shp in MIX_W_SHAPES.items()}
    hT = p.dram("hT", [D, S], BF16, kind="ExternalInput")
    pos = p.dram("pos", [S], I32, kind="ExternalInput")
    oT = p.dram("oT_own", [256, S], BF16, kind="ExternalOutput")
    cm = setup_common(p)
    hv = hT.ap.ap().rearrange("(k q) t -> q k t", q=128)
    phase_mix(p, cm, l, S, Wc, lambda tok0: hv[:, :, tok0:tok0 + 512], pos, oT)
    p.op("sp", lambda e: e.nop(), r=[oT])
    p.emit()
    p.close()
    return nc


def build_tok_prog(l, NT, last):
    nc = bass.Bass("TRN2", target_bir_lowering=False)
    p = Prog(nc)
    W = {k: p.dram(k, shp, F32, kind="ExternalInput") for k, shp in TOK_W.items()}
    x_in = p.dram("x_in", [NT, D], F32, kind="ExternalInput")
    oT = p.dram("oT", [D, NT], BF16, kind="ExternalInput")
    x_out = p.dram("x_out", [NT, D], F32, kind="ExternalOutput")
    outs = [x_out]
    hT_out = None
    if not last:
        hT_out = p.dram("hT_out", [D, NT], BF16, kind="ExternalOutput")
        outs.append(hT_out)
    cm = setup_common(p)
    phase_tok(p, cm, l, NT, W, x_in, oT, x_out, hT_out, gain_next=(None if last else W["mix_pre_norm"][l + 1, :]),
              wout_rowmap=wout_rowmap)
    p.op("sp", lambda e: e.nop(), r=outs)
    p.emit()
    p.close()
    return nc


class HTSrc:
    def __init__(self, hT_sel, trackers, NT):
        self.view = hT_sel.ap.ap().rearrange("(r k q) t -> r q k t", r=4, q=128)
        self.trackers = trackers
        self.NT = NT

    def load(self, p, hb, tok0):
        rr, c0 = tok0 // self.NT, tok0 % self.NT
        p.dma(hb[:, :, :], self.view[rr][:, :, c0:c0 + 512], r=[self.trackers[rr]], w=[hb])


def build_fused(S, NT):
    nc = bass.Bass("TRN2", target_bir_lowering=False)
    p = Prog(nc)
    W = {k: p.dram(k, shp, F32, kind="ExternalInput") for k, shp in TOK_W.items()}
    Wc = {k: p.dram(k, shp, F32, kind="ExternalInput") for k, shp in MIX_W_SHAPES.items()}
    pos = p.dram("pos", [S], I32, kind="ExternalInput")
    x_in = p.dram("x_in", [NT, D], F32, kind="ExternalInput")
    meta = p.dram("meta", [1, 2], I32, kind="ExternalInput")
    out = p.dram("out", [NT, D], F32, kind="ExternalOutput")
    hT_own = [p.dram(f"hT_own{l}", [D, NT], BF16) for l in range(2)]
    hT_all = [p.dram(f"hT_all{l}", [8 * D, NT], BF16) for l in range(2)]
    oT_own = [p.dram(f"oT_own{l}", [256, S], BF16) for l in range(2)]
    oT_all = [p.dram(f"oT_all{l}", [8 * 256, S], BF16) for l in range(2)]
    hT_sel = [p.dram(f"hT_sel{l}", [4 * D, NT], BF16) for l in range(2)]
    hT_sel_tr = [[T(hT_sel[l].ap, f"hT_sel{l}_{rr}") for rr in range(4)] for l in range(2)]
    oT_sel = [p.dram(f"oT_sel{l}", [D, NT], BF16) for l in range(2)]
    x1 = p.dram("x1", [NT, D], F32)
    x_dummy = p.dram("x_dummy", [NT, D], F32)
    groups = [list(range(8))]
    cm = setup_common(p)
    meta_sb = p.tile("meta_sb", [1, 2], I32)
    p.dma(meta_sb[:, :], meta.ap.ap(), w=[meta_sb])
    p.load_meta_regs(meta_sb, 2)

    wbf = [{"w_out": p.dram(f"wob{l}", [D, D], BF16), "w_up": p.dram(f"wub{l}", [D, DFF], BF16),
            "w_down": p.dram(f"wdb{l}", [DFF, D], BF16)} for l in range(2)]

    def convert_weights(l):
        for name, rows in (("w_out", D), ("w_up", D), ("w_down", DFF)):
            src = W[name].ap.ap()[l]
            dst = wbf[l][name]
            for r0 in range(0, rows, 256):
                p.dma(dst.ap.ap()[r0:r0 + 256, :], src[r0:r0 + 256, :], w=[dst], eng="pool")

    def gather_hT(l):
        p.all_gather(hT_own[l], hT_all[l], groups)
        for rr in range(4):
            p.dma_dyn(hT_sel[l].ap.ap()[rr * D:(rr + 1) * D, :], hT_all[l], 0, rr * D * NT, [[NT, D], [1, NT]],
                      w=[hT_sel_tr[l][rr]])

    def gather_oT(l):
        p.all_gather(oT_own[l], oT_all[l], groups)
        p.dma_dyn(oT_sel[l].ap.ap(), oT_all[l], 1, 0, [[S, D], [1, NT]], w=[oT_sel[l]])

    convert_weights(0)
    phase_tok(p, cm, 0, NT, W, x_in, None, x_dummy, hT_own[0], do_mix=False, do_ffn=False, gain_next=W["mix_pre_norm"][0, :])
    gather_hT(0)
    for l in range(2):
        if l == 1:
            convert_weights(1)
        phase_mix(p, cm, l, S, Wc, HTSrc(hT_sel[l], hT_sel_tr[l], NT), pos, oT_own[l])
        gather_oT(l)
        last = (l == 1)
        phase_tok(p, cm, l, NT, W, x_in if l == 0 else x1, oT_sel[l], out if last else x1, None if last else hT_own[1],
                  gain_next=(None if last else W["mix_pre_norm"][l + 1, :]), wout_rowmap=wout_rowmap, wsrc=wbf[l])
        if not last:
            gather_hT(1)
    p.op("sp", lambda e: e.nop(), r=[out])
    p.emit()
    p.close()
    return nc


def kernel(**inp):
    inp = {k: np.asarray(v) for k, v in inp.items()}
    B, S = inp["x"].shape[0], inp["x"].shape[1]
    NR = 4
    NT = S // NR
    ncore = B * NR
    cores = list(range(ncore))
    x = np.ascontiguousarray(inp["x"], dtype=np.float32)
    tokw = {k: np.ascontiguousarray(inp[k], dtype=np.float32) for k in TOK_W}
    invf = rope_freq_const()
    ims = []
    for c in cores:
        b, r = c // NR, c % NR
        m = dict(tokw)
        m.update(core_weight_slices(inp, r))
        m["invf"] = invf
        for k in ("gdn_out_norm", "q_norm", "kv_norm"):
            m[k] = np.ascontiguousarray(inp[k], dtype=np.float32)
        m["pos"] = np.ascontiguousarray(inp["positions"][b], dtype=np.int32)
        m["x_in"] = np.ascontiguousarray(x[b, r * NT:(r + 1) * NT])
        m["meta"] = np.array([[b * NR * D * NT, b * NR * 256 * S + r * NT]], dtype=np.int32)
        ims.append(m)
    res = run_bass_kernel_spmd(build_fused(S, NT), ims, core_ids=cores)
    out = np.empty_like(x)
    for c in cores:
        b, r = c // NR, c % NR
        out[b, r * NT:(r + 1) * NT] = res.results[c]["out"]
    return out
```

```python
import types
import numpy as np
import concourse.bass as bass
import concourse.mybir as mybir
from concourse.bass_utils import run_bass_kernel_spmd

F32 = mybir.dt.float32
BF16 = mybir.dt.bfloat16
I32 = mybir.dt.int32
AF = mybir.ActivationFunctionType
ALU = mybir.AluOpType
AX = mybir.AxisListType

D = 1024
SEQ = 8192
DFF = 4096
EPS = 1e-6
INW = 2472
ENGS = ("pe", "act", "dve", "pool", "sp")


class T:
    __slots__ = ("ap", "name", "w", "r")

    def __init__(self, ap, name=""):
        self.ap = ap
        self.name = name
        self.w = None
        self.r = []

    def __getitem__(self, idx):
        return self.ap[idx]


class Op:
    __slots__ = ("eng", "fn", "deps", "inc", "sem", "val", "dma", "idx", "waits", "name", "incv")

    def __init__(self, eng, fn, dma, name):
        self.incv = 16
        self.eng = eng
        self.fn = fn
        self.deps = set()
        self.inc = False
        self.sem = None
        self.val = 0
        self.dma = dma
        self.waits = []
        self.name = name


def _freeze(fn):
    if fn.__closure__ is None:
        return fn
    cells = []
    for c in fn.__closure__:
        try:
            cells.append(types.CellType(c.cell_contents))
        except ValueError:
            cells.append(c)
    return types.FunctionType(fn.__code__, fn.__globals__, fn.__name__, fn.__defaults__, tuple(cells))


class RR:
    def __init__(self, items):
        self.items = list(items)
        self.i = 0

    def next(self):
        t = self.items[self.i % len(self.items)]
        self.i += 1
        return t


class Prog:
    def __init__(self, nc, n_dma_sems=24):
        self.nc = nc
        self.ops = {e: [] for e in ENGS}
        self.n_dma_sems = n_dma_sems
        self._stack = []
        self.nops = 0
        self.last = {e: None for e in ENGS}
        self.dmas_since_bar = []

    def mark(self):
        return len(self._stack)

    def release(self, mark):
        while len(self._stack) > mark:
            g = self._stack.pop()
            g.__exit__(None, None, None)

    def tile(self, name, shape, dtype, space="sbuf"):
        self.uid = getattr(self, "uid", 0) + 1
        name = f"t{self.uid}_{name}"
        if space == "sbuf":
            g = self.nc.sbuf_tensor(name, list(shape), dtype)
        else:
            g = self.nc.psum_tensor(name, list(shape), dtype)
        h = g.__enter__()
        self._stack.append(g)
        return T(h, name)

    def dram(self, name, shape, dtype, kind="Internal"):
        h = self.nc.dram_tensor(name, list(shape), dtype, kind=kind)
        return T(h, name)

    def op(self, eng, fn, r=(), w=(), dma=False, name="", extra=(), incv=16):
        o = Op(eng, _freeze(fn), dma, name)
        o.incv = incv
        for t in r:
            if t.w is not None:
                o.deps.add(t.w)
        for t in w:
            if t.w is not None:
                o.deps.add(t.w)
            for rd in t.r:
                o.deps.add(rd)
        for d in extra:
            if d is not None:
                o.deps.add(d)
        for t in w:
            t.w = o
            t.r = []
        for t in r:
            if t.w is not o:
                t.r.append(o)
        o.deps.discard(o)
        o.idx = len(self.ops[eng])
        self.ops[eng].append(o)
        self.last[eng] = o
        if dma:
            self.dmas_since_bar.append(o)
        self.nops += 1
        return o

    def dma(self, out_ap, in_ap, r=(), w=(), eng="sp", name="", **kw):
        return self.op(eng, lambda e: e.dma_start(out=out_ap, in_=in_ap, **kw), r=r, w=w, dma=True, name=name)

    def barrier(self):
        pend = list(self.dmas_since_bar)
        self.dmas_since_bar = []
        for e in ENGS:
            mine = [o for o in pend if o.eng == e]
            if mine:
                self.op(e, lambda en: en.nop(), extra=mine, name="bar_dma")
        marks = [self.last[e] for e in ENGS]
        for e in ENGS:
            self.op(e, lambda en: en.nop(), extra=marks, name="bar")

    def load_meta_regs(self, meta_sb, n):
        self.regs = {}

        def fn(e):
            last = None
            for i in range(n):
                g = e.alloc_register(f"meta{i}")
                last = e.reg_load(g, meta_sb[0:1, i:i + 1])
                self.regs[i] = g
            self.regs["scratch"] = e.alloc_register("metas")
            return last
        self.op("pool", fn, r=[meta_sb], name="meta_regs")

    def dma_dyn(self, out_ap, src_t, reg_i, static_off, ap_list, r=(), w=(), name=""):
        def fn(e):
            gs = self.regs["scratch"]
            e.reg_add(gs, self.regs[reg_i], static_off)
            return e.dma_start(out=out_ap, in_=bass.AP(src_t.ap, gs, ap_list))
        return self.op("pool", fn, r=list(r) + [src_t], w=w, dma=True, name=name)

    def all_gather(self, src_t, dst_t, groups):
        return self.op("pool", lambda e: e.collective_compute("AllGather", ALU.bypass, replica_groups=groups,
                                                              ins=[src_t.ap.ap()], outs=[dst_t.ap.ap()]),
                       r=[src_t], w=[dst_t], dma=True, incv=1, name="allgather")

    def emit(self):
        nc = self.nc

        def skip(d, o):
            return d.eng == "pe" and o.eng == "pe" and not d.dma and not o.dma

        for e in ENGS:
            for o in self.ops[e]:
                for d in o.deps:
                    if not skip(d, o):
                        d.inc = True
        eng_sem = {e: nc.alloc_semaphore(name=f"s_{e}") for e in ENGS}
        dma_pool = {}
        dma_state = {}
        for e in ENGS:
            if any(o.dma for o in self.ops[e]):
                dma_pool[e] = [nc.alloc_semaphore(name=f"d_{e}{i}") for i in range(self.n_dma_sems)]
                dma_state[e] = [0, [0] * self.n_dma_sems]
        for e in ENGS:
            cnt = 0
            for o in self.ops[e]:
                if o.dma:
                    st = dma_state[e]
                    i = st[0] % self.n_dma_sems
                    st[0] += 1
                    prev = st[1][i]
                    o.sem = dma_pool[e][i]
                    o.val = prev + o.incv
                    st[1][i] = o.val
                    o.inc = True
                    o.waits.append((o.sem, prev))
                elif o.inc:
                    cnt += 1
                    o.sem = eng_sem[e]
                    o.val = cnt
        for e in ENGS:
            seen = {}
            for o in self.ops[e]:
                need = {}
                for (s, v) in o.waits:
                    if v > 0:
                        need[id(s)] = (s, v)
                if e == "pool" and o.inc and not o.dma and o.val > 1:
                    need[id(eng_sem[e])] = (eng_sem[e], o.val - 1)
                for d in o.deps:
                    if skip(d, o):
                        continue
                    k = id(d.sem)
                    if k not in need or need[k][1] < d.val:
                        need[k] = (d.sem, d.val)
                waits = []
                for k, (s, v) in need.items():
                    if seen.get(k, 0) >= v:
                        continue
                    seen[k] = v
                    waits.append((s, v))
                o.waits = waits

        with nc.Block() as block:
            def mk(e):
                def body(eng):
                    for o in self.ops[e]:
                        for (s, v) in o.waits:
                            eng.wait_ge(s, v)
                        ins = o.fn(eng)
                        if o.inc:
                            ins.then_inc(o.sem, o.incv if o.dma else 1)
                return body
            if self.ops["sp"]:
                block.sync(mk("sp"))
            if self.ops["pe"]:
                block.tensor(mk("pe"))
            if self.ops["act"]:
                block.scalar(mk("act"))
            if self.ops["dve"]:
                block.vector(mk("dve"))
            if self.ops["pool"]:
                block.gpsimd(mk("pool"))

    def close(self):
        self.release(0)


class Common:
    pass


def setup_common(p):
    c = Common()
    idf = p.tile("c_idf", [128, 128], F32)
    p.op("pool", lambda e: e.memset(idf[:, :], 0.0), w=[idf])
    p.op("pool", lambda e: e.affine_select(out=idf[:, :], in_=idf[:, :], pattern=[[-1, 128]],
                                           compare_op=ALU.not_equal, fill=1.0, base=0, channel_multiplier=1),
         r=[idf], w=[idf])
    c.identf = idf
    c.ident = p.tile("c_ident", [128, 128], BF16)
    p.op("dve", lambda e: e.tensor_copy(out=c.ident[:, :], in_=idf[:, :]), r=[idf], w=[c.ident])
    p.eps_tile = p.tile("c_eps", [128, 1], F32)
    p.op("pool", lambda e: e.memset(p.eps_tile[:, :], EPS), w=[p.eps_tile])
    return c


def bcast_load(p, dst, src_ap_1d, n, eng="sp"):
    p.dma(dst[:, :], src_ap_1d.partition_broadcast(128), w=[dst], eng=eng)


def rsqrt_act(p, out_ap, in_ap, scale, bias_ap, tmp_ap, r, w, wtmp):
    p.op("act", lambda e: e.activation(out=tmp_ap, in_=in_ap, func=AF.Ln, bias=bias_ap, scale=scale), r=list(r) + [p.eps_tile], w=[wtmp])
    p.op("act", lambda e: e.activation(out=out_ap, in_=tmp_ap, func=AF.Exp, scale=-0.5), r=[wtmp], w=list(w))


def rstd_from_ss(p, ss, rstd, n, tmp):
    rsqrt_act(p, rstd[:, :], ss[:, :], 1.0 / n, p.eps_tile[:, 0:1], tmp[:, :], [ss], [rstd], tmp)


def norm_to_hT(p, cm, x_ap, gain_t, hT_blk, j, sc, psT):
    xt = sc["xt"]
    p.op("act", lambda e: e.activation(out=sc["junk"][:, :], in_=x_ap, func=AF.Square, accum_out=sc["ss1"][:, 0:1]),
         r=[xt], w=[sc["ss1"]])
    rstd_from_ss(p, sc["ss1"], sc["rs1"], D, sc["tmp1"])
    hb = sc["hb"]
    p.op("dve", lambda e: e.scalar_tensor_tensor(out=hb[:, :], in0=x_ap, scalar=sc["rs1"][:, 0:1], in1=gain_t[:, :],
                                                 op0=ALU.mult, op1=ALU.mult), r=[xt, sc["rs1"], gain_t], w=[hb])
    pt = psT.next()
    for k in range(8):
        p.op("pe", lambda e, k=k: e.transpose(out=pt[:, k * 128:(k + 1) * 128], in_=hb[:, k * 128:(k + 1) * 128],
                                              identity=cm.ident[:, :]), r=[hb, cm.ident], w=[pt])
    p.op("act", lambda e: e.copy(out=hT_blk[:, :, j * 128:(j + 1) * 128],
                                 in_=pt[:, :].rearrange("p (k t) -> p k t", k=8)), r=[pt], w=[hT_blk])


def phase_tok(p, cm, l, NT, W, x_in, oT, x_out, hT_out, do_mix=True, do_ffn=True, gain_next=None, wout_rowmap=None,
              load_oT=None, wsrc=None):
    mk = p.mark()
    TB = 512
    nblk = NT // TB
    psA = RR([p.tile(f"psA{i}", [128, 512], F32, space="psum") for i in range(5)])
    psT = RR([p.tile(f"psT{i}", [128, 1024], BF16, space="psum") for i in range(2)])

    def gain_tile(name, ap1d):
        t = p.tile("g_" + name, [128, D], F32)
        bcast_load(p, t, ap1d, D)
        return t

    if do_mix:
        g_post = gain_tile("post", W["mix_post_norm"][l, :])
        wout = p.tile("wout", [128, 8, D], BF16)
        wo = W["w_out"].ap
        for k in range(8):
            r0 = wout_rowmap(k) if wout_rowmap else k * 128
            if wsrc is not None:
                p.dma(wout[:, k, :], wsrc["w_out"].ap.ap()[r0:r0 + 128, :], r=[wsrc["w_out"]], w=[wout])
            else:
                p.dma(wout[:, k, :], wo[l, r0:r0 + 128, :], w=[wout], eng="pool")
        oTb = p.tile("oTb", [128, 8, TB], BF16)
    if do_ffn:
        g_pre = gain_tile("pre", W["ffn_pre_norm"][l, :])
        g_fpost = gain_tile("fpost", W["ffn_post_norm"][l, :])
        wup_bufs = RR([p.tile(f"wup{i}", [128, 8, 512], BF16) for i in range(3)])
        wdn_bufs = RR([p.tile(f"wdn{i}", [128, 32, 256], BF16) for i in range(2)])
        fT = p.tile("fT", [128, 32, TB], BF16)
        h2T = p.tile("h2T", [128, 8, TB], BF16)
    if gain_next is not None:
        g_next = gain_tile("next", gain_next)
        hTn = RR([p.tile(f"hTn{i}", [128, 8, TB], BF16) for i in range(2)])
    xb = p.tile("xb", [128, 4, D], F32)
    fsb = p.tile("fsb", [128, 4, D], F32)
    ss4 = p.tile("ss4", [128, 4, 4], F32)
    sc = {
        "xt": xb,
        "junk": p.tile("junk", [128, D], BF16),
        "ss1": p.tile("ss1", [128, 1], F32), "rs1": p.tile("rs1", [128, 1], F32), "tmp1": p.tile("tmp1", [128, 1], F32),
        "sst": p.tile("sst", [128, 1], F32),
        "hb": p.tile("hb", [128, D], BF16),
    }
    tt = p.tile("tt", [128, D], F32)
    relu_t = RR([p.tile(f"relu{i}", [128, 512], BF16) for i in range(2)])

    x_in_v = x_in.ap.ap().rearrange("(n j q) d -> n q j d", j=4, q=128)
    x_out_v = x_out.ap.ap().rearrange("(n j q) d -> n q j d", j=4, q=128)

    def evac(ps, j, col0, ncol, slot):
        p.op("act", lambda e: e.activation(out=sc["junk"][:, 0:ncol], in_=ps[:, 0:ncol], func=AF.Square,
                                           accum_out=ss4[:, j, slot:slot + 1]), r=[ps], w=[ss4])
        p.op("act", lambda e: e.copy(out=fsb[:, j, col0:col0 + ncol], in_=ps[:, 0:ncol]), r=[ps], w=[fsb])

    def finish_residual(j, nslot, gain_t):
        p.op("dve", lambda e: e.tensor_reduce(out=sc["sst"][:, :], in_=ss4[:, j, 0:nslot], axis=AX.X, op=ALU.add),
             r=[ss4], w=[sc["sst"]])
        rstd_from_ss(p, sc["sst"], sc["rs1"], D, sc["tmp1"])
        p.op("dve", lambda e: e.scalar_tensor_tensor(out=tt[:, :], in0=fsb[:, j, :], scalar=sc["rs1"][:, 0:1], in1=gain_t[:, :],
                                                     op0=ALU.mult, op1=ALU.mult), r=[fsb, sc["rs1"], gain_t], w=[tt])
        p.op("dve", lambda e: e.tensor_tensor(out=xb[:, j, :], in0=xb[:, j, :], in1=tt[:, :], op=ALU.add),
             r=[tt, xb], w=[xb])

    for n in range(nblk):
        tok0 = n * TB
        p.dma(xb[:, :, :], x_in_v[n], r=[x_in], w=[xb])
        if do_mix:
            if load_oT is not None:
                load_oT(p, oTb, tok0)
            else:
                p.dma(oTb[:, :, :], oT.ap.ap().rearrange("(k q) t -> q k t", q=128)[:, :, tok0:tok0 + TB], r=[oT], w=[oTb])
            for j in range(4):
                for h in range(2):
                    ps = psA.next()
                    for k in range(8):
                        p.op("pe", lambda e, k=k, h=h, ps=ps, j=j: e.matmul(
                            ps[:, :], lhsT=oTb[:, k, j * 128:(j + 1) * 128], rhs=wout[:, k, h * 512:(h + 1) * 512],
                            start=(k == 0), stop=(k == 7)), r=[oTb, wout], w=[ps])
                    evac(ps, j, h * 512, 512, h)
                finish_residual(j, 2, g_post)
        if do_ffn:
            for j in range(4):
                norm_to_hT(p, cm, xb[:, j, :], g_pre, h2T, j, sc, psT)
            if wsrc is not None:
                wu_l, wd_l = wsrc["w_up"].ap.ap(), wsrc["w_down"].ap.ap()
                wq_eng, wr_u, wr_d = "sp", [wsrc["w_up"]], [wsrc["w_down"]]
            else:
                wu_l, wd_l = W["w_up"].ap.ap()[l], W["w_down"].ap.ap()[l]
                wq_eng, wr_u, wr_d = "pool", [], []
            for g in range(8):
                wb = wup_bufs.next()
                p.dma(wb[:, :, :], wu_l.rearrange("(k q) f -> q k f", q=128)[:, :, g * 512:(g + 1) * 512], r=wr_u, w=[wb], eng=wq_eng)
                for cc in range(4):
                    c = g * 4 + cc
                    ps = psA.next()
                    for k in range(8):
                        p.op("pe", lambda e, k=k, cc=cc, ps=ps, wb=wb: e.matmul(
                            ps[:, :], lhsT=wb[:, k, cc * 128:(cc + 1) * 128], rhs=h2T[:, k, :],
                            start=(k == 0), stop=(k == 7)), r=[wb, h2T], w=[ps])
                    rt = relu_t.next()
                    p.op("act", lambda e, ps=ps, rt=rt: e.activation(out=rt[:, :], in_=ps[:, :], func=AF.Relu), r=[ps], w=[rt])
                    p.op("pool", lambda e, c=c, rt=rt: e.tensor_tensor(out=fT[:, c, :], in0=rt[:, :], in1=rt[:, :], op=ALU.mult),
                         r=[rt], w=[fT])
            for qd in range(4):
                wq = wdn_bufs.next()
                p.dma(wq[:, :, :], wd_l.rearrange("(c q) f -> q c f", q=128)[:, :, qd * 256:(qd + 1) * 256], r=wr_d, w=[wq], eng=wq_eng)
                for j in range(4):
                    ps = psA.next()
                    for c in range(32):
                        p.op("pe", lambda e, c=c, j=j, ps=ps, wq=wq: e.matmul(
                            ps[:, 0:256], lhsT=fT[:, c, j * 128:(j + 1) * 128], rhs=wq[:, c, :],
                            start=(c == 0), stop=(c == 31)), r=[fT, wq], w=[ps])
                    evac(ps, j, qd * 256, 256, qd)
            for j in range(4):
                finish_residual(j, 4, g_fpost)
        p.dma(x_out_v[n], xb[:, :, :], r=[xb], w=[x_out])
        if gain_next is not None:
            hb_ = hTn.next()
            for j in range(4):
                norm_to_hT(p, cm, xb[:, j, :], g_next, hb_, j, sc, psT)
            p.dma(hT_out.ap.ap().rearrange("(k q) t -> q k t", q=128)[:, :, tok0:tok0 + TB], hb_[:, :, :], r=[hb_], w=[hT_out])
    p.barrier()
    p.release(mk)


GQ, GK, GV, GG, CQ, CKV, KR, KROT, AB, NCOL = 0, 128, 256, 384, 512, 768, 896, 928, 960, 962
NEG = -1.0e30


def phase_mix(p, cm, l, S, Wc, hT_src, pos, oT_own, dbg=None):
    mk = p.mark()
    TB = 512
    nblk = S // TB
    NKB = S // 128
    banks = [p.tile(f"bank{i}", [128, 512], F32, space="psum") for i in range(8)]
    misc = RR(banks[0:2])
    psS = RR(banks[2:4])
    psO = banks[4:6]
    bR = banks[6]
    bO = banks[7]

    def bfv(bank):
        return bank.ap.bitcast(BF16)

    def t_(name, shape, dt=F32):
        return p.tile(name, shape, dt)

    ones_bf = t_("ones_bf", [128, 128], BF16)
    p.op("pool", lambda e: e.memset(ones_bf[:, :], 1.0), w=[ones_bf])
    ones64 = t_("ones64", [64, 64])
    p.op("pool", lambda e: e.memset(ones64[:, :], 1.0), w=[ones64])

    def mask_tile(name, shape, init, pattern, cmul, cmp_op, fill):
        t = t_(name, shape)
        p.op("pool", lambda e: e.memset(t.ap[:], init), w=[t])
        p.op("pool", lambda e: e.affine_select(out=t.ap[:], in_=t.ap[:], pattern=pattern, compare_op=cmp_op, fill=fill,
                                               base=0, channel_multiplier=cmul), r=[t], w=[t])
        return t

    U = mask_tile("mU", [64, 64], 1.0, [[1, 64]], -1, ALU.is_ge, 0.0)
    Ust = mask_tile("mUst", [64, 64], 1.0, [[-1, 64]], 1, ALU.is_gt, 0.0)
    Mneg = mask_tile("mMneg", [64, 64], 0.0, [[1, 64]], -1, ALU.is_ge, NEG)
    S01 = mask_tile("mS01", [64, 64], 1.0, [[1, 64]], -1, ALU.is_gt, 0.0)

    def b8(m):
        return m.ap[0:64, 0:64].unsqueeze(1).to_broadcast([64, 8, 64])
    c_eps128 = t_("c_eps128", [128, 1])
    p.op("pool", lambda e: e.memset(c_eps128[:, :], 128.0 * EPS), w=[c_eps128])
    c_one = t_("c_one", [128, 1])
    p.op("pool", lambda e: e.memset(c_one[:, :], 1.0), w=[c_one])

    Wown = t_("Wown", [128, 8, NCOL], BF16)
    wsrc = Wc["w_in_own"].ap.ap()[l].rearrange("(k q) c -> q k c", q=128)
    for k0 in range(0, 8, 2):
        p.dma(Wown[:, k0:k0 + 2, :], wsrc[:, k0:k0 + 2, :], w=[Wown], eng="pool")
    convw = t_("convw", [128, 3, 4])
    p.dma(convw[:, :, :], Wc["conv_own"].ap.ap()[l], w=[convw])
    gnorm = t_("gnorm", [64, 128])
    p.dma(gnorm[:, :], Wc["gdn_out_norm"].ap.ap()[l].partition_broadcast(64), w=[gnorm])
    nA = t_("nA", [64, 1])
    p.dma(nA[:, :], Wc["alog_own"].ap.ap()[l].partition_broadcast(64), w=[nA])
    p.op("act", lambda e: e.activation(out=nA[:, :], in_=nA[:, :], func=AF.Exp), r=[nA], w=[nA])
    p.op("dve", lambda e: e.tensor_scalar(out=nA[:, :], in0=nA[:, :], scalar1=-1.0, scalar2=None, op0=ALU.mult), r=[nA], w=[nA])
    dtb = t_("dtb", [64, 1])
    p.dma(dtb[:, :], Wc["dtb_own"].ap.ap()[l].partition_broadcast(64), w=[dtb])
    invf = t_("invf", [96, 1])
    p.dma(invf[:, :], Wc["invf"].ap.ap(), w=[invf])
    wq_f = t_("wq_f", [128, 2, 256])
    p.dma(wq_f[:, :, :], Wc["wq_own"].ap.ap()[l].rearrange("(k q) c -> q k c", q=128), w=[wq_f])
    qg = t_("qg", [128, 2])
    p.dma(qg[:, :], Wc["q_norm"].ap.ap()[l].rearrange("(k q) -> q k", q=128), w=[qg], allow_slow_non_contiguous=True)
    wq = t_("wq", [128, 2, 256], BF16)
    p.op("dve", lambda e: e.tensor_tensor(out=wq[:, :, :], in0=wq_f[:, :, :], in1=qg[:, :].unsqueeze(2).to_broadcast([128, 2, 256]),
                                          op=ALU.mult), r=[wq_f, qg], w=[wq])
    wkv_f = t_("wkv_f", [128, 256])
    p.dma(wkv_f[:, :], Wc["wkv_own"].ap.ap()[l], w=[wkv_f])
    kvg = t_("kvg", [128, 1])
    p.dma(kvg[:, :], Wc["kv_norm"].ap.ap()[l].rearrange("(q o) -> q o", o=1), w=[kvg])
    wkv = t_("wkv", [128, 256], BF16)
    p.op("dve", lambda e: e.tensor_scalar(out=wkv[:, :], in0=wkv_f[:, :], scalar1=kvg[:, 0:1], scalar2=None, op0=ALU.mult),
         r=[wkv_f, kvg], w=[wkv])

    kT = [t_(f"kT{h}", [96, S], BF16) for h in range(2)]
    vaug = [t_(f"vaug{h}", [128, NKB, 65], BF16) for h in range(2)]
    kTb = [[T(kT[h].ap, f"kT{h}_{i}") for i in range(nblk)] for h in range(2)]
    vab = [[T(vaug[h].ap, f"va{h}_{i}") for i in range(nblk)] for h in range(2)]
    for h in range(2):
        p.op("pool", lambda e, h=h: e.memset(vaug[h][:, :, 64:65], 1.0), w=vab[h])

    hTb = RR([t_(f"hTb{i}", [128, 8, TB], BF16) for i in range(2)])
    raw = [t_(f"raw{s}", [128, 3 + TB]) for s in range(3)]
    for s in range(3):
        p.op("pool", lambda e, s=s: e.memset(raw[s][:, 0:3], 0.0), w=[raw[s]])
    caccs = [t_(f"cacc{s}", [128, TB]) for s in range(3)]
    sqb = t_("sqb", [128, 2, TB], BF16)
    rn = t_("rn", [128, TB])
    rtmp = t_("rtmp", [128, TB])
    qTg = t_("qTg", [128, TB], BF16)
    kTg = t_("kTg", [128, TB], BF16)
    vTg = t_("vTg", [128, TB], BF16)
    gate = t_("gate", [128, TB])
    cqT = t_("cqT", [128, 2, TB], BF16)
    ckvT = t_("ckvT", [128, TB], BF16)
    rstd_q = t_("rstd_q", [128, TB])
    rstd_kv = t_("rstd_kv", [128, TB])
    rkv_col = t_("rkv_col", [128, 4])
    rkv_tmp = t_("rkv_tmp", [128, 4])
    ab_row = t_("ab_row", [2, TB])
    ab_col = t_("ab_col", [64, 8, 2])
    la_col = t_("la_col", [64, 8])
    sp_t = t_("sp_t", [64, 8])
    beta_col = t_("beta_col", [64, 8])
    nbeta_col = t_("nbeta_col", [64, 8])
    eg_col = t_("eg_col", [64, 16])
    X1 = t_("X1", [64, 8, 128])
    X2 = t_("X2", [64, 8, 64])
    DT = t_("DT", [64, 8, 64])
    DST = t_("DST", [64, 8, 64])
    tmpA = t_("tmpA", [64, 8, 64])
    chX = [t_(f"chX{i}", [64, 8, 64]) for i in range(2)]
    chY = [t_(f"chY{i}", [64, 8, 64]) for i in range(2)]
    chT = [t_(f"chT{i}", [64, 8, 64]) for i in range(2)]
    TTb = t_("TTb", [64, 8, 64], BF16)
    PT = t_("PTg", [64, 8, 64], BF16)
    EGB = t_("EGB", [128, 8, 64])
    qdecT = t_("qdecT", [128, TB], BF16)
    kTok = t_("kTok", [64, 8, 128], BF16)
    vTok = t_("vTok", [64, 8, 128], BF16)
    keg = t_("keg", [64, 8, 128], BF16)
    kdec = t_("kdec", [64, 8, 128], BF16)
    bu = t_("bu", [64, 8, 128])
    wT = t_("wT", [128, 8, 64], BF16)
    Sst = t_("Sst", [128, 128])
    Sb = t_("Sb", [128, 128], BF16)
    p.op("pool", lambda e: e.memset(Sst[:, :], 0.0), w=[Sst])
    p.op("pool", lambda e: e.memset(Sb[:, :], 0.0), w=[Sb])
    vnew = RR([t_(f"vnew{i}", [64, 128], BF16) for i in range(2)])
    osq = t_("osq", [64, 4, 128])
    oss = t_("oss", [64, 4])
    ors = t_("ors", [64, 4])
    otmp = t_("otmp", [64, 4])
    on1 = t_("on1", [64, 4, 128])
    on2 = t_("on2", [64, 4, 128], BF16)
    oTg = t_("oTg", [128, TB], BF16)
    posi = t_("posi", [96, TB], I32)
    ph = t_("ph", [96, 2, TB])
    phi = t_("phi", [96, 2, TB], I32)
    phm = t_("phm", [96, 2, TB])
    CS = t_("CS", [96, 2, TB])
    u1 = t_("u1", [96, TB])
    u2 = t_("u2", [96, TB])
    qT = [t_(f"qT{h}", [96, TB], BF16) for h in range(2)]
    PTa = RR([t_(f"PTa{i}", [128, TB], BF16) for i in range(3)])
    rden = t_("rden", [128, 4])
    on_tok = t_("on_tok", [128, 4, 128], BF16)
    oTm = t_("oTm", [128, TB], BF16)
    SCALE = float(96 ** -0.5)

    def mm(out_ap, lhsT, rhs, start, stop, r, w):
        p.op("pe", lambda e: e.matmul(out_ap, lhsT=lhsT, rhs=rhs, start=start, stop=stop, skip_group_check=True), r=r, w=w)

    def proj_group(hb, c0, M):
        ps = misc.next()
        for k in range(8):
            mm(ps[0:M, :], Wown[:, k, c0:c0 + M], hb[:, k, :], k == 0, k == 7, [Wown, hb], [ps])
        return ps

    for t in range(nblk):
        tok0 = t * TB
        hb = hTb.next()
        if callable(getattr(hT_src, "load", None)):
            hT_src.load(p, hb, tok0)
        else:
            p.dma(hb[:, :, :], hT_src(tok0), w=[hb])
        p.dma(posi[64:96, :], pos.ap.ap()[tok0:tok0 + TB].partition_broadcast(32), r=[pos], w=[posi])
        R_ = slice(64, 96)
        p.op("dve", lambda e: e.tensor_copy(out=u1[R_, :], in_=posi[R_, :]), r=[posi], w=[u1])
        p.op("dve", lambda e: e.tensor_scalar(out=ph[R_, 0, :], in0=u1[R_, :], scalar1=invf[R_, 0:1], scalar2=None, op0=ALU.mult),
             r=[u1, invf], w=[ph])
        p.op("dve", lambda e: e.tensor_scalar(out=ph[R_, 1, :], in0=ph[R_, 0, :], scalar1=0.25, scalar2=None, op0=ALU.add),
             r=[ph], w=[ph])
        p.op("dve", lambda e: e.tensor_copy(out=phi[R_, :, :], in_=ph[R_, :, :]), r=[ph], w=[phi])
        p.op("dve", lambda e: e.tensor_copy(out=phm[R_, :, :], in_=phi[R_, :, :]), r=[phi], w=[phm])
        p.op("dve", lambda e: e.tensor_tensor(out=ph[R_, :, :], in0=ph[R_, :, :], in1=phm[R_, :, :], op=ALU.subtract), r=[ph, phm], w=[ph])
        p.op("dve", lambda e: e.scalar_tensor_tensor(out=phm[R_, :, :], in0=ph[R_, :, :], scalar=0.5, in1=ph[R_, :, :],
                                                     op0=ALU.is_gt, op1=ALU.subtract), r=[ph], w=[phm])
        p.op("dve", lambda e: e.scalar_tensor_tensor(out=ph[R_, :, :], in0=phm[R_, :, :], scalar=0.5, in1=phm[R_, :, :],
                                                     op0=ALU.is_gt, op1=ALU.subtract), r=[phm], w=[ph])
        p.op("act", lambda e: e.activation(out=CS[R_, :, :], in_=ph[R_, :, :], func=AF.Sin, scale=float(2 * np.pi)), r=[ph], w=[CS])


        for s, c0 in enumerate((GQ, GK, GV)):
            if t > 0:
                p.op("pool", lambda e, s=s: e.tensor_copy(out=raw[s][:, 0:3], in_=raw[s][:, TB:TB + 3]), r=[raw[s]], w=[raw[s]])
            ps = proj_group(hb, c0, 128)
            p.op("act", lambda e, s=s, ps=ps: e.copy(out=raw[s][:, 3:3 + TB], in_=ps[:, :]), r=[ps], w=[raw[s]])
        ps = proj_group(hb, GG, 128)
        p.op("act", lambda e, ps=ps: e.copy(out=gate[:, :], in_=ps[:, :]), r=[ps], w=[gate])
        for kc in range(2):
            ps = proj_group(hb, CQ + kc * 128, 128)
            p.op("act", lambda e, ps=ps, kc=kc: e.copy(out=cqT[:, kc, :], in_=ps[:, :]), r=[ps], w=[cqT])
            p.op("act", lambda e, ps=ps, kc=kc: e.activation(out=sqb[:, kc, :], in_=ps[:, :], func=AF.Square), r=[ps], w=[sqb])
        ps = misc.next()
        for kc in range(2):
            mm(ps[:, :], ones_bf[:, :], sqb[:, kc, :], kc == 0, kc == 1, [ones_bf, sqb], [ps])
        rsqrt_act(p, rstd_q[:, :], ps[:, :], 1.0 / 256, p.eps_tile[:, 0:1], rtmp[:, :], [ps], [rstd_q], rtmp)
        ps = proj_group(hb, CKV, 128)
        p.op("act", lambda e, ps=ps: e.copy(out=ckvT[:, :], in_=ps[:, :]), r=[ps], w=[ckvT])
        p.op("act", lambda e, ps=ps: e.activation(out=sqb[:, 0, :], in_=ps[:, :], func=AF.Square), r=[ps], w=[sqb])
        ps = misc.next()
        mm(ps[:, :], ones_bf[:, :], sqb[:, 0, :], True, True, [ones_bf, sqb], [ps])
        rsqrt_act(p, rstd_kv[:, :], ps[:, :], 1.0 / 128, p.eps_tile[:, 0:1], rtmp[:, :], [ps], [rstd_kv], rtmp)
        ps = misc.next()
        for j in range(4):
            mm(ps[:, j:j + 1], sqb[:, 0, j * 128:(j + 1) * 128], ones_bf[:, 0:1], j == 0, j == 3, [ones_bf, sqb], [ps])
        rsqrt_act(p, rkv_col[:, :], ps[:, 0:4], 1.0 / 128, p.eps_tile[:, 0:1], rkv_tmp[:, :], [ps], [rkv_col], rkv_tmp)

        psKA = proj_group(hb, KR - 64, 96)
        p.op("dve", lambda e, ps=psKA: e.tensor_tensor(out=u1[R_, :], in0=ps[R_, :], in1=CS[R_, 1, :], op=ALU.mult), r=[psKA, CS], w=[u1])
        psKB = proj_group(hb, KROT - 64, 96)
        p.op("dve", lambda e, ps=psKB: e.tensor_tensor(out=u2[R_, :], in0=ps[R_, :], in1=CS[R_, 0, :], op=ALU.mult), r=[psKB, CS], w=[u2])
        for h in range(2):
            p.op("pool", lambda e, h=h: e.tensor_tensor(out=kT[h][R_, tok0:tok0 + TB], in0=u1[R_, :], in1=u2[R_, :], op=ALU.add),
                 r=[u1, u2], w=[kTb[h][t]])
        ps = proj_group(hb, AB, 2)
        p.op("act", lambda e, ps=ps: e.copy(out=ab_row[:, :], in_=ps[0:2, :]), r=[ps], w=[ab_row])

        for h in range(2):
            psA = misc.next()
            for kc in range(2):
                mm(psA[0:96, :], wq[:, kc, h * 128:h * 128 + 96], cqT[:, kc, :], kc == 0, kc == 1, [wq, cqT], [psA])
            p.op("dve", lambda e, h=h, ps=psA: e.tensor_tensor(out=qT[h][0:64, :], in0=ps[0:64, :], in1=rstd_q[0:64, :], op=ALU.mult),
                 r=[psA, rstd_q], w=[qT[h]])
            p.op("dve", lambda e, ps=psA: e.tensor_tensor(out=u1[R_, :], in0=ps[R_, :], in1=CS[R_, 1, :], op=ALU.mult), r=[psA, CS], w=[u1])
            psB = misc.next()
            for kc in range(2):
                mm(psB[0:96, :], wq[:, kc, h * 128 + 32:h * 128 + 128], cqT[:, kc, :], kc == 0, kc == 1, [wq, cqT], [psB])
            p.op("dve", lambda e, ps=psB: e.tensor_tensor(out=u2[R_, :], in0=ps[R_, :], in1=CS[R_, 0, :], op=ALU.mult), r=[psB, CS], w=[u2])
            p.op("pool", lambda e: e.tensor_tensor(out=u1[R_, :], in0=u1[R_, :], in1=u2[R_, :], op=ALU.add), r=[u1, u2], w=[u1])
            p.op("pool", lambda e, h=h: e.tensor_tensor(out=qT[h][R_, :], in0=u1[R_, :], in1=rstd_q[R_, :], op=ALU.mult),
                 r=[u1, rstd_q], w=[qT[h]])
            psK = misc.next()
            mm(psK[0:64, :], wkv[:, h * 64:(h + 1) * 64], ckvT[:, :], True, True, [wkv, ckvT], [psK])
            p.op("dve", lambda e, h=h, ps=psK: e.tensor_tensor(out=kT[h][0:64, tok0:tok0 + TB], in0=ps[0:64, :], in1=rstd_kv[0:64, :],
                                                               op=ALU.mult), r=[psK, rstd_kv], w=[kTb[h][t]])
        psV = misc.next()
        for j in range(4):
            mm(psV[:, j * 128:(j + 1) * 128], ckvT[:, j * 128:(j + 1) * 128], wkv[:, 128:256], j == 0, j == 3, [ckvT, wkv], [psV])
        for h in range(2):
            p.op("dve", lambda e, h=h, ps=psV: e.tensor_tensor(
                out=vaug[h][:, 4 * t:4 * t + 4, 0:64],
                in0=ps[:, :].rearrange("q (j c) -> q j c", j=4)[:, :, h * 64:(h + 1) * 64],
                in1=rkv_col[:, :].unsqueeze(2).to_broadcast([128, 4, 64]), op=ALU.mult), r=[psV, rkv_col], w=[vab[h][t]])

        def gdn_prep_gen():
            for s in range(3):
                cacc = caccs[s]
                p.op("dve", lambda e, s=s, cacc=cacc: e.tensor_scalar(out=cacc[:, :], in0=raw[s][:, 0:TB], scalar1=convw[:, s, 0:1],
                                                                      scalar2=None, op0=ALU.mult), r=[raw[s], convw], w=[cacc])
                for k in range(1, 4):
                    p.op("dve", lambda e, s=s, k=k, cacc=cacc: e.scalar_tensor_tensor(
                        out=cacc[:, :], in0=raw[s][:, k:k + TB], scalar=convw[:, s, k:k + 1], in1=cacc[:, :], op0=ALU.mult, op1=ALU.add),
                        r=[raw[s], convw, cacc], w=[cacc])
            p.op("act", lambda e: e.activation(out=gate[:, :], in_=gate[:, :], func=AF.Silu), r=[gate], w=[gate])
            p.op("act", lambda e: e.activation(out=vTg[:, :], in_=caccs[2][:, :], func=AF.Silu), r=[caccs[2]], w=[vTg])
            for s in range(2):
                p.op("act", lambda e, s=s: e.activation(out=caccs[s][:, :], in_=caccs[s][:, :], func=AF.Silu), r=[caccs[s]], w=[caccs[s]])
            yield
            for s, dst in ((0, qTg), (1, kTg)):
                sil = caccs[s]
                p.op("act", lambda e, sil=sil, s=s: e.activation(out=sqb[:, s, :], in_=sil[:, :], func=AF.Square), r=[sil], w=[sqb])
                yield
                ps = misc.next()
                mm(ps[:, :], ones_bf[:, :], sqb[:, s, :], True, True, [ones_bf, sqb], [ps])
                if s == 0:
                    rsqrt_act(p, rn[:, :], ps[:, :], 128.0, c_eps128[:, 0:1], rtmp[:, :], [ps, c_eps128], [rn], rtmp)
                else:
                    rsqrt_act(p, rn[:, :], ps[:, :], 1.0, p.eps_tile[:, 0:1], rtmp[:, :], [ps], [rn], rtmp)
                p.op("dve", lambda e, dst=dst, sil=sil: e.tensor_tensor(out=dst[:, :], in0=sil[:, :], in1=rn[:, :], op=ALU.mult),
                     r=[sil, rn], w=[dst])
            for src, dst in ((kTg, kTok), (vTg, vTok)):
                yield
                ps = misc.next()
                for c in range(8):
                    p.op("pe", lambda e, c=c, ps=ps, src=src: e.transpose(out=bfv(ps)[0:64, c * 128:(c + 1) * 128],
                                                                          in_=src[:, c * 64:(c + 1) * 64], identity=cm.ident[:, :]),
                         r=[src, cm.ident], w=[ps])
                p.op("act", lambda e, ps=ps, dst=dst: e.copy(out=dst[:, :, :], in_=bfv(ps)[0:64, :].rearrange("q (c d) -> q c d", c=8)),
                     r=[ps], w=[dst])
            yield
            ps = misc.next()
            for c in range(8):
                p.op("pe", lambda e, c=c, ps=ps: e.transpose(out=ps[0:64, 2 * c:2 * c + 2], in_=ab_row[0:2, c * 64:(c + 1) * 64],
                                                             identity=cm.identf[0:2, 0:2]), r=[ab_row, cm.identf], w=[ps])
            p.op("act", lambda e, ps=ps: e.copy(out=ab_col[:, :, :], in_=ps[0:64, 0:16].rearrange("q (c two) -> q c two", two=2)),
                 r=[ps], w=[ab_col])
            p.op("act", lambda e: e.activation(out=nbeta_col[:, :], in_=ab_col[:, :, 1], func=AF.Exp, scale=-1.0), r=[ab_col], w=[nbeta_col])
            p.op("dve", lambda e: e.tensor_scalar(out=nbeta_col[:, :], in0=nbeta_col[:, :], scalar1=1.0, scalar2=None, op0=ALU.add),
                 r=[nbeta_col], w=[nbeta_col])
            p.op("dve", lambda e: e.reciprocal(out=beta_col[:, :], in_=nbeta_col[:, :]), r=[nbeta_col], w=[beta_col])
            p.op("dve", lambda e: e.tensor_scalar(out=nbeta_col[:, :], in0=beta_col[:, :], scalar1=-1.0, scalar2=None, op0=ALU.mult),
                 r=[beta_col], w=[nbeta_col])
            p.op("act", lambda e: e.activation(out=sp_t[:, :], in_=ab_col[:, :, 0], func=AF.Exp, bias=dtb[:, 0:1]), r=[ab_col, dtb], w=[sp_t])
            p.op("act", lambda e: e.activation(out=sp_t[:, :], in_=sp_t[:, :], func=AF.Ln, bias=c_one[0:64, 0:1]), r=[sp_t, c_one], w=[sp_t])
            p.op("dve", lambda e: e.tensor_scalar(out=la_col[:, :], in0=sp_t[:, :], scalar1=nA[:, 0:1], scalar2=None, op0=ALU.mult),
                 r=[sp_t, nA], w=[la_col])
            p.op("dve", lambda e: e.tensor_copy(out=X1[:, :, :], in_=la_col[:, :].unsqueeze(2).to_broadcast([64, 8, 128])), r=[la_col], w=[X1])
            p.op("dve", lambda e: e.tensor_scalar(out=sp_t[:, :], in0=la_col[:, :], scalar1=-1.0, scalar2=None, op0=ALU.mult),
                 r=[la_col], w=[sp_t])
            p.op("pool", lambda e: e.tensor_tensor(out=X2[:, :, :], in0=b8(U), in1=sp_t[:, :].unsqueeze(2).to_broadcast([64, 8, 64]),
                                                   op=ALU.mult), r=[U, sp_t], w=[X2])
            yield
            ps = misc.next()
            mm(ps[0:64, 0:8], U[:, :], la_col[:, :], True, True, [U, la_col], [ps])
            mm(ps[0:64, 8:16], Ust[:, :], la_col[:, :], False, True, [Ust, la_col], [ps])
            p.op("act", lambda e, ps=ps: e.activation(out=eg_col[:, :], in_=ps[0:64, 0:16], func=AF.Exp), r=[ps], w=[eg_col])
            yield
            ps = misc.next()
            for c in range(8):
                mm(ps[0:64, c * 64:(c + 1) * 64], X1[:, c, 0:64], U[:, :], c == 0, False, [X1, U], [ps])
                mm(ps[0:64, c * 64:(c + 1) * 64], X2[:, c, :], ones64[:, :], False, True, [X2, ones64], [ps])
            p.op("dve", lambda e, ps=ps: e.tensor_tensor(out=tmpA[:, :, :], in0=ps[0:64, :].rearrange("q (c i) -> q c i", c=8),
                                                         in1=b8(Mneg), op=ALU.add), r=[ps, Mneg], w=[tmpA])
            p.op("act", lambda e: e.activation(out=DT[:, :, :], in_=tmpA[:, :, :], func=AF.Exp), r=[tmpA], w=[DT])
            p.op("pool", lambda e: e.tensor_tensor(out=DST[:, :, :], in0=DT[:, :, :], in1=b8(S01), op=ALU.mult), r=[DT, S01], w=[DST])
            yield
            ps = misc.next()
            for c in range(8):
                mm(ps[:, c * 64:(c + 1) * 64], X1[:, c, :], U[:, :], c == 0, c == 7, [X1, U], [ps])
            p.op("act", lambda e, ps=ps: e.activation(out=EGB[:, :, :], in_=ps[:, :].rearrange("q (c i) -> q c i", c=8), func=AF.Exp),
                 r=[ps], w=[EGB])
            p.op("dve", lambda e: e.tensor_tensor(out=qdecT[:, :], in0=qTg[:, :], in1=EGB[:, :, :].rearrange("q c i -> q (c i)"), op=ALU.mult),
                 r=[qTg, EGB], w=[qdecT])
            yield
            ps = misc.next()
            for c in range(8):
                mm(ps[0:64, c * 64:(c + 1) * 64], kTg[:, c * 64:(c + 1) * 64], kTg[:, c * 64:(c + 1) * 64], c == 0, c == 7, [kTg], [ps])
            Y = chY[0]
            p.op("dve", lambda e, ps=ps: e.tensor_tensor(out=tmpA[:, :, :], in0=ps[0:64, :].rearrange("q (c i) -> q c i", c=8),
                                                         in1=DST[:, :, :], op=ALU.mult), r=[ps, DST], w=[tmpA])
            p.op("pool", lambda e, Y=Y: e.tensor_tensor(out=Y[:, :, :], in0=tmpA[:, :, :],
                                                        in1=nbeta_col[:, :].unsqueeze(2).to_broadcast([64, 8, 64]), op=ALU.mult),
                 r=[tmpA, nbeta_col], w=[Y])
            yield
            ps = misc.next()
            for c in range(8):
                mm(ps[0:64, c * 64:(c + 1) * 64], kTg[:, c * 64:(c + 1) * 64], qTg[:, c * 64:(c + 1) * 64], c == 0, c == 7, [kTg, qTg], [ps])
            p.op("dve", lambda e, ps=ps: e.tensor_tensor(out=PT[:, :, :], in0=ps[0:64, :].rearrange("q (c i) -> q c i", c=8),
                                                         in1=DT[:, :, :], op=ALU.mult), r=[ps, DT], w=[PT])
            X = chX[0]
            yield
            ps = misc.next()
            for c in range(8):
                p.op("pe", lambda e, c=c, ps=ps, Y=Y: e.transpose(out=ps[0:64, c * 64:(c + 1) * 64], in_=Y[:, c, :],
                                                                  identity=cm.identf[0:64, 0:64]), r=[Y, cm.identf], w=[ps])
            p.op("act", lambda e, ps=ps, X=X: e.copy(out=X[:, :, :], in_=ps[0:64, :].rearrange("q (c i) -> q c i", c=8)), r=[ps], w=[X])
            TT = chT[0]
            p.op("pool", lambda e, TT=TT, Y=Y: e.tensor_tensor(out=TT[:, :, :], in0=Y[:, :, :], in1=b8(cm.identf), op=ALU.add), r=[Y, cm.identf], w=[TT])
            for lvl in range(5):
                Xn, Yn, Tn = chX[(lvl + 1) % 2], chY[(lvl + 1) % 2], chT[(lvl + 1) % 2]
                yield
                ps = misc.next()
                for c in range(8):
                    mm(ps[0:64, c * 64:(c + 1) * 64], Y[:, c, :], X[:, c, :], c == 0, c == 7, [X, Y], [ps])
                p.op("act", lambda e, ps=ps, Xn=Xn: e.copy(out=Xn[:, :, :], in_=ps[0:64, :].rearrange("q (c i) -> q c i", c=8)), r=[ps], w=[Xn])
                if lvl < 4:
                    yield
                    ps = misc.next()
                    for c in range(8):
                        mm(ps[0:64, c * 64:(c + 1) * 64], X[:, c, :], Y[:, c, :], c == 0, c == 7, [X, Y], [ps])
                    p.op("dve", lambda e, ps=ps, Yn=Yn: e.tensor_copy(out=Yn[:, :, :], in_=ps[0:64, :].rearrange("q (c i) -> q c i", c=8)),
                         r=[ps], w=[Yn])
                yield
                ps = misc.next()
                for c in range(8):
                    mm(ps[0:64, c * 64:(c + 1) * 64], Xn[:, c, :], TT[:, c, :], c == 0, c == 7, [Xn, TT], [ps])
                p.op("dve", lambda e, ps=ps, Tn=Tn, TT=TT: e.tensor_tensor(out=Tn[:, :, :], in0=ps[0:64, :].rearrange("q (c i) -> q c i", c=8),
                                                                           in1=TT[:, :, :], op=ALU.add), r=[ps, TT], w=[Tn])
                X, Y, TT = Xn, Yn, Tn
            p.op("act", lambda e, TT=TT: e.copy(out=TTb[:, :, :], in_=TT[:, :, :]), r=[TT], w=[TTb])
            p.op("pool", lambda e: e.tensor_tensor(out=keg[:, :, :], in0=kTok[:, :, :], in1=eg_col[:, 0:8].unsqueeze(2).to_broadcast([64, 8, 128]),
                                                   op=ALU.mult), r=[kTok, eg_col], w=[keg])
            p.op("pool", lambda e: e.tensor_tensor(out=kdec[:, :, :], in0=kTok[:, :, :], in1=eg_col[:, 8:16].unsqueeze(2).to_broadcast([64, 8, 128]),
                                                   op=ALU.mult), r=[kTok, eg_col], w=[kdec])
            for half in range(2):
                yield
                ps = misc.next()
                for cc in range(4):
                    c = half * 4 + cc
                    mm(ps[0:64, cc * 128:(cc + 1) * 128], TTb[:, c, :], vTok[:, c, :], cc == 0, cc == 3, [TTb, vTok], [ps])
                p.op("dve", lambda e, ps=ps, half=half: e.tensor_tensor(
                    out=bu[:, half * 4:half * 4 + 4, :], in0=ps[0:64, :].rearrange("q (c d) -> q c d", c=4),
                    in1=beta_col[:, half * 4:half * 4 + 4].unsqueeze(2).to_broadcast([64, 4, 128]), op=ALU.mult), r=[ps, beta_col], w=[bu])
            yield
            ps = misc.next()
            for c in range(8):
                mm(ps[:, c * 64:(c + 1) * 64], keg[:, c, :], TTb[:, c, :], c == 0, c == 7, [keg, TTb], [ps])
            p.op("act", lambda e, ps=ps: e.copy(out=wT[:, :, :], in_=ps[:, :].rearrange("q (c i) -> q c i", c=8)), r=[ps], w=[wT])

        def gdn_a(c):
            mm(bR[0:64, 0:128], wT[:, c, :], Sb[:, :], True, True, [wT, Sb], [bR])
            vn = vnew.next()
            p.op("dve", lambda e: e.scalar_tensor_tensor(out=vn[:, :], in0=bR[0:64, 0:128], scalar=nbeta_col[:, c:c + 1], in1=bu[:, c, :],
                                                         op0=ALU.mult, op1=ALU.add), r=[bR, nbeta_col, bu], w=[vn])
            return vn

        def gdn_b(c, vn):
            cc = c % 4
            mm(bO[0:64, cc * 128:(cc + 1) * 128], qdecT[:, c * 64:(c + 1) * 64], Sb[:, :], cc == 0, False, [qdecT, Sb], [bO])
            mm(bO[0:64, cc * 128:(cc + 1) * 128], PT[:, c, :], vn[:, :], False, True, [PT, vn], [bO])
            mm(bR[:, 128:256], kdec[:, c, :], vn[:, :], True, True, [kdec, vn], [bR])
            p.op("dve", lambda e: e.scalar_tensor_tensor(out=Sb[:, :], in0=Sst[:, :], scalar=EGB[:, c, 63:64], in1=bR[:, 128:256],
                                                         op0=ALU.mult, op1=ALU.add), r=[Sst, EGB, bR], w=[Sb])
            p.op("dve", lambda e: e.scalar_tensor_tensor(out=Sst[:, :], in0=Sst[:, :], scalar=EGB[:, c, 63:64], in1=bR[:, 128:256],
                                                         op0=ALU.mult, op1=ALU.add), r=[Sst, EGB, bR], w=[Sst])
            if cc == 3:
                gdn_out(c - 3)

        def gdn_out(c0):
            o3 = bO[0:64, :].rearrange("q (c d) -> q c d", c=4)
            p.op("act", lambda e: e.activation(out=osq[:, :, :], in_=o3, func=AF.Square), r=[bO], w=[osq])
            p.op("dve", lambda e: e.tensor_reduce(out=oss[:, :], in_=osq[:, :, :], axis=AX.X, op=ALU.add), r=[osq], w=[oss])
            rsqrt_act(p, ors[:, :], oss[:, :], 1.0 / 128, p.eps_tile[0:64, 0:1], otmp[:, :], [oss], [ors], otmp)
            p.op("dve", lambda e: e.tensor_tensor(out=on1[:, :, :], in0=o3, in1=ors[:, :].unsqueeze(2).to_broadcast([64, 4, 128]), op=ALU.mult),
                 r=[bO, ors], w=[on1])
            p.op("pool", lambda e: e.tensor_tensor(out=on2[:, :, :], in0=on1[:, :, :], in1=gnorm[:, :].unsqueeze(1).to_broadcast([64, 4, 128]),
                                                   op=ALU.mult), r=[on1, gnorm], w=[on2])
            ps = misc.next()
            for cc in range(4):
                p.op("pe", lambda e, cc=cc, ps=ps: e.transpose(out=bfv(ps)[:, cc * 64:(cc + 1) * 64], in_=on2[:, cc, :],
                                                               identity=cm.ident[0:64, 0:64]), r=[on2, cm.ident], w=[ps])
            p.op("dve", lambda e, ps=ps: e.tensor_tensor(out=oTg[:, c0 * 64:c0 * 64 + 256], in0=bfv(ps)[:, 0:256],
                                                         in1=gate[:, c0 * 64:c0 * 64 + 256], op=ALU.mult), r=[ps, gate], w=[oTg])

        def attn_unit(h, kb):
            r_ = kb - 4 * t
            q0 = max(0, r_) * 128
            nq = TB - q0
            ps = psS.next()
            mm(ps[:, 0:nq], kT[h][:, kb * 128:(kb + 1) * 128], qT[h][:, q0:TB], True, True, [kTb[h][kb // 4], qT[h]], [ps])
            pt = PTa.next()
            p.op("act", lambda e: e.activation(out=pt[:, 0:nq], in_=ps[:, 0:nq], func=AF.Exp, scale=SCALE), r=[ps], w=[pt])
            if r_ >= 0:
                p.op("pool", lambda e: e.memset(pt[64:128, 0:64], 0.0), r=[], w=[pt])
            return (h, kb, q0, pt)

        def attn_unit_pv(st):
            h, kb, q0, pt = st
            for qs in range(q0 // 128, 4):
                first = (kb == 0 and qs == 0)
                last = (kb == 4 * t + qs)
                mm(psO[h][:, qs * 65:(qs + 1) * 65], pt[:, qs * 128 - q0:(qs + 1) * 128 - q0], vaug[h][:, kb, :], first, last,
                   [pt, vab[h][kb // 4]], [psO[h]])

        def attn_out():
            for h in range(2):
                o3 = psO[h][:, 0:260].rearrange("q (s c) -> q s c", s=4)
                p.op("dve", lambda e, o3=o3: e.reciprocal(out=rden[:, :], in_=o3[:, :, 64]), r=[psO[h]], w=[rden])
                p.op("dve", lambda e, o3=o3, h=h: e.tensor_tensor(out=on_tok[:, :, h * 64:(h + 1) * 64], in0=o3[:, :, 0:64],
                                                                  in1=rden[:, :].unsqueeze(2).to_broadcast([128, 4, 64]), op=ALU.mult),
                     r=[psO[h], rden], w=[on_tok])
            ps = misc.next()
            for qs in range(4):
                p.op("pe", lambda e, qs=qs, ps=ps: e.transpose(out=bfv(ps)[:, qs * 128:(qs + 1) * 128], in_=on_tok[:, qs, :],
                                                               identity=cm.ident[:, :]), r=[on_tok, cm.ident], w=[ps])
            p.op("act", lambda e, ps=ps: e.copy(out=oTm[:, :], in_=bfv(ps)[:, 0:512]), r=[ps], w=[oTm])
            p.dma(oT_own.ap.ap()[128:256, tok0:tok0 + TB], oTm[:, :], r=[oTm], w=[oT_own])

        units = [(h, kb) for h in range(2) for kb in range(4 * t + 4)]
        nu = len(units)

        def gdn_all():
            yield from gdn_prep_gen()
            vn_cur = {}
            for c in range(8):
                yield
                vn_cur[c] = gdn_a(c)
                yield
                gdn_b(c, vn_cur[c])

        NPIECES = 50
        LAG = 2
        ui = 0
        npc = 0
        pend = []

        def emit_unit(i):
            pend.append(attn_unit(*units[i]))
            if len(pend) > LAG:
                attn_unit_pv(pend.pop(0))

        for _ in gdn_all():
            npc += 1
            target = min(nu, npc * nu // NPIECES)
            while ui < target:
                emit_unit(ui)
                ui += 1
        assert npc <= NPIECES, npc
        while ui < nu:
            emit_unit(ui)
            ui += 1
        while pend:
            attn_unit_pv(pend.pop(0))
        p.dma(oT_own.ap.ap()[0:128, tok0:tok0 + TB], oTg[:, :], r=[oTg], w=[oT_own])
        attn_out()
    p.barrier()
    p.release(mk)


def core_weight_slices(inp, r):
    f32 = np.float32
    w_in = inp["w_in"]
    L = w_in.shape[0]
    o_q, o_k, o_v, o_g, o_a, o_b, o_cq, o_ckv, o_kr = 0, 512, 1024, 1536, 2048, 2052, 2056, 2312, 2440
    hs = slice(r * 128, (r + 1) * 128)
    kr = w_in[:, :, o_kr:o_kr + 32]
    krot = np.concatenate([kr[:, :, 16:32], kr[:, :, 0:16]], axis=-1)
    w_in_own = np.concatenate([
        w_in[:, :, o_q:o_q + 512][:, :, hs], w_in[:, :, o_k:o_k + 512][:, :, hs], w_in[:, :, o_v:o_v + 512][:, :, hs],
        w_in[:, :, o_g:o_g + 512][:, :, hs], w_in[:, :, o_cq:o_cq + 256], w_in[:, :, o_ckv:o_ckv + 128], kr, krot,
        w_in[:, :, o_a + r:o_a + r + 1], w_in[:, :, o_b + r:o_b + r + 1]], axis=-1)
    cw = inp["conv_w"]
    conv_own = np.stack([cw[:, :, s * 512 + r * 128: s * 512 + (r + 1) * 128] for s in range(3)], axis=1)
    conv_own = np.ascontiguousarray(np.transpose(conv_own, (0, 3, 1, 2)))
    wq = inp["w_q_up"]
    parts = []
    for hh in range(2):
        base = (2 * r + hh) * 96
        nope = wq[:, :, base:base + 64]
        rope = wq[:, :, base + 64:base + 96]
        rot = np.concatenate([rope[:, :, 16:32], rope[:, :, 0:16]], axis=-1)
        parts += [nope, rope, rot]
    wq_own = np.concatenate(parts, axis=-1)
    wkv = inp["w_kv_up"]
    b0, b1 = (2 * r) * 128, (2 * r + 1) * 128
    wkv_own = np.concatenate([wkv[:, :, b0:b0 + 64], wkv[:, :, b1:b1 + 64], wkv[:, :, b0 + 64:b0 + 128], wkv[:, :, b1 + 64:b1 + 128]], axis=-1)
    return {
        "w_in_own": np.ascontiguousarray(w_in_own, dtype=f32),
        "conv_own": np.ascontiguousarray(conv_own, dtype=f32),
        "alog_own": np.ascontiguousarray(inp["a_log"][:, r:r + 1], dtype=f32),
        "dtb_own": np.ascontiguousarray(inp["dt_bias"][:, r:r + 1], dtype=f32),
        "wq_own": np.ascontiguousarray(wq_own, dtype=f32),
        "wkv_own": np.ascontiguousarray(wkv_own, dtype=f32),
    }


def rope_freq_const():
    inv = (np.float32(10000.0) ** (-np.arange(0, 32, 2, dtype=np.float32) / np.float32(32))).astype(np.float32)
    t = np.zeros((96, 1), np.float32)
    t[64:80, 0] = -inv / np.float32(2 * np.pi)
    t[80:96, 0] = inv / np.float32(2 * np.pi)
    return t


MIX_W_SHAPES = {"w_in_own": [2, 1024, NCOL], "conv_own": [2, 128, 3, 4], "alog_own": [2, 1], "dtb_own": [2, 1],
                "wq_own": [2, 256, 256], "wkv_own": [2, 128, 256], "invf": [96, 1],
                "gdn_out_norm": [2, 128], "q_norm": [2, 256], "kv_norm": [2, 128]}


TOK_W = {"mix_post_norm": [2, D], "ffn_pre_norm": [2, D], "ffn_post_norm": [2, D], "mix_pre_norm": [2, D],
         "w_out": [2, D, D], "w_up": [2, D, DFF], "w_down": [2, DFF, D]}


def wout_rowmap(kk):
    return (kk // 2) * 128 if kk % 2 == 0 else 512 + (kk // 2) * 128


def build_norm_prog(NT):
    nc = bass.Bass("TRN2", target_bir_lowering=False)
    p = Prog(nc)
    W = {k: p.dram(k, TOK_W[k], F32, kind="ExternalInput") for k in ("mix_pre_norm",)}
    x_in = p.dram("x_in", [NT, D], F32, kind="ExternalInput")
    x_dummy = p.dram("x_dummy", [NT, D], F32)
    hT_out = p.dram("hT_out", [D, NT], BF16, kind="ExternalOutput")
    cm = setup_common(p)
    phase_tok(p, cm, 0, NT, W, x_in, None, x_dummy, hT_out, do_mix=False, do_ffn=False, gain_next=W["mix_pre_norm"][0, :])
    p.op("sp", lambda e: e.nop(), r=[hT_out])
    p.emit()
    p.close()
    return nc


def build_mix_prog(l, S):
    nc = bass.Bass("TRN2", target_bir_lowering=False)
    p = Prog(nc)
    Wc = {k: p.dram(k, shp, F32, kind="ExternalInput") for k, shp in MIX_W_SHAPES.items()}
    hT = p.dram("hT", [D, S], BF16, kind="ExternalInput")
    pos = p.dram("pos", [S], I32, kind="ExternalInput")
    oT = p.dram("oT_own", [256, S], BF16, kind="ExternalOutput")
    cm = setup_common(p)
    hv = hT.ap.ap().rearrange("(k q) t -> q k t", q=128)
    phase_mix(p, cm, l, S, Wc, lambda tok0: hv[:, :, tok0:tok0 + 512], pos, oT)
    p.op("sp", lambda e: e.nop(), r=[oT])
    p.emit()
    p.close()
    return nc


def build_tok_prog(l, NT, last):
    nc = bass.Bass("TRN2", target_bir_lowering=False)
    p = Prog(nc)
    W = {k: p.dram(k, shp, F32, kind="ExternalInput") for k, shp in TOK_W.items()}
    x_in = p.dram("x_in", [NT, D], F32, kind="ExternalInput")
    oT = p.dram("oT", [D, NT], BF16, kind="ExternalInput")
    x_out = p.dram("x_out", [NT, D], F32, kind="ExternalOutput")
    outs = [x_out]
    hT_out = None
    if not last:
        hT_out = p.dram("hT_out", [D, NT], BF16, kind="ExternalOutput")
        outs.append(hT_out)
    cm = setup_common(p)
    phase_tok(p, cm, l, NT, W, x_in, oT, x_out, hT_out, gain_next=(None if last else W["mix_pre_norm"][l + 1, :]),
              wout_rowmap=wout_rowmap)
    p.op("sp", lambda e: e.nop(), r=outs)
    p.emit()
    p.close()
    return nc


class HTSrc:
    def __init__(self, hT_sel, trackers, NT):
        self.view = hT_sel.ap.ap().rearrange("(r k q) t -> r q k t", r=4, q=128)
        self.trackers = trackers
        self.NT = NT

    def load(self, p, hb, tok0):
        rr, c0 = tok0 // self.NT, tok0 % self.NT
        p.dma(hb[:, :, :], self.view[rr][:, :, c0:c0 + 512], r=[self.trackers[rr]], w=[hb])


def build_fused(S, NT):
    nc = bass.Bass("TRN2", target_bir_lowering=False, num_devices=8)
    p = Prog(nc)
    W = {k: p.dram(k, shp, F32, kind="ExternalInput") for k, shp in TOK_W.items()}
    Wc = {k: p.dram(k, shp, F32, kind="ExternalInput") for k, shp in MIX_W_SHAPES.items()}
    pos = p.dram("pos", [S], I32, kind="ExternalInput")
    x_in = p.dram("x_in", [NT, D], F32, kind="ExternalInput")
    meta = p.dram("meta", [1, 2], I32, kind="ExternalInput")
    out = p.dram("out", [NT, D], F32, kind="ExternalOutput")
    hT_own = [p.dram(f"hT_own{l}", [D, NT], BF16) for l in range(2)]
    hT_all = [p.dram(f"hT_all{l}", [8 * D, NT], BF16) for l in range(2)]
    oT_own = [p.dram(f"oT_own{l}", [256, S], BF16) for l in range(2)]
    oT_all = [p.dram(f"oT_all{l}", [8 * 256, S], BF16) for l in range(2)]
    hT_sel = [p.dram(f"hT_sel{l}", [4 * D, NT], BF16) for l in range(2)]
    hT_sel_tr = [[T(hT_sel[l].ap, f"hT_sel{l}_{rr}") for rr in range(4)] for l in range(2)]
    oT_sel = [p.dram(f"oT_sel{l}", [D, NT], BF16) for l in range(2)]
    x1 = p.dram("x1", [NT, D], F32)
    x_dummy = p.dram("x_dummy", [NT, D], F32)
    groups = [list(range(8))]
    cm = setup_common(p)
    meta_sb = p.tile("meta_sb", [1, 2], I32)
    p.dma(meta_sb[:, :], meta.ap.ap(), w=[meta_sb])
    p.load_meta_regs(meta_sb, 2)

    wbf = [{"w_out": p.dram(f"wob{l}", [D, D], BF16), "w_up": p.dram(f"wub{l}", [D, DFF], BF16),
            "w_down": p.dram(f"wdb{l}", [DFF, D], BF16)} for l in range(2)]

    def convert_weights(l):
        for name, rows in (("w_out", D), ("w_up", D), ("w_down", DFF)):
            src = W[name].ap.ap()[l]
            dst = wbf[l][name]
            for r0 in range(0, rows, 256):
                p.dma(dst.ap.ap()[r0:r0 + 256, :], src[r0:r0 + 256, :], w=[dst], eng="pool")

    def gather_hT(l):
        p.all_gather(hT_own[l], hT_all[l], groups)
        for rr in range(4):
            p.dma_dyn(hT_sel[l].ap.ap()[rr * D:(rr + 1) * D, :], hT_all[l], 0, rr * D * NT, [[NT, D], [1, NT]],
                      w=[hT_sel_tr[l][rr]])

    def gather_oT(l):
        p.all_gather(oT_own[l], oT_all[l], groups)
        p.dma_dyn(oT_sel[l].ap.ap(), oT_all[l], 1, 0, [[S, D], [1, NT]], w=[oT_sel[l]])

    convert_weights(0)
    phase_tok(p, cm, 0, NT, W, x_in, None, x_dummy, hT_own[0], do_mix=False, do_ffn=False, gain_next=W["mix_pre_norm"][0, :])
    gather_hT(0)
    for l in range(2):
        if l == 1:
            convert_weights(1)
        phase_mix(p, cm, l, S, Wc, HTSrc(hT_sel[l], hT_sel_tr[l], NT), pos, oT_own[l])
        gather_oT(l)
        last = (l == 1)
        phase_tok(p, cm, l, NT, W, x_in if l == 0 else x1, oT_sel[l], out if last else x1, None if last else hT_own[1],
                  gain_next=(None if last else W["mix_pre_norm"][l + 1, :]), wout_rowmap=wout_rowmap, wsrc=wbf[l])
        if not last:
            gather_hT(1)
    p.op("sp", lambda e: e.nop(), r=[out])
    p.emit()
    p.close()
    return nc


def kernel(**inp):
    inp = {k: np.asarray(v) for k, v in inp.items()}
    B, S = inp["x"].shape[0], inp["x"].shape[1]
    NR = 4
    NT = S // NR
    ncore = B * NR
    cores = list(range(ncore))
    x = np.ascontiguousarray(inp["x"], dtype=np.float32)
    tokw = {k: np.ascontiguousarray(inp[k], dtype=np.float32) for k in TOK_W}
    invf = rope_freq_const()
    ims = []
    for c in cores:
        b, r = c // NR, c % NR
        m = dict(tokw)
        m.update(core_weight_slices(inp, r))
        m["invf"] = invf
        for k in ("gdn_out_norm", "q_norm", "kv_norm"):
            m[k] = np.ascontiguousarray(inp[k], dtype=np.float32)
        m["pos"] = np.ascontiguousarray(inp["positions"][b], dtype=np.int32)
        m["x_in"] = np.ascontiguousarray(x[b, r * NT:(r + 1) * NT])
        m["meta"] = np.array([[b * NR * D * NT, b * NR * 256 * S + r * NT]], dtype=np.int32)
        ims.append(m)
    res = run_bass_kernel_spmd(build_fused(S, NT), ims, core_ids=cores)
    out = np.empty_like(x)
    for c in cores:
        b, r = c // NR, c % NR
        out[b, r * NT:(r + 1) * NT] = res.results[c]["out"]
    return out
```

```python
import types
import numpy as np
import concourse.bass as bass
import concourse.mybir as mybir
from concourse.bass_utils import run_bass_kernel_spmd

F32 = mybir.dt.float32
BF16 = mybir.dt.bfloat16
I32 = mybir.dt.int32
AF = mybir.ActivationFunctionType
ALU = mybir.AluOpType
AX = mybir.AxisListType

D = 1024
SEQ = 8192
DFF = 4096
EPS = 1e-6
INW = 2472
ENGS = ("pe", "act", "dve", "pool", "sp")


class T:
    __slots__ = ("ap", "name", "w", "r")

    def __init__(self, ap, name=""):
        self.ap = ap
        self.name = name
        self.w = None
        self.r = []

    def __getitem__(self, idx):
        return self.ap[idx]


class Op:
    __slots__ = ("eng", "fn", "deps", "inc", "sem", "val", "dma", "idx", "waits", "name", "incv")

    def __init__(self, eng, fn, dma, name):
        self.incv = 16
        self.eng = eng
        self.fn = fn
        self.deps = set()
        self.inc = False
        self.sem = None
        self.val = 0
        self.dma = dma
        self.waits = []
        self.name = name


def _freeze(fn):
    if fn.__closure__ is None:
        return fn
    cells = []
    for c in fn.__closure__:
        try:
            cells.append(types.CellType(c.cell_contents))
        except ValueError:
            cells.append(c)
    return types.FunctionType(fn.__code__, fn.__globals__, fn.__name__, fn.__defaults__, tuple(cells))


class RR:
    def __init__(self, items):
        self.items = list(items)
        self.i = 0

    def next(self):
        t = self.items[self.i % len(self.items)]
        self.i += 1
        return t


class Prog:
    def __init__(self, nc, n_dma_sems=24):
        self.nc = nc
        self.ops = {e: [] for e in ENGS}
        self.n_dma_sems = n_dma_sems
        self._stack = []
        self.nops = 0
        self.last = {e: None for e in ENGS}
        self.dmas_since_bar = []

    def mark(self):
        return len(self._stack)

    def release(self, mark):
        while len(self._stack) > mark:
            g = self._stack.pop()
            g.__exit__(None, None, None)

    def tile(self, name, shape, dtype, space="sbuf"):
        self.uid = getattr(self, "uid", 0) + 1
        name = f"t{self.uid}_{name}"
        if space == "sbuf":
            g = self.nc.sbuf_tensor(name, list(shape), dtype)
        else:
            g = self.nc.psum_tensor(name, list(shape), dtype)
        h = g.__enter__()
        self._stack.append(g)
        return T(h, name)

    def dram(self, name, shape, dtype, kind="Internal"):
        h = self.nc.dram_tensor(name, list(shape), dtype, kind=kind)
        return T(h, name)

    def op(self, eng, fn, r=(), w=(), dma=False, name="", extra=(), incv=16):
        o = Op(eng, _freeze(fn), dma, name)
        o.incv = incv
        for t in r:
            if t.w is not None:
                o.deps.add(t.w)
        for t in w:
            if t.w is not None:
                o.deps.add(t.w)
            for rd in t.r:
                o.deps.add(rd)
        for d in extra:
            if d is not None:
                o.deps.add(d)
        for t in w:
            t.w = o
            t.r = []
        for t in r:
            if t.w is not o:
                t.r.append(o)
        o.deps.discard(o)
        o.idx = len(self.ops[eng])
        self.ops[eng].append(o)
        self.last[eng] = o
        if dma:
            self.dmas_since_bar.append(o)
        self.nops += 1
        return o

    def dma(self, out_ap, in_ap, r=(), w=(), eng="sp", name="", **kw):
        return self.op(eng, lambda e: e.dma_start(out=out_ap, in_=in_ap, **kw), r=r, w=w, dma=True, name=name)

    def barrier(self):
        pend = list(self.dmas_since_bar)
        self.dmas_since_bar = []
        for e in ENGS:
            mine = [o for o in pend if o.eng == e]
            if mine:
                self.op(e, lambda en: en.nop(), extra=mine, name="bar_dma")
        marks = [self.last[e] for e in ENGS]
        for e in ENGS:
            self.op(e, lambda en: en.nop(), extra=marks, name="bar")

    def load_meta_regs(self, meta_sb, n):
        self.regs = {}

        def fn(e):
            last = None
            for i in range(n):
                g = e.alloc_register(f"meta{i}")
                last = e.reg_load(g, meta_sb[0:1, i:i + 1])
                self.regs[i] = g
            self.regs["scratch"] = e.alloc_register("metas")
            return last
        self.op("pool", fn, r=[meta_sb], name="meta_regs")

    def dma_dyn(self, out_ap, src_t, reg_i, static_off, ap_list, r=(), w=(), name=""):
        def fn(e):
            gs = self.regs["scratch"]
            e.reg_add(gs, self.regs[reg_i], static_off)
            return e.dma_start(out=out_ap, in_=bass.AP(src_t.ap, gs, ap_list))
        return self.op("pool", fn, r=list(r) + [src_t], w=w, dma=True, name=name)

    def all_gather(self, src_t, dst_t, groups):
        return self.op("pool", lambda e: e.collective_compute("AllGather", ALU.bypass, replica_groups=groups,
                                                              ins=[src_t.ap.ap()], outs=[dst_t.ap.ap()]),
                       r=[src_t], w=[dst_t], dma=True, incv=1, name="allgather")

    def emit(self):
        nc = self.nc

        def skip(d, o):
            return d.eng == "pe" and o.eng == "pe" and not d.dma and not o.dma

        for e in ENGS:
            for o in self.ops[e]:
                for d in o.deps:
                    if not skip(d, o):
                        d.inc = True
        eng_sem = {e: nc.alloc_semaphore(name=f"s_{e}") for e in ENGS}
        dma_pool = {}
        dma_state = {}
        for e in ENGS:
            if any(o.dma for o in self.ops[e]):
                dma_pool[e] = [nc.alloc_semaphore(name=f"d_{e}{i}") for i in range(self.n_dma_sems)]
                dma_state[e] = [0, [0] * self.n_dma_sems]
        for e in ENGS:
            cnt = 0
            for o in self.ops[e]:
                if o.dma:
                    st = dma_state[e]
                    i = st[0] % self.n_dma_sems
                    st[0] += 1
                    prev = st[1][i]
                    o.sem = dma_pool[e][i]
                    o.val = prev + o.incv
                    st[1][i] = o.val
                    o.inc = True
                    o.waits.append((o.sem, prev))
                elif o.inc:
                    cnt += 1
                    o.sem = eng_sem[e]
                    o.val = cnt
        for e in ENGS:
            seen = {}
            for o in self.ops[e]:
                need = {}
                for (s, v) in o.waits:
                    if v > 0:
                        need[id(s)] = (s, v)
                if e == "pool" and o.inc and not o.dma and o.val > 1:
                    need[id(eng_sem[e])] = (eng_sem[e], o.val - 1)
                for d in o.deps:
                    if skip(d, o):
                        continue
                    k = id(d.sem)
                    if k not in need or need[k][1] < d.val:
                        need[k] = (d.sem, d.val)
                waits = []
                for k, (s, v) in need.items():
                    if seen.get(k, 0) >= v:
                        continue
                    seen[k] = v
                    waits.append((s, v))
                o.waits = waits

        with nc.Block() as block:
            def mk(e):
                def body(eng):
                    for o in self.ops[e]:
                        for (s, v) in o.waits:
                            eng.wait_ge(s, v)
                        ins = o.fn(eng)
                        if o.inc:
                            ins.then_inc(o.sem, o.incv if o.dma else 1)
                return body
            if self.ops["sp"]:
                block.sync(mk("sp"))
            if self.ops["pe"]:
                block.tensor(mk("pe"))
            if self.ops["act"]:
                block.scalar(mk("act"))
            if self.ops["dve"]:
                block.vector(mk("dve"))
            if self.ops["pool"]:
                block.gpsimd(mk("pool"))

    def close(self):
        self.release(0)


class Common:
    pass


def setup_common(p):
    c = Common()
    idf = p.tile("c_idf", [128, 128], F32)
    p.op("pool", lambda e: e.memset(idf[:, :], 0.0), w=[idf])
    p.op("pool", lambda e: e.affine_select(out=idf[:, :], in_=idf[:, :], pattern=[[-1, 128]],
                                           compare_op=ALU.not_equal, fill=1.0, base=0, channel_multiplier=1),
         r=[idf], w=[idf])
    c.identf = idf
    c.ident = p.tile("c_ident", [128, 128], BF16)
    p.op("dve", lambda e: e.tensor_copy(out=c.ident[:, :], in_=idf[:, :]), r=[idf], w=[c.ident])
    p.eps_tile = p.tile("c_eps", [128, 1], F32)
    p.op("pool", lambda e: e.memset(p.eps_tile[:, :], EPS), w=[p.eps_tile])
    return c


def bcast_load(p, dst, src_ap_1d, n, eng="sp"):
    p.dma(dst[:, :], src_ap_1d.partition_broadcast(128), w=[dst], eng=eng)


def rsqrt_act(p, out_ap, in_ap, scale, bias_ap, tmp_ap, r, w, wtmp):
    p.op("act", lambda e: e.activation(out=tmp_ap, in_=in_ap, func=AF.Ln, bias=bias_ap, scale=scale), r=list(r) + [p.eps_tile], w=[wtmp])
    p.op("act", lambda e: e.activation(out=out_ap, in_=tmp_ap, func=AF.Exp, scale=-0.5), r=[wtmp], w=list(w))


def rstd_from_ss(p, ss, rstd, n, tmp):
    rsqrt_act(p, rstd[:, :], ss[:, :], 1.0 / n, p.eps_tile[:, 0:1], tmp[:, :], [ss], [rstd], tmp)


def norm_to_hT(p, cm, x_ap, gain_t, hT_blk, j, sc, psT):
    xt = sc["xt"]
    p.op("act", lambda e: e.activation(out=sc["junk"][:, :], in_=x_ap, func=AF.Square, accum_out=sc["ss1"][:, 0:1]),
         r=[xt], w=[sc["ss1"]])
    rstd_from_ss(p, sc["ss1"], sc["rs1"], D, sc["tmp1"])
    hb = sc["hb"]
    p.op("dve", lambda e: e.scalar_tensor_tensor(out=hb[:, :], in0=x_ap, scalar=sc["rs1"][:, 0:1], in1=gain_t[:, :],
                                                 op0=ALU.mult, op1=ALU.mult), r=[xt, sc["rs1"], gain_t], w=[hb])
    pt = psT.next()
    for k in range(8):
        p.op("pe", lambda e, k=k: e.transpose(out=pt[:, k * 128:(k + 1) * 128], in_=hb[:, k * 128:(k + 1) * 128],
                                              identity=cm.ident[:, :]), r=[hb, cm.ident], w=[pt])
    p.op("act", lambda e: e.copy(out=hT_blk[:, :, j * 128:(j + 1) * 128],
                                 in_=pt[:, :].rearrange("p (k t) -> p k t", k=8)), r=[pt], w=[hT_blk])


def phase_tok(p, cm, l, NT, W, x_in, oT, x_out, hT_out, do_mix=True, do_ffn=True, gain_next=None, wout_rowmap=None,
              load_oT=None, wsrc=None):
    mk = p.mark()
    TB = 512
    nblk = NT // TB
    psA = RR([p.tile(f"psA{i}", [128, 512], F32, space="psum") for i in range(5)])
    psT = RR([p.tile(f"psT{i}", [128, 1024], BF16, space="psum") for i in range(2)])

    def gain_tile(name, ap1d):
        t = p.tile("g_" + name, [128, D], F32)
        bcast_load(p, t, ap1d, D)
        return t

    if do_mix:
        g_post = gain_tile("post", W["mix_post_norm"][l, :])
        wout = p.tile("wout", [128, 8, D], BF16)
        wo = W["w_out"].ap
        for k in range(8):
            r0 = wout_rowmap(k) if wout_rowmap else k * 128
            if wsrc is not None:
                p.dma(wout[:, k, :], wsrc["w_out"].ap.ap()[r0:r0 + 128, :], r=[wsrc["w_out"]], w=[wout])
            else:
                p.dma(wout[:, k, :], wo[l, r0:r0 + 128, :], w=[wout], eng="pool")
        oTb = p.tile("oTb", [128, 8, TB], BF16)
    if do_ffn:
        g_pre = gain_tile("pre", W["ffn_pre_norm"][l, :])
        g_fpost = gain_tile("fpost", W["ffn_post_norm"][l, :])
        wup_bufs = RR([p.tile(f"wup{i}", [128, 8, 512], BF16) for i in range(3)])
        wdn_bufs = RR([p.tile(f"wdn{i}", [128, 32, 256], BF16) for i in range(2)])
        fT = p.tile("fT", [128, 32, TB], BF16)
        h2T = p.tile("h2T", [128, 8, TB], BF16)
    if gain_next is not None:
        g_next = gain_tile("next", gain_next)
        hTn = RR([p.tile(f"hTn{i}", [128, 8, TB], BF16) for i in range(2)])
    xb = p.tile("xb", [128, 4, D], F32)
    fsb = p.tile("fsb", [128, 4, D], F32)
    ss4 = p.tile("ss4", [128, 4, 4], F32)
    sc = {
        "xt": xb,
        "junk": p.tile("junk", [128, D], BF16),
        "ss1": p.tile("ss1", [128, 1], F32), "rs1": p.tile("rs1", [128, 1], F32), "tmp1": p.tile("tmp1", [128, 1], F32),
        "sst": p.tile("sst", [128, 1], F32),
        "hb": p.tile("hb", [128, D], BF16),
    }
    tt = p.tile("tt", [128, D], F32)
    relu_t = RR([p.tile(f"relu{i}", [128, 512], BF16) for i in range(2)])

    x_in_v = x_in.ap.ap().rearrange("(n j q) d -> n q j d", j=4, q=128)
    x_out_v = x_out.ap.ap().rearrange("(n j q) d -> n q j d", j=4, q=128)

    def evac(ps, j, col0, ncol, slot):
        p.op("act", lambda e: e.activation(out=sc["junk"][:, 0:ncol], in_=ps[:, 0:ncol], func=AF.Square,
                                           accum_out=ss4[:, j, slot:slot + 1]), r=[ps], w=[ss4])
        p.op("act", lambda e: e.copy(out=fsb[:, j, col0:col0 + ncol], in_=ps[:, 0:ncol]), r=[ps], w=[fsb])

    def finish_residual(j, nslot, gain_t):
        p.op("dve", lambda e: e.tensor_reduce(out=sc["sst"][:, :], in_=ss4[:, j, 0:nslot], axis=AX.X, op=ALU.add),
             r=[ss4], w=[sc["sst"]])
        rstd_from_ss(p, sc["sst"], sc["rs1"], D, sc["tmp1"])
        p.op("dve", lambda e: e.scalar_tensor_tensor(out=tt[:, :], in0=fsb[:, j, :], scalar=sc["rs1"][:, 0:1], in1=gain_t[:, :],
                                                     op0=ALU.mult, op1=ALU.mult), r=[fsb, sc["rs1"], gain_t], w=[tt])
        p.op("dve", lambda e: e.tensor_tensor(out=xb[:, j, :], in0=xb[:, j, :], in1=tt[:, :], op=ALU.add),
             r=[tt, xb], w=[xb])

    for n in range(nblk):
        tok0 = n * TB
        p.dma(xb[:, :, :], x_in_v[n], r=[x_in], w=[xb])
        if do_mix:
            if load_oT is not None:
                load_oT(p, oTb, tok0)
            else:
                p.dma(oTb[:, :, :], oT.ap.ap().rearrange("(k q) t -> q k t", q=128)[:, :, tok0:tok0 + TB], r=[oT], w=[oTb])
            for j in range(4):
                for h in range(2):
                    ps = psA.next()
                    for k in range(8):
                        p.op("pe", lambda e, k=k, h=h, ps=ps, j=j: e.matmul(
                            ps[:, :], lhsT=oTb[:, k, j * 128:(j + 1) * 128], rhs=wout[:, k, h * 512:(h + 1) * 512],
                            start=(k == 0), stop=(k == 7)), r=[oTb, wout], w=[ps])
                    evac(ps, j, h * 512, 512, h)
                finish_residual(j, 2, g_post)
        if do_ffn:
            for j in range(4):
                norm_to_hT(p, cm, xb[:, j, :], g_pre, h2T, j, sc, psT)
            if wsrc is not None:
                wu_l, wd_l = wsrc["w_up"].ap.ap(), wsrc["w_down"].ap.ap()
                wq_eng, wr_u, wr_d = "sp", [wsrc["w_up"]], [wsrc["w_down"]]
            else:
                wu_l, wd_l = W["w_up"].ap.ap()[l], W["w_down"].ap.ap()[l]
                wq_eng, wr_u, wr_d = "pool", [], []
            for g in range(8):
                wb = wup_bufs.next()
                p.dma(wb[:, :, :], wu_l.rearrange("(k q) f -> q k f", q=128)[:, :, g * 512:(g + 1) * 512], r=wr_u, w=[wb], eng=wq_eng)
                for cc in range(4):
                    c = g * 4 + cc
                    ps = psA.next()
                    for k in range(8):
                        p.op("pe", lambda e, k=k, cc=cc, ps=ps, wb=wb: e.matmul(
                            ps[:, :], lhsT=wb[:, k, cc * 128:(cc + 1) * 128], rhs=h2T[:, k, :],
                            start=(k == 0), stop=(k == 7)), r=[wb, h2T], w=[ps])
                    rt = relu_t.next()
                    p.op("act", lambda e, ps=ps, rt=rt: e.activation(out=rt[:, :], in_=ps[:, :], func=AF.Relu), r=[ps], w=[rt])
                    p.op("pool", lambda e, c=c, rt=rt: e.tensor_tensor(out=fT[:, c, :], in0=rt[:, :], in1=rt[:, :], op=ALU.mult),
                         r=[rt], w=[fT])
            for qd in range(4):
                wq = wdn_bufs.next()
                p.dma(wq[:, :, :], wd_l.rearrange("(c q) f -> q c f", q=128)[:, :, qd * 256:(qd + 1) * 256], r=wr_d, w=[wq], eng=wq_eng)
                for j in range(4):
                    ps = psA.next()
                    for c in range(32):
                        p.op("pe", lambda e, c=c, j=j, ps=ps, wq=wq: e.matmul(
                            ps[:, 0:256], lhsT=fT[:, c, j * 128:(j + 1) * 128], rhs=wq[:, c, :],
                            start=(c == 0), stop=(c == 31)), r=[fT, wq], w=[ps])
                    evac(ps, j, qd * 256, 256, qd)
            for j in range(4):
                finish_residual(j, 4, g_fpost)
        p.dma(x_out_v[n], xb[:, :, :], r=[xb], w=[x_out])
        if gain_next is not None:
            hb_ = hTn.next()
            for j in range(4):
                norm_to_hT(p, cm, xb[:, j, :], g_next, hb_, j, sc, psT)
            p.dma(hT_out.ap.ap().rearrange("(k q) t -> q k t", q=128)[:, :, tok0:tok0 + TB], hb_[:, :, :], r=[hb_], w=[hT_out])
    p.barrier()
    p.release(mk)


GQ, GK, GV, GG, CQ, CKV, KR, KROT, AB, NCOL = 0, 128, 256, 384, 512, 768, 896, 928, 960, 962
NEG = -1.0e30


def phase_mix(p, cm, l, S, Wc, hT_src, pos, oT_own, dbg=None):
    mk = p.mark()
    TB = 512
    nblk = S // TB
    NKB = S // 128
    banks = [p.tile(f"bank{i}", [128, 512], F32, space="psum") for i in range(8)]
    misc = RR(banks[0:2])
    psS = RR(banks[2:4])
    psO = banks[4:6]
    bR = banks[6]
    bO = banks[7]

    def bfv(bank):
        return bank.ap.bitcast(BF16)

    def t_(name, shape, dt=F32):
        return p.tile(name, shape, dt)

    ones_bf = t_("ones_bf", [128, 128], BF16)
    p.op("pool", lambda e: e.memset(ones_bf[:, :], 1.0), w=[ones_bf])
    ones64 = t_("ones64", [64, 64])
    p.op("pool", lambda e: e.memset(ones64[:, :], 1.0), w=[ones64])

    def mask_tile(name, shape, init, pattern, cmul, cmp_op, fill):
        t = t_(name, shape)
        p.op("pool", lambda e: e.memset(t.ap[:], init), w=[t])
        p.op("pool", lambda e: e.affine_select(out=t.ap[:], in_=t.ap[:], pattern=pattern, compare_op=cmp_op, fill=fill,
                                               base=0, channel_multiplier=cmul), r=[t], w=[t])
        return t

    U = mask_tile("mU", [64, 64], 1.0, [[1, 64]], -1, ALU.is_ge, 0.0)
    Ust = mask_tile("mUst", [64, 64], 1.0, [[-1, 64]], 1, ALU.is_gt, 0.0)
    Mneg = mask_tile("mMneg", [64, 64], 0.0, [[1, 64]], -1, ALU.is_ge, NEG)
    S01 = mask_tile("mS01", [64, 64], 1.0, [[1, 64]], -1, ALU.is_gt, 0.0)

    def b8(m):
        return m.ap[0:64, 0:64].unsqueeze(1).to_broadcast([64, 8, 64])
    c_eps128 = t_("c_eps128", [128, 1])
    p.op("pool", lambda e: e.memset(c_eps128[:, :], 128.0 * EPS), w=[c_eps128])
    c_one = t_("c_one", [128, 1])
    p.op("pool", lambda e: e.memset(c_one[:, :], 1.0), w=[c_one])

    Wown = t_("Wown", [128, 8, NCOL], BF16)
    wsrc = Wc["w_in_own"].ap.ap()[l].rearrange("(k q) c -> q k c", q=128)
    for k0 in range(0, 8, 2):
        p.dma(Wown[:, k0:k0 + 2, :], wsrc[:, k0:k0 + 2, :], w=[Wown], eng="pool")
    convw = t_("convw", [128, 3, 4])
    p.dma(convw[:, :, :], Wc["conv_own"].ap.ap()[l], w=[convw])
    gnorm = t_("gnorm", [64, 128])
    p.dma(gnorm[:, :], Wc["gdn_out_norm"].ap.ap()[l].partition_broadcast(64), w=[gnorm])
    nA = t_("nA", [64, 1])
    p.dma(nA[:, :], Wc["alog_own"].ap.ap()[l].partition_broadcast(64), w=[nA])
    p.op("act", lambda e: e.activation(out=nA[:, :], in_=nA[:, :], func=AF.Exp), r=[nA], w=[nA])
    p.op("dve", lambda e: e.tensor_scalar(out=nA[:, :], in0=nA[:, :], scalar1=-1.0, scalar2=None, op0=ALU.mult), r=[nA], w=[nA])
    dtb = t_("dtb", [64, 1])
    p.dma(dtb[:, :], Wc["dtb_own"].ap.ap()[l].partition_broadcast(64), w=[dtb])
    invf = t_("invf", [96, 1])
    p.dma(invf[:, :], Wc["invf"].ap.ap(), w=[invf])
    wq_f = t_("wq_f", [128, 2, 256])
    p.dma(wq_f[:, :, :], Wc["wq_own"].ap.ap()[l].rearrange("(k q) c -> q k c", q=128), w=[wq_f])
    qg = t_("qg", [128, 2])
    p.dma(qg[:, :], Wc["q_norm"].ap.ap()[l].rearrange("(k q) -> q k", q=128), w=[qg], allow_slow_non_contiguous=True)
    wq = t_("wq", [128, 2, 256], BF16)
    p.op("dve", lambda e: e.tensor_tensor(out=wq[:, :, :], in0=wq_f[:, :, :], in1=qg[:, :].unsqueeze(2).to_broadcast([128, 2, 256]),
                                          op=ALU.mult), r=[wq_f, qg], w=[wq])
    wkv_f = t_("wkv_f", [128, 256])
    p.dma(wkv_f[:, :], Wc["wkv_own"].ap.ap()[l], w=[wkv_f])
    kvg = t_("kvg", [128, 1])
    p.dma(kvg[:, :], Wc["kv_norm"].ap.ap()[l].rearrange("(q o) -> q o", o=1), w=[kvg])
    wkv = t_("wkv", [128, 256], BF16)
    p.op("dve", lambda e: e.tensor_scalar(out=wkv[:, :], in0=wkv_f[:, :], scalar1=kvg[:, 0:1], scalar2=None, op0=ALU.mult),
         r=[wkv_f, kvg], w=[wkv])

    kT = [t_(f"kT{h}", [96, S], BF16) for h in range(2)]
    vaug = [t_(f"vaug{h}", [128, NKB, 65], BF16) for h in range(2)]
    kTb = [[T(kT[h].ap, f"kT{h}_{i}") for i in range(nblk)] for h in range(2)]
    vab = [[T(vaug[h].ap, f"va{h}_{i}") for i in range(nblk)] for h in range(2)]
    for h in range(2):
        p.op("pool", lambda e, h=h: e.memset(vaug[h][:, :, 64:65], 1.0), w=vab[h])

    hTb = RR([t_(f"hTb{i}", [128, 8, TB], BF16) for i in range(2)])
    raw = [t_(f"raw{s}", [128, 3 + TB]) for s in range(3)]
    for s in range(3):
        p.op("pool", lambda e, s=s: e.memset(raw[s][:, 0:3], 0.0), w=[raw[s]])
    caccs = [t_(f"cacc{s}", [128, TB]) for s in range(3)]
    sqb = t_("sqb", [128, 2, TB], BF16)
    rn = t_("rn", [128, TB])
    rtmp = t_("rtmp", [128, TB])
    qTg = t_("qTg", [128, TB], BF16)
    kTg = t_("kTg", [128, TB], BF16)
    vTg = t_("vTg", [128, TB], BF16)
    gates = [t_(f"gate{i}", [128, TB]) for i in range(2)]
    R_ = slice(64, 96)
    cqT = t_("cqT", [128, 2, TB], BF16)
    ckvT = t_("ckvT", [128, TB], BF16)
    rstd_q = t_("rstd_q", [128, TB])
    rstd_kv = t_("rstd_kv", [128, TB])
    rkv_col = t_("rkv_col", [128, 4])
    rkv_tmp = t_("rkv_tmp", [128, 4])
    ab_row = t_("ab_row", [2, TB])
    ab_col = t_("ab_col", [64, 8, 2])
    la_col = t_("la_col", [64, 8])
    sp_t = t_("sp_t", [64, 8])
    beta_col = t_("beta_col", [64, 8])
    nbeta_col = t_("nbeta_col", [64, 8])
    eg_col = t_("eg_col", [64, 16])
    X1 = t_("X1", [64, 8, 128])
    X2 = t_("X2", [64, 8, 64])
    DT = t_("DT", [64, 8, 64])
    DST = t_("DST", [64, 8, 64])
    tmpA = t_("tmpA", [64, 8, 64])
    chX = [t_(f"chX{i}", [64, 8, 64]) for i in range(2)]
    chY = [t_(f"chY{i}", [64, 8, 64]) for i in range(2)]
    chT = [t_(f"chT{i}", [64, 8, 64]) for i in range(2)]
    TTb = t_("TTb", [64, 8, 64], BF16)
    PT = t_("PTg", [64, 8, 64], BF16)
    EGB = t_("EGB", [128, 8, 64])
    qdecT = t_("qdecT", [128, TB], BF16)
    kTok = t_("kTok", [64, 8, 128], BF16)
    vTok = t_("vTok", [64, 8, 128], BF16)
    keg = t_("keg", [64, 8, 128], BF16)
    kdec = t_("kdec", [64, 8, 128], BF16)
    bu = t_("bu", [64, 8, 128])
    wT = t_("wT", [128, 8, 64], BF16)
    Sst = t_("Sst", [128, 128])
    Sb = t_("Sb", [128, 128], BF16)
    p.op("pool", lambda e: e.memset(Sst[:, :], 0.0), w=[Sst])
    p.op("pool", lambda e: e.memset(Sb[:, :], 0.0), w=[Sb])
    vnew = RR([t_(f"vnew{i}", [64, 128], BF16) for i in range(2)])
    osq = t_("osq", [64, 4, 128])
    oss = t_("oss", [64, 4])
    ors = t_("ors", [64, 4])
    otmp = t_("otmp", [64, 4])
    on1 = t_("on1", [64, 4, 128])
    on2 = t_("on2", [64, 4, 128], BF16)
    oTg = t_("oTg", [128, TB], BF16)
    posi = t_("posi", [96, TB], I32)
    ph = t_("ph", [96, 2, TB])
    phi = t_("phi", [96, 2, TB], I32)
    phm = t_("phm", [96, 2, TB])
    CS = t_("CS", [96, 2, TB])
    u1 = t_("u1", [96, TB])
    u2 = t_("u2", [96, TB])
    qT = [t_(f"qT{h}", [96, TB], BF16) for h in range(2)]
    PTa = RR([t_(f"PTa{i}", [128, TB], BF16) for i in range(3)])
    rden = t_("rden", [128, 4])
    on_tok = t_("on_tok", [128, 4, 128], BF16)
    oTm = t_("oTm", [128, TB], BF16)
    SCALE = float(96 ** -0.5)

    def mm(out_ap, lhsT, rhs, start, stop, r, w):
        p.op("pe", lambda e: e.matmul(out_ap, lhsT=lhsT, rhs=rhs, start=start, stop=stop, skip_group_check=True), r=r, w=w)

    def proj_group(hb, c0, M):
        ps = misc.next()
        for k in range(8):
            mm(ps[0:M, :], Wown[:, k, c0:c0 + M], hb[:, k, :], k == 0, k == 7, [Wown, hb], [ps])
        return ps

    def stage1(t):
        gate = gates[t % 2]
        tok0 = t * TB
        hb = hTb.next()
        if callable(getattr(hT_src, "load", None)):
            hT_src.load(p, hb, tok0)
        else:
            p.dma(hb[:, :, :], hT_src(tok0), w=[hb])
        p.dma(posi[64:96, :], pos.ap.ap()[tok0:tok0 + TB].partition_broadcast(32), r=[pos], w=[posi])
        p.op("dve", lambda e: e.tensor_copy(out=u1[R_, :], in_=posi[R_, :]), r=[posi], w=[u1])
        p.op("dve", lambda e: e.tensor_scalar(out=ph[R_, 0, :], in0=u1[R_, :], scalar1=invf[R_, 0:1], scalar2=None, op0=ALU.mult),
             r=[u1, invf], w=[ph])
        p.op("dve", lambda e: e.tensor_scalar(out=ph[R_, 1, :], in0=ph[R_, 0, :], scalar1=0.25, scalar2=None, op0=ALU.add),
             r=[ph], w=[ph])
        p.op("dve", lambda e: e.tensor_copy(out=phi[R_, :, :], in_=ph[R_, :, :]), r=[ph], w=[phi])
        p.op("dve", lambda e: e.tensor_copy(out=phm[R_, :, :], in_=phi[R_, :, :]), r=[phi], w=[phm])
        p.op("dve", lambda e: e.tensor_tensor(out=ph[R_, :, :], in0=ph[R_, :, :], in1=phm[R_, :, :], op=ALU.subtract), r=[ph, phm], w=[ph])
        p.op("dve", lambda e: e.scalar_tensor_tensor(out=phm[R_, :, :], in0=ph[R_, :, :], scalar=0.5, in1=ph[R_, :, :],
                                                     op0=ALU.is_gt, op1=ALU.subtract), r=[ph], w=[phm])
        p.op("dve", lambda e: e.scalar_tensor_tensor(out=ph[R_, :, :], in0=phm[R_, :, :], scalar=0.5, in1=phm[R_, :, :],
                                                     op0=ALU.is_gt, op1=ALU.subtract), r=[phm], w=[ph])
        p.op("act", lambda e: e.activation(out=CS[R_, :, :], in_=ph[R_, :, :], func=AF.Sin, scale=float(2 * np.pi)), r=[ph], w=[CS])


        for s, c0 in enumerate((GQ, GK, GV)):
            if t > 0:
                p.op("pool", lambda e, s=s: e.tensor_copy(out=raw[s][:, 0:3], in_=raw[s][:, TB:TB + 3]), r=[raw[s]], w=[raw[s]])
            yield
            ps = proj_group(hb, c0, 128)
            p.op("act", lambda e, s=s, ps=ps: e.copy(out=raw[s][:, 3:3 + TB], in_=ps[:, :]), r=[ps], w=[raw[s]])
        yield
        ps = proj_group(hb, GG, 128)
        p.op("act", lambda e, ps=ps: e.copy(out=gate[:, :], in_=ps[:, :]), r=[ps], w=[gate])
        for kc in range(2):
            yield
            ps = proj_group(hb, CQ + kc * 128, 128)
            p.op("act", lambda e, ps=ps, kc=kc: e.copy(out=cqT[:, kc, :], in_=ps[:, :]), r=[ps], w=[cqT])
            p.op("act", lambda e, ps=ps, kc=kc: e.activation(out=sqb[:, kc, :], in_=ps[:, :], func=AF.Square), r=[ps], w=[sqb])
        yield
        ps = misc.next()
        for kc in range(2):
            mm(ps[:, :], ones_bf[:, :], sqb[:, kc, :], kc == 0, kc == 1, [ones_bf, sqb], [ps])
        rsqrt_act(p, rstd_q[:, :], ps[:, :], 1.0 / 256, p.eps_tile[:, 0:1], rtmp[:, :], [ps], [rstd_q], rtmp)
        yield
        ps = proj_group(hb, CKV, 128)
        p.op("act", lambda e, ps=ps: e.copy(out=ckvT[:, :], in_=ps[:, :]), r=[ps], w=[ckvT])
        p.op("act", lambda e, ps=ps: e.activation(out=sqb[:, 0, :], in_=ps[:, :], func=AF.Square), r=[ps], w=[sqb])
        yield
        ps = misc.next()
        mm(ps[:, :], ones_bf[:, :], sqb[:, 0, :], True, True, [ones_bf, sqb], [ps])
        rsqrt_act(p, rstd_kv[:, :], ps[:, :], 1.0 / 128, p.eps_tile[:, 0:1], rtmp[:, :], [ps], [rstd_kv], rtmp)
        yield
        ps = misc.next()
        for j in range(4):
            mm(ps[:, j:j + 1], sqb[:, 0, j * 128:(j + 1) * 128], ones_bf[:, 0:1], j == 0, j == 3, [ones_bf, sqb], [ps])
        rsqrt_act(p, rkv_col[:, :], ps[:, 0:4], 1.0 / 128, p.eps_tile[:, 0:1], rkv_tmp[:, :], [ps], [rkv_col], rkv_tmp)

        yield
        psKA = proj_group(hb, KR - 64, 96)
        p.op("dve", lambda e, ps=psKA: e.tensor_tensor(out=u1[R_, :], in0=ps[R_, :], in1=CS[R_, 1, :], op=ALU.mult), r=[psKA, CS], w=[u1])
        yield
        psKB = proj_group(hb, KROT - 64, 96)
        p.op("dve", lambda e, ps=psKB: e.tensor_tensor(out=u2[R_, :], in0=ps[R_, :], in1=CS[R_, 0, :], op=ALU.mult), r=[psKB, CS], w=[u2])
        for h in range(2):
            p.op("pool", lambda e, h=h: e.tensor_tensor(out=kT[h][R_, tok0:tok0 + TB], in0=u1[R_, :], in1=u2[R_, :], op=ALU.add),
                 r=[u1, u2], w=[kTb[h][t]])
        yield
        ps = proj_group(hb, AB, 2)
        p.op("act", lambda e, ps=ps: e.copy(out=ab_row[:, :], in_=ps[0:2, :]), r=[ps], w=[ab_row])


    s1 = stage1(0)
    for _ in s1:
        pass
    for t in range(nblk):
        tok0 = t * TB
        gate = gates[t % 2]
        nxt = stage1(t + 1) if t + 1 < nblk else iter(())
        for h in range(2):
            psA = misc.next()
            for kc in range(2):
                mm(psA[0:96, :], wq[:, kc, h * 128:h * 128 + 96], cqT[:, kc, :], kc == 0, kc == 1, [wq, cqT], [psA])
            p.op("dve", lambda e, h=h, ps=psA: e.tensor_tensor(out=qT[h][0:64, :], in0=ps[0:64, :], in1=rstd_q[0:64, :], op=ALU.mult),
                 r=[psA, rstd_q], w=[qT[h]])
            p.op("dve", lambda e, ps=psA: e.tensor_tensor(out=u1[R_, :], in0=ps[R_, :], in1=CS[R_, 1, :], op=ALU.mult), r=[psA, CS], w=[u1])
            psB = misc.next()
            for kc in range(2):
                mm(psB[0:96, :], wq[:, kc, h * 128 + 32:h * 128 + 128], cqT[:, kc, :], kc == 0, kc == 1, [wq, cqT], [psB])
            p.op("dve", lambda e, ps=psB: e.tensor_tensor(out=u2[R_, :], in0=ps[R_, :], in1=CS[R_, 0, :], op=ALU.mult), r=[psB, CS], w=[u2])
            p.op("pool", lambda e: e.tensor_tensor(out=u1[R_, :], in0=u1[R_, :], in1=u2[R_, :], op=ALU.add), r=[u1, u2], w=[u1])
            p.op("pool", lambda e, h=h: e.tensor_tensor(out=qT[h][R_, :], in0=u1[R_, :], in1=rstd_q[R_, :], op=ALU.mult),
                 r=[u1, rstd_q], w=[qT[h]])
            psK = misc.next()
            mm(psK[0:64, :], wkv[:, h * 64:(h + 1) * 64], ckvT[:, :], True, True, [wkv, ckvT], [psK])
            p.op("dve", lambda e, h=h, ps=psK: e.tensor_tensor(out=kT[h][0:64, tok0:tok0 + TB], in0=ps[0:64, :], in1=rstd_kv[0:64, :],
                                                               op=ALU.mult), r=[psK, rstd_kv], w=[kTb[h][t]])
        psV = misc.next()
        for j in range(4):
            mm(psV[:, j * 128:(j + 1) * 128], ckvT[:, j * 128:(j + 1) * 128], wkv[:, 128:256], j == 0, j == 3, [ckvT, wkv], [psV])
        for h in range(2):
            p.op("dve", lambda e, h=h, ps=psV: e.tensor_tensor(
                out=vaug[h][:, 4 * t:4 * t + 4, 0:64],
                in0=ps[:, :].rearrange("q (j c) -> q j c", j=4)[:, :, h * 64:(h + 1) * 64],
                in1=rkv_col[:, :].unsqueeze(2).to_broadcast([128, 4, 64]), op=ALU.mult), r=[psV, rkv_col], w=[vab[h][t]])

        def gdn_prep_gen():
            for s in range(3):
                cacc = caccs[s]
                p.op("dve", lambda e, s=s, cacc=cacc: e.tensor_scalar(out=cacc[:, :], in0=raw[s][:, 0:TB], scalar1=convw[:, s, 0:1],
                                                                      scalar2=None, op0=ALU.mult), r=[raw[s], convw], w=[cacc])
                for k in range(1, 4):
                    p.op("dve", lambda e, s=s, k=k, cacc=cacc: e.scalar_tensor_tensor(
                        out=cacc[:, :], in0=raw[s][:, k:k + TB], scalar=convw[:, s, k:k + 1], in1=cacc[:, :], op0=ALU.mult, op1=ALU.add),
                        r=[raw[s], convw, cacc], w=[cacc])
            p.op("act", lambda e: e.activation(out=gate[:, :], in_=gate[:, :], func=AF.Silu), r=[gate], w=[gate])
            p.op("act", lambda e: e.activation(out=vTg[:, :], in_=caccs[2][:, :], func=AF.Silu), r=[caccs[2]], w=[vTg])
            for s in range(2):
                p.op("act", lambda e, s=s: e.activation(out=caccs[s][:, :], in_=caccs[s][:, :], func=AF.Silu), r=[caccs[s]], w=[caccs[s]])
            yield
            for s, dst in ((0, qTg), (1, kTg)):
                sil = caccs[s]
                p.op("act", lambda e, sil=sil, s=s: e.activation(out=sqb[:, s, :], in_=sil[:, :], func=AF.Square), r=[sil], w=[sqb])
                yield
                ps = misc.next()
                mm(ps[:, :], ones_bf[:, :], sqb[:, s, :], True, True, [ones_bf, sqb], [ps])
                if s == 0:
                    rsqrt_act(p, rn[:, :], ps[:, :], 128.0, c_eps128[:, 0:1], rtmp[:, :], [ps, c_eps128], [rn], rtmp)
                else:
                    rsqrt_act(p, rn[:, :], ps[:, :], 1.0, p.eps_tile[:, 0:1], rtmp[:, :], [ps], [rn], rtmp)
                p.op("dve", lambda e, dst=dst, sil=sil: e.tensor_tensor(out=dst[:, :], in0=sil[:, :], in1=rn[:, :], op=ALU.mult),
                     r=[sil, rn], w=[dst])
            for src, dst in ((kTg, kTok), (vTg, vTok)):
                yield
                ps = misc.next()
                for c in range(8):
                    p.op("pe", lambda e, c=c, ps=ps, src=src: e.transpose(out=bfv(ps)[0:64, c * 128:(c + 1) * 128],
                                                                          in_=src[:, c * 64:(c + 1) * 64], identity=cm.ident[:, :]),
                         r=[src, cm.ident], w=[ps])
                p.op("act", lambda e, ps=ps, dst=dst: e.copy(out=dst[:, :, :], in_=bfv(ps)[0:64, :].rearrange("q (c d) -> q c d", c=8)),
                     r=[ps], w=[dst])
            yield
            ps = misc.next()
            for c in range(8):
                p.op("pe", lambda e, c=c, ps=ps: e.transpose(out=ps[0:64, 2 * c:2 * c + 2], in_=ab_row[0:2, c * 64:(c + 1) * 64],
                                                             identity=cm.identf[0:2, 0:2]), r=[ab_row, cm.identf], w=[ps])
            p.op("act", lambda e, ps=ps: e.copy(out=ab_col[:, :, :], in_=ps[0:64, 0:16].rearrange("q (c two) -> q c two", two=2)),
                 r=[ps], w=[ab_col])
            p.op("act", lambda e: e.activation(out=nbeta_col[:, :], in_=ab_col[:, :, 1], func=AF.Exp, scale=-1.0), r=[ab_col], w=[nbeta_col])
            p.op("dve", lambda e: e.tensor_scalar(out=nbeta_col[:, :], in0=nbeta_col[:, :], scalar1=1.0, scalar2=None, op0=ALU.add),
                 r=[nbeta_col], w=[nbeta_col])
            p.op("dve", lambda e: e.reciprocal(out=beta_col[:, :], in_=nbeta_col[:, :]), r=[nbeta_col], w=[beta_col])
            p.op("dve", lambda e: e.tensor_scalar(out=nbeta_col[:, :], in0=beta_col[:, :], scalar1=-1.0, scalar2=None, op0=ALU.mult),
                 r=[beta_col], w=[nbeta_col])
            p.op("act", lambda e: e.activation(out=sp_t[:, :], in_=ab_col[:, :, 0], func=AF.Exp, bias=dtb[:, 0:1]), r=[ab_col, dtb], w=[sp_t])
            p.op("act", lambda e: e.activation(out=sp_t[:, :], in_=sp_t[:, :], func=AF.Ln, bias=c_one[0:64, 0:1]), r=[sp_t, c_one], w=[sp_t])
            p.op("dve", lambda e: e.tensor_scalar(out=la_col[:, :], in0=sp_t[:, :], scalar1=nA[:, 0:1], scalar2=None, op0=ALU.mult),
                 r=[sp_t, nA], w=[la_col])
            p.op("dve", lambda e: e.tensor_copy(out=X1[:, :, :], in_=la_col[:, :].unsqueeze(2).to_broadcast([64, 8, 128])), r=[la_col], w=[X1])
            p.op("dve", lambda e: e.tensor_scalar(out=sp_t[:, :], in0=la_col[:, :], scalar1=-1.0, scalar2=None, op0=ALU.mult),
                 r=[la_col], w=[sp_t])
            p.op("pool", lambda e: e.tensor_tensor(out=X2[:, :, :], in0=b8(U), in1=sp_t[:, :].unsqueeze(2).to_broadcast([64, 8, 64]),
                                                   op=ALU.mult), r=[U, sp_t], w=[X2])
            yield
            ps = misc.next()
            mm(ps[0:64, 0:8], U[:, :], la_col[:, :], True, True, [U, la_col], [ps])
            mm(ps[0:64, 8:16], Ust[:, :], la_col[:, :], False, True, [Ust, la_col], [ps])
            p.op("act", lambda e, ps=ps: e.activation(out=eg_col[:, :], in_=ps[0:64, 0:16], func=AF.Exp), r=[ps], w=[eg_col])
            yield
            ps = misc.next()
            for c in range(8):
                mm(ps[0:64, c * 64:(c + 1) * 64], X1[:, c, 0:64], U[:, :], c == 0, False, [X1, U], [ps])
                mm(ps[0:64, c * 64:(c + 1) * 64], X2[:, c, :], ones64[:, :], False, True, [X2, ones64], [ps])
            p.op("dve", lambda e, ps=ps: e.tensor_tensor(out=tmpA[:, :, :], in0=ps[0:64, :].rearrange("q (c i) -> q c i", c=8),
                                                         in1=b8(Mneg), op=ALU.add), r=[ps, Mneg], w=[tmpA])
            p.op("act", lambda e: e.activation(out=DT[:, :, :], in_=tmpA[:, :, :], func=AF.Exp), r=[tmpA], w=[DT])
            p.op("pool", lambda e: e.tensor_tensor(out=DST[:, :, :], in0=DT[:, :, :], in1=b8(S01), op=ALU.mult), r=[DT, S01], w=[DST])
            yield
            ps = misc.next()
            for c in range(8):
                mm(ps[:, c * 64:(c + 1) * 64], X1[:, c, :], U[:, :], c == 0, c == 7, [X1, U], [ps])
            p.op("act", lambda e, ps=ps: e.activation(out=EGB[:, :, :], in_=ps[:, :].rearrange("q (c i) -> q c i", c=8), func=AF.Exp),
                 r=[ps], w=[EGB])
            p.op("dve", lambda e: e.tensor_tensor(out=qdecT[:, :], in0=qTg[:, :], in1=EGB[:, :, :].rearrange("q c i -> q (c i)"), op=ALU.mult),
                 r=[qTg, EGB], w=[qdecT])
            yield
            ps = misc.next()
            for c in range(8):
                mm(ps[0:64, c * 64:(c + 1) * 64], kTg[:, c * 64:(c + 1) * 64], kTg[:, c * 64:(c + 1) * 64], c == 0, c == 7, [kTg], [ps])
            Y = chY[0]
            p.op("dve", lambda e, ps=ps: e.tensor_tensor(out=tmpA[:, :, :], in0=ps[0:64, :].rearrange("q (c i) -> q c i", c=8),
                                                         in1=DST[:, :, :], op=ALU.mult), r=[ps, DST], w=[tmpA])
            p.op("pool", lambda e, Y=Y: e.tensor_tensor(out=Y[:, :, :], in0=tmpA[:, :, :],
                                                        in1=nbeta_col[:, :].unsqueeze(2).to_broadcast([64, 8, 64]), op=ALU.mult),
                 r=[tmpA, nbeta_col], w=[Y])
            yield
            ps = misc.next()
            for c in range(8):
                mm(ps[0:64, c * 64:(c + 1) * 64], kTg[:, c * 64:(c + 1) * 64], qTg[:, c * 64:(c + 1) * 64], c == 0, c == 7, [kTg, qTg], [ps])
            p.op("dve", lambda e, ps=ps: e.tensor_tensor(out=PT[:, :, :], in0=ps[0:64, :].rearrange("q (c i) -> q c i", c=8),
                                                         in1=DT[:, :, :], op=ALU.mult), r=[ps, DT], w=[PT])
            X = chX[0]
            yield
            ps = misc.next()
            for c in range(8):
                p.op("pe", lambda e, c=c, ps=ps, Y=Y: e.transpose(out=ps[0:64, c * 64:(c + 1) * 64], in_=Y[:, c, :],
                                                                  identity=cm.identf[0:64, 0:64]), r=[Y, cm.identf], w=[ps])
            p.op("act", lambda e, ps=ps, X=X: e.copy(out=X[:, :, :], in_=ps[0:64, :].rearrange("q (c i) -> q c i", c=8)), r=[ps], w=[X])
            TT = chT[0]
            p.op("pool", lambda e, TT=TT, Y=Y: e.tensor_tensor(out=TT[:, :, :], in0=Y[:, :, :], in1=b8(cm.identf), op=ALU.add), r=[Y, cm.identf], w=[TT])
            for lvl in range(5):
                Xn, Yn, Tn = chX[(lvl + 1) % 2], chY[(lvl + 1) % 2], chT[(lvl + 1) % 2]
                yield
                ps = misc.next()
                for c in range(8):
                    mm(ps[0:64, c * 64:(c + 1) * 64], Y[:, c, :], X[:, c, :], c == 0, c == 7, [X, Y], [ps])
                p.op("act", lambda e, ps=ps, Xn=Xn: e.copy(out=Xn[:, :, :], in_=ps[0:64, :].rearrange("q (c i) -> q c i", c=8)), r=[ps], w=[Xn])
                if lvl < 4:
                    yield
                    ps = misc.next()
                    for c in range(8):
                        mm(ps[0:64, c * 64:(c + 1) * 64], X[:, c, :], Y[:, c, :], c == 0, c == 7, [X, Y], [ps])
                    p.op("dve", lambda e, ps=ps, Yn=Yn: e.tensor_copy(out=Yn[:, :, :], in_=ps[0:64, :].rearrange("q (c i) -> q c i", c=8)),
                         r=[ps], w=[Yn])
                yield
                ps = misc.next()
                for c in range(8):
                    mm(ps[0:64, c * 64:(c + 1) * 64], Xn[:, c, :], TT[:, c, :], c == 0, c == 7, [Xn, TT], [ps])
                p.op("dve", lambda e, ps=ps, Tn=Tn, TT=TT: e.tensor_tensor(out=Tn[:, :, :], in0=ps[0:64, :].rearrange("q (c i) -> q c i", c=8),
                                                                           in1=TT[:, :, :], op=ALU.add), r=[ps, TT], w=[Tn])
                X, Y, TT = Xn, Yn, Tn
            p.op("act", lambda e, TT=TT: e.copy(out=TTb[:, :, :], in_=TT[:, :, :]), r=[TT], w=[TTb])
            p.op("pool", lambda e: e.tensor_tensor(out=keg[:, :, :], in0=kTok[:, :, :], in1=eg_col[:, 0:8].unsqueeze(2).to_broadcast([64, 8, 128]),
                                                   op=ALU.mult), r=[kTok, eg_col], w=[keg])
            p.op("pool", lambda e: e.tensor_tensor(out=kdec[:, :, :], in0=kTok[:, :, :], in1=eg_col[:, 8:16].unsqueeze(2).to_broadcast([64, 8, 128]),
                                                   op=ALU.mult), r=[kTok, eg_col], w=[kdec])
            for half in range(2):
                yield
                ps = misc.next()
                for cc in range(4):
                    c = half * 4 + cc
                    mm(ps[0:64, cc * 128:(cc + 1) * 128], TTb[:, c, :], vTok[:, c, :], cc == 0, cc == 3, [TTb, vTok], [ps])
                p.op("dve", lambda e, ps=ps, half=half: e.tensor_tensor(
                    out=bu[:, half * 4:half * 4 + 4, :], in0=ps[0:64, :].rearrange("q (c d) -> q c d", c=4),
                    in1=beta_col[:, half * 4:half * 4 + 4].unsqueeze(2).to_broadcast([64, 4, 128]), op=ALU.mult), r=[ps, beta_col], w=[bu])
            yield
            ps = misc.next()
            for c in range(8):
                mm(ps[:, c * 64:(c + 1) * 64], keg[:, c, :], TTb[:, c, :], c == 0, c == 7, [keg, TTb], [ps])
            p.op("act", lambda e, ps=ps: e.copy(out=wT[:, :, :], in_=ps[:, :].rearrange("q (c i) -> q c i", c=8)), r=[ps], w=[wT])

        def gdn_a(c):
            mm(bR[0:64, 0:128], wT[:, c, :], Sb[:, :], True, True, [wT, Sb], [bR])
            vn = vnew.next()
            p.op("dve", lambda e: e.scalar_tensor_tensor(out=vn[:, :], in0=bR[0:64, 0:128], scalar=nbeta_col[:, c:c + 1], in1=bu[:, c, :],
                                                         op0=ALU.mult, op1=ALU.add), r=[bR, nbeta_col, bu], w=[vn])
            return vn

        def gdn_b(c, vn):
            cc = c % 4
            mm(bO[0:64, cc * 128:(cc + 1) * 128], qdecT[:, c * 64:(c + 1) * 64], Sb[:, :], cc == 0, False, [qdecT, Sb], [bO])
            mm(bO[0:64, cc * 128:(cc + 1) * 128], PT[:, c, :], vn[:, :], False, True, [PT, vn], [bO])
            mm(bR[:, 128:256], kdec[:, c, :], vn[:, :], True, True, [kdec, vn], [bR])
            p.op("dve", lambda e: e.scalar_tensor_tensor(out=Sb[:, :], in0=Sst[:, :], scalar=EGB[:, c, 63:64], in1=bR[:, 128:256],
                                                         op0=ALU.mult, op1=ALU.add), r=[Sst, EGB, bR], w=[Sb])
            p.op("dve", lambda e: e.scalar_tensor_tensor(out=Sst[:, :], in0=Sst[:, :], scalar=EGB[:, c, 63:64], in1=bR[:, 128:256],
                                                         op0=ALU.mult, op1=ALU.add), r=[Sst, EGB, bR], w=[Sst])
            if cc == 3:
                gdn_out(c - 3)

        def gdn_out(c0):
            o3 = bO[0:64, :].rearrange("q (c d) -> q c d", c=4)
            p.op("act", lambda e: e.activation(out=osq[:, :, :], in_=o3, func=AF.Square), r=[bO], w=[osq])
            p.op("dve", lambda e: e.tensor_reduce(out=oss[:, :], in_=osq[:, :, :], axis=AX.X, op=ALU.add), r=[osq], w=[oss])
            rsqrt_act(p, ors[:, :], oss[:, :], 1.0 / 128, p.eps_tile[0:64, 0:1], otmp[:, :], [oss], [ors], otmp)
            p.op("dve", lambda e: e.tensor_tensor(out=on1[:, :, :], in0=o3, in1=ors[:, :].unsqueeze(2).to_broadcast([64, 4, 128]), op=ALU.mult),
                 r=[bO, ors], w=[on1])
            p.op("pool", lambda e: e.tensor_tensor(out=on2[:, :, :], in0=on1[:, :, :], in1=gnorm[:, :].unsqueeze(1).to_broadcast([64, 4, 128]),
                                                   op=ALU.mult), r=[on1, gnorm], w=[on2])
            ps = misc.next()
            for cc in range(4):
                p.op("pe", lambda e, cc=cc, ps=ps: e.transpose(out=bfv(ps)[:, cc * 64:(cc + 1) * 64], in_=on2[:, cc, :],
                                                               identity=cm.ident[0:64, 0:64]), r=[on2, cm.ident], w=[ps])
            p.op("dve", lambda e, ps=ps: e.tensor_tensor(out=oTg[:, c0 * 64:c0 * 64 + 256], in0=bfv(ps)[:, 0:256],
                                                         in1=gate[:, c0 * 64:c0 * 64 + 256], op=ALU.mult), r=[ps, gate], w=[oTg])

        def attn_unit(h, kb):
            r_ = kb - 4 * t
            q0 = max(0, r_) * 128
            nq = TB - q0
            ps = psS.next()
            mm(ps[:, 0:nq], kT[h][:, kb * 128:(kb + 1) * 128], qT[h][:, q0:TB], True, True, [kTb[h][kb // 4], qT[h]], [ps])
            pt = PTa.next()
            p.op("act", lambda e: e.activation(out=pt[:, 0:nq], in_=ps[:, 0:nq], func=AF.Exp, scale=SCALE), r=[ps], w=[pt])
            if r_ >= 0:
                p.op("pool", lambda e: e.memset(pt[64:128, 0:64], 0.0), r=[], w=[pt])
            return (h, kb, q0, pt)

        def attn_unit_pv(st):
            h, kb, q0, pt = st
            for qs in range(q0 // 128, 4):
                first = (kb == 0 and qs == 0)
                last = (kb == 4 * t + qs)
                mm(psO[h][:, qs * 65:(qs + 1) * 65], pt[:, qs * 128 - q0:(qs + 1) * 128 - q0], vaug[h][:, kb, :], first, last,
                   [pt, vab[h][kb // 4]], [psO[h]])

        def attn_out():
            for h in range(2):
                o3 = psO[h][:, 0:260].rearrange("q (s c) -> q s c", s=4)
                p.op("dve", lambda e, o3=o3: e.reciprocal(out=rden[:, :], in_=o3[:, :, 64]), r=[psO[h]], w=[rden])
                p.op("dve", lambda e, o3=o3, h=h: e.tensor_tensor(out=on_tok[:, :, h * 64:(h + 1) * 64], in0=o3[:, :, 0:64],
                                                                  in1=rden[:, :].unsqueeze(2).to_broadcast([128, 4, 64]), op=ALU.mult),
                     r=[psO[h], rden], w=[on_tok])
            ps = misc.next()
            for qs in range(4):
                p.op("pe", lambda e, qs=qs, ps=ps: e.transpose(out=bfv(ps)[:, qs * 128:(qs + 1) * 128], in_=on_tok[:, qs, :],
                                                               identity=cm.ident[:, :]), r=[on_tok, cm.ident], w=[ps])
            p.op("act", lambda e, ps=ps: e.copy(out=oTm[:, :], in_=bfv(ps)[:, 0:512]), r=[ps], w=[oTm])
            p.dma(oT_own.ap.ap()[128:256, tok0:tok0 + TB], oTm[:, :], r=[oTm], w=[oT_own])

        units = [(h, kb) for h in range(2) for kb in range(4 * t + 4)]
        nu = len(units)

        def gdn_all():
            yield from gdn_prep_gen()
            vn_cur = {}
            for c in range(8):
                yield
                vn_cur[c] = gdn_a(c)
                yield
                gdn_b(c, vn_cur[c])

        NPIECES = 50
        LAG = 2
        ui = 0
        npc = 0
        pend = []

        def emit_unit(i):
            pend.append(attn_unit(*units[i]))
            if len(pend) > LAG:
                attn_unit_pv(pend.pop(0))

        for _ in gdn_all():
            npc += 1
            target = min(nu, npc * nu // NPIECES)
            while ui < target:
                emit_unit(ui)
                ui += 1
            if npc >= 14 and npc % 2 == 0:
                next(nxt, None)
        assert npc <= NPIECES, npc
        while ui < nu:
            emit_unit(ui)
            ui += 1
        while pend:
            attn_unit_pv(pend.pop(0))
        for _ in nxt:
            pass
        p.dma(oT_own.ap.ap()[0:128, tok0:tok0 + TB], oTg[:, :], r=[oTg], w=[oT_own])
        attn_out()
    p.barrier()
    p.release(mk)


def core_weight_slices(inp, r):
    f32 = np.float32
    w_in = inp["w_in"]
    L = w_in.shape[0]
    o_q, o_k, o_v, o_g, o_a, o_b, o_cq, o_ckv, o_kr = 0, 512, 1024, 1536, 2048, 2052, 2056, 2312, 2440
    hs = slice(r * 128, (r + 1) * 128)
    kr = w_in[:, :, o_kr:o_kr + 32]
    krot = np.concatenate([kr[:, :, 16:32], kr[:, :, 0:16]], axis=-1)
    w_in_own = np.concatenate([
        w_in[:, :, o_q:o_q + 512][:, :, hs], w_in[:, :, o_k:o_k + 512][:, :, hs], w_in[:, :, o_v:o_v + 512][:, :, hs],
        w_in[:, :, o_g:o_g + 512][:, :, hs], w_in[:, :, o_cq:o_cq + 256], w_in[:, :, o_ckv:o_ckv + 128], kr, krot,
        w_in[:, :, o_a + r:o_a + r + 1], w_in[:, :, o_b + r:o_b + r + 1]], axis=-1)
    cw = inp["conv_w"]
    conv_own = np.stack([cw[:, :, s * 512 + r * 128: s * 512 + (r + 1) * 128] for s in range(3)], axis=1)
    conv_own = np.ascontiguousarray(np.transpose(conv_own, (0, 3, 1, 2)))
    wq = inp["w_q_up"]
    parts = []
    for hh in range(2):
        base = (2 * r + hh) * 96
        nope = wq[:, :, base:base + 64]
        rope = wq[:, :, base + 64:base + 96]
        rot = np.concatenate([rope[:, :, 16:32], rope[:, :, 0:16]], axis=-1)
        parts += [nope, rope, rot]
    wq_own = np.concatenate(parts, axis=-1)
    wkv = inp["w_kv_up"]
    b0, b1 = (2 * r) * 128, (2 * r + 1) * 128
    wkv_own = np.concatenate([wkv[:, :, b0:b0 + 64], wkv[:, :, b1:b1 + 64], wkv[:, :, b0 + 64:b0 + 128], wkv[:, :, b1 + 64:b1 + 128]], axis=-1)
    return {
        "w_in_own": np.ascontiguousarray(w_in_own, dtype=f32),
        "conv_own": np.ascontiguousarray(conv_own, dtype=f32),
        "alog_own": np.ascontiguousarray(inp["a_log"][:, r:r + 1], dtype=f32),
        "dtb_own": np.ascontiguousarray(inp["dt_bias"][:, r:r + 1], dtype=f32),
        "wq_own": np.ascontiguousarray(wq_own, dtype=f32),
        "wkv_own": np.ascontiguousarray(wkv_own, dtype=f32),
    }


def rope_freq_const():
    inv = (np.float32(10000.0) ** (-np.arange(0, 32, 2, dtype=np.float32) / np.float32(32))).astype(np.float32)
    t = np.zeros((96, 1), np.float32)
    t[64:80, 0] = -inv / np.float32(2 * np.pi)
    t[80:96, 0] = inv / np.float32(2 * np.pi)
    return t


MIX_W_SHAPES = {"w_in_own": [2, 1024, NCOL], "conv_own": [2, 128, 3, 4], "alog_own": [2, 1], "dtb_own": [2, 1],
                "wq_own": [2, 256, 256], "wkv_own": [2, 128, 256], "invf": [96, 1],
                "gdn_out_norm": [2, 128], "q_norm": [2, 256], "kv_norm": [2, 128]}


TOK_W = {"mix_post_norm": [2, D], "ffn_pre_norm": [2, D], "ffn_post_norm": [2, D], "mix_pre_norm": [2, D],
         "w_out": [2, D, D], "w_up": [2, D, DFF], "w_down": [2, DFF, D]}


def wout_rowmap(kk):
    return (kk // 2) * 128 if kk % 2 == 0 else 512 + (kk // 2) * 128


def build_norm_prog(NT):
    nc = bass.Bass("TRN2", target_bir_lowering=False)
    p = Prog(nc)
    W = {k: p.dram(k, TOK_W[k], F32, kind="ExternalInput") for k in ("mix_pre_norm",)}
    x_in = p.dram("x_in", [NT, D], F32, kind="ExternalInput")
    x_dummy = p.dram("x_dummy", [NT, D], F32)
    hT_out = p.dram("hT_out", [D, NT], BF16, kind="ExternalOutput")
    cm = setup_common(p)
    phase_tok(p, cm, 0, NT, W, x_in, None, x_dummy, hT_out, do_mix=False, do_ffn=False, gain_next=W["mix_pre_norm"][0, :])
    p.op("sp", lambda e: e.nop(), r=[hT_out])
    p.emit()
    p.close()
    return nc


def build_mix_prog(l, S):
    nc = bass.Bass("TRN2", target_bir_lowering=False)
    p = Prog(nc)
    Wc = {k: p.dram(k, shp, F32, kind="ExternalInput") for k, shp in MIX_W_SHAPES.items()}
    hT = p.dram("hT", [D, S], BF16, kind="ExternalInput")
    pos = p.dram("pos", [S], I32, kind="ExternalInput")
    oT = p.dram("oT_own", [256, S], BF16, kind="ExternalOutput")
    cm = setup_common(p)
    hv = hT.ap.ap().rearrange("(k q) t -> q k t", q=128)
    phase_mix(p, cm, l, S, Wc, lambda tok0: hv[:, :, tok0:tok0 + 512], pos, oT)
    p.op("sp", lambda e: e.nop(), r=[oT])
    p.emit()
    p.close()
    return nc


def build_tok_prog(l, NT, last):
    nc = bass.Bass("TRN2", target_bir_lowering=False)
    p = Prog(nc)
    W = {k: p.dram(k, shp, F32, kind="ExternalInput") for k, shp in TOK_W.items()}
    x_in = p.dram("x_in", [NT, D], F32, kind="ExternalInput")
    oT = p.dram("oT", [D, NT], BF16, kind="ExternalInput")
    x_out = p.dram("x_out", [NT, D], F32, kind="ExternalOutput")
    outs = [x_out]
    hT_out = None
    if not last:
        hT_out = p.dram("hT_out", [D, NT], BF16, kind="ExternalOutput")
        outs.append(hT_out)
    cm = setup_common(p)
    phase_tok(p, cm, l, NT, W, x_in, oT, x_out, hT_out, gain_next=(None if last else W["mix_pre_norm"][l + 1, :]),
              wout_rowmap=wout_rowmap)
    p.op("sp", lambda e: e.nop(), r=outs)
    p.emit()
    p.close()
    return nc


class HTSrc:
    def __init__(self, hT_sel, trackers, NT):
        self.view = hT_sel.ap.ap().rearrange("(r k q) t -> r q k t", r=4, q=128)
        self.trackers = trackers
        self.NT = NT

    def load(self, p, hb, tok0):
        rr, c0 = tok0 // self.NT, tok0 % self.NT
        p.dma(hb[:, :, :], self.view[rr][:, :, c0:c0 + 512], r=[self.trackers[rr]], w=[hb])


def build_fused(S, NT):
    nc = bass.Bass("TRN2", target_bir_lowering=False, num_devices=8)
    p = Prog(nc)
    W = {k: p.dram(k, shp, F32, kind="ExternalInput") for k, shp in TOK_W.items()}
    Wc = {k: p.dram(k, shp, F32, kind="ExternalInput") for k, shp in MIX_W_SHAPES.items()}
    pos = p.dram("pos", [S], I32, kind="ExternalInput")
    x_in = p.dram("x_in", [NT, D], F32, kind="ExternalInput")
    meta = p.dram("meta", [1, 2], I32, kind="ExternalInput")
    out = p.dram("out", [NT, D], F32, kind="ExternalOutput")
    hT_own = [p.dram(f"hT_own{l}", [D, NT], BF16) for l in range(2)]
    hT_all = [p.dram(f"hT_all{l}", [8 * D, NT], BF16) for l in range(2)]
    oT_own = [p.dram(f"oT_own{l}", [256, S], BF16) for l in range(2)]
    oT_all = [p.dram(f"oT_all{l}", [8 * 256, S], BF16) for l in range(2)]
    hT_sel = [p.dram(f"hT_sel{l}", [4 * D, NT], BF16) for l in range(2)]
    hT_sel_tr = [[T(hT_sel[l].ap, f"hT_sel{l}_{rr}") for rr in range(4)] for l in range(2)]
    oT_sel = [p.dram(f"oT_sel{l}", [D, NT], BF16) for l in range(2)]
    x1 = p.dram("x1", [NT, D], F32)
    x_dummy = p.dram("x_dummy", [NT, D], F32)
    groups = [list(range(8))]
    cm = setup_common(p)
    meta_sb = p.tile("meta_sb", [1, 2], I32)
    p.dma(meta_sb[:, :], meta.ap.ap(), w=[meta_sb])
    p.load_meta_regs(meta_sb, 2)

    wbf = [{"w_out": p.dram(f"wob{l}", [D, D], BF16), "w_up": p.dram(f"wub{l}", [D, DFF], BF16),
            "w_down": p.dram(f"wdb{l}", [DFF, D], BF16)} for l in range(2)]

    def convert_weights(l):
        for name, rows in (("w_out", D), ("w_up", D), ("w_down", DFF)):
            src = W[name].ap.ap()[l]
            dst = wbf[l][name]
            for r0 in range(0, rows, 256):
                p.dma(dst.ap.ap()[r0:r0 + 256, :], src[r0:r0 + 256, :], w=[dst], eng="pool")

    def gather_hT(l):
        p.all_gather(hT_own[l], hT_all[l], groups)
        for rr in range(4):
            p.dma_dyn(hT_sel[l].ap.ap()[rr * D:(rr + 1) * D, :], hT_all[l], 0, rr * D * NT, [[NT, D], [1, NT]],
                      w=[hT_sel_tr[l][rr]])

    def gather_oT(l):
        p.all_gather(oT_own[l], oT_all[l], groups)
        p.dma_dyn(oT_sel[l].ap.ap(), oT_all[l], 1, 0, [[S, D], [1, NT]], w=[oT_sel[l]])

    convert_weights(0)
    phase_tok(p, cm, 0, NT, W, x_in, None, x_dummy, hT_own[0], do_mix=False, do_ffn=False, gain_next=W["mix_pre_norm"][0, :])
    gather_hT(0)
    for l in range(2):
        if l == 1:
            convert_weights(1)
        phase_mix(p, cm, l, S, Wc, HTSrc(hT_sel[l], hT_sel_tr[l], NT), pos, oT_own[l])
        gather_oT(l)
        last = (l == 1)
        phase_tok(p, cm, l, NT, W, x_in if l == 0 else x1, oT_sel[l], out if last else x1, None if last else hT_own[1],
                  gain_next=(None if last else W["mix_pre_norm"][l + 1, :]), wout_rowmap=wout_rowmap, wsrc=wbf[l])
        if not last:
            gather_hT(1)
    p.op("sp", lambda e: e.nop(), r=[out])
    p.emit()
    p.close()
    return nc


def kernel(**inp):
    inp = {k: np.asarray(v) for k, v in inp.items()}
    B, S = inp["x"].shape[0], inp["x"].shape[1]
    NR = 4
    NT = S // NR
    ncore = B * NR
    cores = list(range(ncore))
    x = np.ascontiguousarray(inp["x"], dtype=np.float32)
    tokw = {k: np.ascontiguousarray(inp[k], dtype=np.float32) for k in TOK_W}
    invf = rope_freq_const()
    ims = []
    for c in cores:
        b, r = c // NR, c % NR
        m = dict(tokw)
        m.update(core_weight_slices(inp, r))
        m["invf"] = invf
        for k in ("gdn_out_norm", "q_norm", "kv_norm"):
            m[k] = np.ascontiguousarray(inp[k], dtype=np.float32)
        m["pos"] = np.ascontiguousarray(inp["positions"][b], dtype=np.int32)
        m["x_in"] = np.ascontiguousarray(x[b, r * NT:(r + 1) * NT])
        m["meta"] = np.array([[b * NR * D * NT, b * NR * 256 * S + r * NT]], dtype=np.int32)
        ims.append(m)
    res = run_bass_kernel_spmd(build_fused(S, NT), ims, core_ids=cores)
    out = np.empty_like(x)
    for c in cores:
        b, r = c // NR, c % NR
        out[b, r * NT:(r + 1) * NT] = res.results[c]["out"]
    return out
```
